# Optimizing a Trainium2 kernel written in Bass

```python
import jax, jax.numpy as jnp
from jax import lax
import numpy as np

D_MODEL = 1024
BATCH = 8
SEQ = 2048
DEPTH = 1
DEC_BATCH = 128
DEC_SEQ = 1
PAST_LEN = 16384
PAGE_SIZE = 128

CONV_WIDTH = 3
CONV_DIM = D_MODEL
GLA_HEADS = 4
GLA_DK = (D_MODEL // 2) // GLA_HEADS
GLA_DV = D_MODEL // GLA_HEADS
GATE_RANK = 16
GATE_NORMALIZER = 16.0
GLA_CHUNK = 64
D_FF = -(-8 * D_MODEL // (3 * 256)) * 256
PLE_DIM = 256
EPS = 1e-6
SPLITS = (CONV_DIM, CONV_DIM, CONV_DIM,
          GLA_HEADS * GLA_DK, GLA_HEADS * GLA_DK,
          GLA_HEADS * GLA_DV, GLA_HEADS * GLA_DV,
          GATE_RANK, D_MODEL, D_MODEL)
N_IN_COLS = sum(SPLITS)

kernel_name = "hybrid_shortconv_gla_gated_merge_step"


def rms_norm(x, w):
    x32 = x.astype(jnp.float32)
    y = x32 * lax.rsqrt(jnp.mean(x32 * x32, axis=-1, keepdims=True) + EPS)
    return (y * w.astype(jnp.float32)).astype(x.dtype)


def gla_recurrence(q, k, v, logw, s0):
    bsz, t = q.shape[0], q.shape[1]
    c = min(GLA_CHUNK, t)
    n = -(-t // c)
    pad = n * c - t

    def prep(a):
        a = jnp.pad(a, ((0, 0), (0, pad), (0, 0), (0, 0)))
        return a.reshape(bsz, n, c, a.shape[2], a.shape[3]).transpose(1, 0, 3, 2, 4)

    qc, kc, vc, wc = prep(q), prep(k), prep(v), prep(logw)
    mask = jnp.tril(jnp.ones((c, c), dtype=bool))

    def step(s, inp):
        qi, ki, vi, wi = inp
        b = jnp.cumsum(wi, axis=2)
        inter = jnp.einsum('bhtk,bhkv->bhtv', qi * jnp.exp(b), s)
        diff = b[:, :, :, None, :] - b[:, :, None, :, :]
        decay = jnp.where(mask[:, :, None], jnp.exp(jnp.minimum(diff, 0.0)), 0.0)
        scores = jnp.einsum('bhtk,bhsk,bhtsk->bhts', qi, ki, decay)
        intra = jnp.einsum('bhts,bhsv->bhtv', scores, vi)
        b_last = b[:, :, -1, :]
        s_new = jnp.exp(b_last)[..., None] * s + jnp.einsum(
            'bhsk,bhsv->bhkv', ki * jnp.exp(b_last[:, :, None, :] - b), vi)
        return s_new, inter + intra

    s_fin, o = lax.scan(step, s0, (qc, kc, vc, wc))
    o = o.transpose(1, 0, 3, 2, 4).reshape(bsz, n * c, GLA_HEADS, GLA_DV)[:, :t]
    return o, s_fin


def trunk_layer(x, p, conv_buf, s0, w_norm_mix_pre, w_in, w_conv, w_a_out, w_gk, b_gk, w_gla_norm,
                w_b_out, w_o, w_norm_mix_post, w_norm_ffn_pre, w_ffn_gate, w_ffn_up, w_ffn_down,
                w_norm_ffn_post, w_ple_proj, w_ple_gate, w_norm_ple_post):
    bsz, t, _ = x.shape
    hn = rms_norm(x, w_norm_mix_pre)
    z = jnp.einsum('btd,dn->btn', hn, w_in)
    idx = [int(i) for i in np.cumsum(SPLITS)[:-1]]
    b_a, c_a, x_a, q, k, v, g, gk_lr, gate_a, gate_b = jnp.split(z, idx, axis=-1)

    u = c_a * x_a
    up = jnp.concatenate([conv_buf.astype(u.dtype), u], axis=1)
    y_conv = (w_conv[0] * up[:, 0:t] + w_conv[1] * up[:, 1:t + 1] + w_conv[2] * up[:, 2:t + 2])
    new_buf = up[:, t:t + CONV_WIDTH - 1]
    y_a = jnp.einsum('btc,cd->btd', b_a * y_conv, w_a_out)

    qh = q.reshape(bsz, t, GLA_HEADS, GLA_DK).astype(jnp.float32) * (GLA_DK ** -0.5)
    kh = k.reshape(bsz, t, GLA_HEADS, GLA_DK).astype(jnp.float32)
    vh = v.reshape(bsz, t, GLA_HEADS, GLA_DV).astype(jnp.float32)
    gk = (jnp.einsum('btr,rk->btk', gk_lr, w_gk) + b_gk).astype(jnp.float32)
    logw = (jax.nn.log_sigmoid(gk) / GATE_NORMALIZER).reshape(bsz, t, GLA_HEADS, GLA_DK)
    o, s_new = gla_recurrence(qh, kh, vh, logw, s0.astype(jnp.float32))
    o = o * lax.rsqrt(jnp.mean(o * o, axis=-1, keepdims=True) + EPS) * w_gla_norm.astype(jnp.float32)
    o = (o * jax.nn.silu(g.reshape(bsz, t, GLA_HEADS, GLA_DV).astype(jnp.float32))).astype(x.dtype)
    y_b = jnp.einsum('btf,fd->btd', o.reshape(bsz, t, GLA_HEADS * GLA_DV), w_b_out)

    merged = jax.nn.sigmoid(gate_a) * y_a + jax.nn.sigmoid(gate_b) * y_b
    mix = jnp.einsum('btd,de->bte', merged, w_o)
    h = x + rms_norm(mix, w_norm_mix_post)

    f = rms_norm(h, w_norm_ffn_pre)
    f = jax.nn.silu(jnp.einsum('btd,df->btf', f, w_ffn_gate)) * jnp.einsum('btd,df->btf', f, w_ffn_up)
    f = jnp.einsum('btf,fd->btd', f, w_ffn_down)
    h = h + rms_norm(f, w_norm_ffn_post)

    e = jnp.einsum('btp,pd->btd', p, w_ple_proj) * jax.nn.sigmoid(jnp.einsum('btd,de->bte', h, w_ple_gate))
    h = h + rms_norm(e, w_norm_ple_post)
    return h, new_buf, s_new


def setup_inputs(seed: int = 0) -> dict:
    key = jax.random.key(seed)
    ks = jax.random.split(key, 32)
    f32 = jnp.float32

    def nrm(k, shape, scale):
        return jax.random.normal(k, shape, f32) * scale

    def gain(k, shape):
        return 1.0 + 0.05 * jax.random.normal(k, shape, f32)

    return {
        "x_prompt": nrm(ks[0], (BATCH, SEQ, D_MODEL), 1.0),
        "x_sample": nrm(ks[1], (DEC_BATCH, DEC_SEQ, D_MODEL), 1.0),
        "state_conv": nrm(ks[2], (DEPTH, DEC_BATCH, CONV_WIDTH - 1, CONV_DIM), 1.0),
        "state_gla": nrm(ks[3], (DEPTH, DEC_BATCH, GLA_HEADS, GLA_DK, GLA_DV), 0.5),
        "p_prompt": nrm(ks[4], (DEPTH, BATCH, SEQ, PLE_DIM), 1.0),
        "p_sample": nrm(ks[5], (DEPTH, DEC_BATCH, DEC_SEQ, PLE_DIM), 1.0),
        "w_norm_mix_pre": gain(ks[6], (DEPTH, D_MODEL)),
        "w_in": nrm(ks[7], (DEPTH, D_MODEL, N_IN_COLS), D_MODEL ** -0.5),
        "w_conv": nrm(ks[8], (DEPTH, CONV_WIDTH, CONV_DIM), CONV_WIDTH ** -0.5),
        "w_a_out": nrm(ks[9], (DEPTH, CONV_DIM, D_MODEL), CONV_DIM ** -0.5),
        "w_gk": nrm(ks[10], (DEPTH, GATE_RANK, GLA_HEADS * GLA_DK), GATE_RANK ** -0.5),
        "b_gk": nrm(ks[11], (DEPTH, GLA_HEADS * GLA_DK), 0.1),
        "w_gla_norm": gain(ks[12], (DEPTH, GLA_DV)),
        "w_b_out": nrm(ks[13], (DEPTH, GLA_HEADS * GLA_DV, D_MODEL), (GLA_HEADS * GLA_DV) ** -0.5),
        "w_o": nrm(ks[14], (DEPTH, D_MODEL, D_MODEL), D_MODEL ** -0.5),
        "w_norm_mix_post": gain(ks[15], (DEPTH, D_MODEL)),
        "w_norm_ffn_pre": gain(ks[16], (DEPTH, D_MODEL)),
        "w_ffn_gate": nrm(ks[17], (DEPTH, D_MODEL, D_FF), D_MODEL ** -0.5),
        "w_ffn_up": nrm(ks[18], (DEPTH, D_MODEL, D_FF), D_MODEL ** -0.5),
        "w_ffn_down": nrm(ks[19], (DEPTH, D_FF, D_MODEL), D_FF ** -0.5),
        "w_norm_ffn_post": gain(ks[20], (DEPTH, D_MODEL)),
        "w_ple_proj": nrm(ks[21], (DEPTH, PLE_DIM, D_MODEL), PLE_DIM ** -0.5),
        "w_ple_gate": nrm(ks[22], (DEPTH, D_MODEL, D_MODEL), D_MODEL ** -0.5),
        "w_norm_ple_post": gain(ks[23], (DEPTH, D_MODEL)),
    }


def reference(x_prompt, x_sample, state_conv, state_gla, p_prompt, p_sample,
              w_norm_mix_pre, w_in, w_conv, w_a_out, w_gk, b_gk, w_gla_norm, w_b_out, w_o,
              w_norm_mix_post, w_norm_ffn_pre, w_ffn_gate, w_ffn_up, w_ffn_down, w_norm_ffn_post,
              w_ple_proj, w_ple_gate, w_norm_ple_post):
    hp, hs = x_prompt, x_sample
    conv_p, gla_p, conv_s, gla_s = [], [], [], []
    for i in range(DEPTH):
        weights = (w_norm_mix_pre[i], w_in[i], w_conv[i], w_a_out[i], w_gk[i], b_gk[i], w_gla_norm[i],
                   w_b_out[i], w_o[i], w_norm_mix_post[i], w_norm_ffn_pre[i], w_ffn_gate[i], w_ffn_up[i],
                   w_ffn_down[i], w_norm_ffn_post[i], w_ple_proj[i], w_ple_gate[i], w_norm_ple_post[i])
        buf0 = jnp.zeros((BATCH, CONV_WIDTH - 1, CONV_DIM), x_prompt.dtype)
        s0 = jnp.zeros((BATCH, GLA_HEADS, GLA_DK, GLA_DV), jnp.float32)
        hp, cbp, sp = trunk_layer(hp, p_prompt[i], buf0, s0, *weights)
        hs, cbs, ss = trunk_layer(hs, p_sample[i], state_conv[i], state_gla[i], *weights)
        conv_p.append(cbp); gla_p.append(sp); conv_s.append(cbs); gla_s.append(ss)
    new_conv_prompt = jnp.stack(conv_p)
    new_gla_prompt = jnp.stack(gla_p)
    new_conv_sample = jnp.stack(conv_s)
    new_gla_sample = jnp.stack(gla_s)
    return (hp, hs, new_conv_prompt, new_gla_prompt, new_conv_sample, new_gla_sample)
```

```python
import contextlib
import numpy as np
import concourse.bass as bass
import concourse.mybir as mybir
from concourse.bass_utils import run_bass_kernel_spmd

F32 = mybir.dt.float32
BF16 = mybir.dt.bfloat16
AF = mybir.ActivationFunctionType
ALU = mybir.AluOpType

ENGS = ("sp", "pe", "act", "dve", "pool")
EPS = 1e-6
NST = 4
TP = 512
NS = 16
DFF = 2816
NFC = DFF // 128
import os
STRICT = os.environ.get("KSTRICT", "0") == "1"


class Buf:
    __slots__ = ("name", "last_w", "readers", "excl")

    def __init__(self, name, excl=False):
        self.name = name
        self.last_w = None
        self.readers = []
        self.excl = excl


class Op:
    __slots__ = ("eng", "fn", "deps", "is_dma", "signal", "sig_val", "sem", "prev_same_sem")

    def __init__(self, eng, fn, is_dma):
        self.eng = eng
        self.fn = fn
        self.is_dma = is_dma
        self.deps = []
        self.signal = is_dma
        self.sig_val = None
        self.sem = None
        self.prev_same_sem = None


class Prog:
    def __init__(self):
        self.ops = {e: [] for e in ENGS}
        self.n_dma_sems = {"sp": 12, "pool": 8}

    def op(self, eng, fn, reads=(), writes=(), dma=False):
        o = Op(eng, fn, dma)
        deps = {}

        def add(w, kind):
            cur = deps.get(id(w))
            if cur is None or kind < cur[1]:
                deps[id(w)] = (w, kind)

        for b in reads:
            if b.last_w is not None:
                add(b.last_w, 0)
            if b.excl:
                for r in b.readers:
                    add(r, 2)
        for b in writes:
            if b.last_w is not None:
                add(b.last_w, 1)
            for r in b.readers:
                add(r, 1)
        for w, kind in deps.values():
            if w is o:
                continue
            if (not w.is_dma) and (not dma) and w.eng == eng:
                if eng == "pe" or kind == 2 or (kind == 1 and not STRICT):
                    continue
            o.deps.append(w)
            w.signal = True
        for b in reads:
            b.readers.append(o)
        for b in writes:
            b.last_w = o
            b.readers = []
        self.ops[eng].append(o)
        return o

    def emit(self, nc, es):
        sems = {e: es.enter_context(nc.semaphore("s_" + e)) for e in ("pe", "act", "dve", "pool")}
        dsems = {q: [es.enter_context(nc.semaphore(f"d_{q}{i}")) for i in range(n)]
                 for q, n in self.n_dma_sems.items()}
        for e in ENGS:
            cnt = 0
            dcnt = 0
            last_on_sem = {}
            for o in self.ops[e]:
                if o.is_dma:
                    pool = dsems[e]
                    k = dcnt % len(pool)
                    o.sem = pool[k]
                    o.sig_val = 16 * (dcnt // len(pool) + 1)
                    o.prev_same_sem = last_on_sem.get(k)
                    last_on_sem[k] = o
                    dcnt += 1
                elif o.signal:
                    cnt += 1
                    o.sem = sems[e]
                    o.sig_val = cnt
        finals = []
        for q in dsems:
            last = {}
            for o in self.ops[q]:
                if o.is_dma:
                    last[id(o.sem)] = o
            finals.extend(last.values())
        block = es.enter_context(nc.Block())

        def run(e, h):
            waited = {}

            def wait(sem, val):
                if waited.get(id(sem), 0) >= val:
                    return
                h.wait_ge(sem, val)
                waited[id(sem)] = val

            for o in self.ops[e]:
                for d in o.deps:
                    wait(d.sem, d.sig_val)
                if o.is_dma and o.prev_same_sem is not None:
                    wait(o.prev_same_sem.sem, o.prev_same_sem.sig_val)
                ins = o.fn(h)
                if o.signal:
                    ins.then_inc(o.sem, 16 if o.is_dma else 1)
            if e == "sp":
                for o in finals:
                    wait(o.sem, o.sig_val)

        @block.sync
        def _(h):
            run("sp", h)

        @block.tensor
        def _(h):
            run("pe", h)

        @block.scalar
        def _(h):
            run("act", h)

        @block.vector
        def _(h):
            run("dve", h)

        @block.gpsimd
        def _(h):
            run("pool", h)


def handoff(src, dst):
    users = []
    for s in src:
        users.extend(s.readers)
        if s.last_w is not None:
            users.append(s.last_w)
    for d in dst:
        d.readers = list(d.readers) + users


SLOTW = 256
NSLOT = 8
LAG = 9


def build_program():
    nc = bass.Bass("TRN2", target_bir_lowering=False)
    P = Prog()
    es = contextlib.ExitStack()

    def din(name, shape):
        return nc.dram_tensor(name, shape, F32, kind="ExternalInput").ap()

    def dout(name, shape):
        return nc.dram_tensor(name, shape, F32, kind="ExternalOutput").ap()

    xp = din("xp", [NST * TP, 1024])
    xs = din("xs", [NS, 1024])
    sconv = din("sconv", [NS, 2, 1024])
    sgla = din("sgla", [NS, 4, 128, 256])
    pp_d = din("pp", [NST * TP, 256])
    ps_d = din("psm", [NS, 256])
    w_pre = din("w_pre", [1024])
    w_in = din("w_in", [1024, 8208])
    w_conv = din("w_conv", [3, 1024])
    w_a = din("w_a", [1024, 1024])
    w_gk = din("w_gk", [16, 512])
    b_gk = din("b_gk", [512])
    w_gn = din("w_gn", [256])
    w_b = din("w_b", [1024, 1024])
    w_o = din("w_o", [1024, 1024])
    w_mixpost = din("w_mixpost", [1024])
    w_ffnpre = din("w_ffnpre", [1024])
    w_fg = din("w_fg", [1024, DFF])
    w_fu = din("w_fu", [1024, DFF])
    w_fd = din("w_fd", [DFF, 1024])
    w_ffnpost = din("w_ffnpost", [1024])
    w_pp = din("w_pp", [256, 1024])
    w_pg = din("w_pg", [1024, 1024])
    w_plepost = din("w_plepost", [1024])
    yp = dout("yp", [NST * TP, 1024])
    ys = dout("ys", [NS, 1024])
    ncp = dout("ncp", [2, 1024])
    ngp = dout("ngp", [4, 128, 256])
    ncs = dout("ncs", [NS, 2, 1024])
    ngs = dout("ngs", [NS, 4, 128, 256])

    def sb(name, shape, dt):
        return es.enter_context(nc.sbuf_tensor(name, shape, dt))

    WM = TP + NS

    ident = sb("ident", [128, 128], BF16)
    identf = sb("identf", [128, 128], F32)
    tri = sb("tri", [128, 128], F32)
    wn_pre = sb("wn_pre", [128, 8], F32)
    wn_ffn = sb("wn_ffn", [128, 8], F32)
    wcv = sb("wcv", [128, 3, 8], F32)
    wgn_bc = sb("wgn_bc", [128, 256], F32)
    wbc = [sb(f"wbc{i}", [128, 1024], F32) for i in range(2)]
    Bwbc = [Buf("wbc0"), Buf("wbc1")]
    wgk = sb("wgk", [32, 512], F32)
    gkl = sb("gkl", [32, WM], F32)
    maskrow = sb("maskrow", [128, NS, 4, NS], BF16)
    uhist = sb("uhist", [128, 8, 2], F32)
    scT = sb("scT", [128, 2, 8, NS], F32)
    utail = sb("utail", [128, 8, 18], F32)
    S = sb("S", [128, 4, 256], F32)
    stat = sb("stat", [128, 16, 4], F32)
    ssg = sb("ssg", [128, 2, 8], F32)
    aS = sb("aS", [128, 4, NS], F32)
    aStmp = sb("aStmp", [128, 4, NS], F32)
    sqk = sb("sqk", [NS, 4], F32)
    QA = sb("QA", [128, 4, NS], BF16)
    BQA = Buf("QA")
    B = {}

    def mk(*names):
        for n in names:
            B[n] = Buf(n)

    mk("ident", "identf", "tri", "wn_pre", "wn_ffn", "wcv", "wgn_bc",
       "wgk", "gkl", "maskrow", "uhist", "scT", "utail", "S", "aS", "aStmp", "ssg0", "ssg1")
    statB = [Buf(f"stat{i}") for i in range(16)]

    ht = sb("ht", [128, 5, 1024], F32)
    hT = sb("hT", [128, 8, WM], BF16)
    Bht = [Buf(f"ht{i}") for i in range(5)]
    BhT = [Buf(f"hT{i}") for i in range(5)]

    regA = sb("regA", [128, 12672], BF16)
    merged_a = regA[:, 0:8448].bitcast(F32).rearrange("p (k c) -> p k c", k=8)
    mergedT = regA[:, 8448:12672].rearrange("p (k c) -> p k c", k=8)
    actT = regA[:, 0:NFC * WM].rearrange("p (k c) -> p k c", k=NFC)
    Bma = [Buf(f"ma{i}") for i in range(8)]
    BmT = [Buf(f"mT{i}") for i in range(8)]
    BactT = [Buf(f"actT{i}") for i in range(NFC)]

    regB = sb("regB", [128, NFC * 1024], BF16)
    o = 0
    goT = regB[:, o:o + 8 * WM].rearrange("p (k c) -> p k c", k=8); o += 8 * WM
    qT = regB[:, o:o + 4 * WM].rearrange("p (k c) -> p k c", k=4); o += 4 * WM
    kT = regB[:, o:o + 4 * WM].rearrange("p (k c) -> p k c", k=4); o += 4 * WM
    v_tok = regB[:, o:o + 5 * 1024].rearrange("p (k c) -> p k c", k=5); o += 5 * 1024
    sg_tok = regB[:, o:o + 5 * 1024].rearrange("p (k c) -> p k c", k=5); o += 5 * 1024
    gtmp = []
    for i in range(5):
        gtmp.append(regB[:, o:o + 512]); o += 512
    qe_a, ke_a, kd_a, scm_a = [t.rearrange("p (h c) -> p h c", h=4) for t in gtmp[0:4]]
    kdt_a = gtmp[4]
    assert o <= NFC * 1024, o
    wdn = regB[:, 0:NFC * 1024].rearrange("p (k c) -> p k c", k=NFC)
    Sbf_t = [sb(f"Sbf{i}", [128, 4, 256], BF16) for i in range(2)]
    Sbf = [t[:, :, :] for t in Sbf_t]
    BgoT = [Buf(f"goT{i}") for i in range(8)]
    BgoTc = [Buf(f"goTc{i}") for i in range(5)]
    Bq, Bk = Buf("qT"), Buf("kT")
    Bv = [Buf(f"v{i}") for i in range(5)]
    Bsg = [Buf(f"sg{i}") for i in range(5)]
    BSbf = [Buf("Sbf0"), Buf("Sbf1")]
    Bqe, Bke, Bkd, Bscm, Bkdt = [Buf(n) for n in ("qe", "ke", "kd", "scm", "kdt")]
    Bwdn = [Buf(f"wdn{i}") for i in range(6)]
    mixB_bufs = BgoT + BgoTc + [Bq, Bk] + Bv + Bsg + [Bqe, Bke, Bkd, Bscm, Bkdt]

    regC = sb("regC", [128, 4608], F32)
    regCc = sb("regCc", [128, 2116], F32)
    cS = regCc[:, 0:528]
    ubuf = [regCc[:, 528:1058], regCc[:, 1058:1588]]
    ycv = regCc[:, 1588:2116]
    e1 = regC[:, 0:512]
    lt = [regC[:, 512:1024], regC[:, 1024:1536]]
    eb_a = regC[:, 1536:2048].rearrange("p (h c) -> p h c", h=4)
    einv_a = regC[:, 2048:2560].rearrange("p (h c) -> p h c", h=4)
    ed_a = regC[:, 2560:3072].rearrange("p (h c) -> p h c", h=4)
    onb = [regC[:, 3072:3328], regC[:, 3328:3584]]
    oS = regC[0:NS, 3584:4608]
    BcS, Bu, Byc = Buf("cS"), [Buf("u0"), Buf("u1")], Buf("yc")
    Be1, Blt = Buf("e1"), [Buf("lt0"), Buf("lt1")]
    Beb, Beinv, Bed = Buf("eb"), Buf("einv"), Buf("ed")
    Bon = [Buf("on0"), Buf("on1")]
    BoS = Buf("oS")
    convC = [BcS, Byc] + Bu
    glaC = [Be1] + Blt + [Beb, Beinv, Bed] + Bon

    NBT = 2
    bigt = [sb(f"bigt{i}", [128, 1024], F32) for i in range(NBT)]
    Bbig = [Buf(f"bigt{i}") for i in range(NBT)]
    junk = sb("junk", [128, 1024], BF16)
    Bjunk = Buf("junk")
    SSbf = junk[:, :].rearrange("p (h c) -> p h c", h=4)
    hs = [sb(f"hs{i}", [128, 1024], BF16) for i in range(2)]
    Bhs = [Buf("hs0"), Buf("hs1")]
    ogt, Bogt = hs[0], Bhs[0]
    kS_tok = hs[1][0:NS, 0:512]
    KSj = hs[1][0:NS, 512:1024]
    BkS, BKSj = Buf("kS"), Buf("KSj")
    SS = [sb(f"SS{i}", [128, 4, 256], F32) for i in range(2)]
    BSS = [Buf("SS0"), Buf("SS1")]
    QmJ = [sb(f"QmJ{i}", [128, 4, NS], BF16) for i in range(2)]
    BQmJ = [Buf("QmJ0"), Buf("QmJ1")]
    pb16 = [sb(f"pb16{i}", [128, 256], BF16) for i in range(2)]
    Bpb16 = [Buf("pb160"), Buf("pb161")]
    pT = sb("pT", [128, 2, WM], BF16)
    BpT = [Buf(f"pT{i}") for i in range(5)]

    wslot = [sb(f"wslot{i}", [128, 8, SLOTW], BF16) for i in range(NSLOT)]
    Bws = [Buf(f"wslot{i}") for i in range(NSLOT)]

    psum = es.enter_context(nc.psum_tensor("psum", [128, 4096], F32))
    Bps = [Buf(f"psb{i}", excl=True) for i in range(8)]
    ps_state = {"next": 0, "lo": 0, "hi": 8}

    def ps_alloc(n):
        p = ps_state["next"]
        lo, hi = ps_state["lo"], ps_state["hi"]
        if p < lo or p >= hi:
            p = lo
        if n == 2 and (p % 2):
            p += 1
        if p + n > hi:
            p = lo
        ps_state["next"] = p + n
        return p * 512, Bps[p:p + n]

    def G(b, n):
        return b * 512, Bps[b:b + n]

    big_state = {"next": 0}

    def big_alloc():
        i = big_state["next"]
        big_state["next"] = (i + 1) % NBT
        return bigt[i], Bbig[i]

    stat_state = {"next": 0}

    def stat_alloc():
        i = stat_state["next"]
        stat_state["next"] = (i + 1) % 16
        return stat[:, i, :], statB[i]

    def ACT(out, in_, func, reads, writes, scale=None, bias=None, accum=None):
        kw = {}
        if scale is not None:
            kw["scale"] = scale
        if bias is not None:
            kw["bias"] = bias
        if accum is not None:
            kw["accum_out"] = accum
        P.op("act", lambda h: h.activation(out=out, in_=in_, func=func, **kw), reads, writes)

    def TT(eng, out, in0, in1, op, reads, writes):
        P.op(eng, lambda h: h.tensor_tensor(out=out, in0=in0, in1=in1, op=op), reads, writes)

    def TS(eng, out, in0, s1, op0, reads, writes):
        P.op(eng, lambda h: h.tensor_scalar(out=out, in0=in0, scalar1=s1, scalar2=None, op0=op0), reads, writes)

    def STT(eng, out, in0, scalar, in1, op0, op1, reads, writes):
        P.op(eng, lambda h: h.scalar_tensor_tensor(out=out, in0=in0, scalar=scalar, in1=in1, op0=op0, op1=op1), reads, writes)

    def CP(eng, out, in_, reads, writes):
        P.op(eng, lambda h: h.tensor_copy(out=out, in_=in_), reads, writes)

    def MS(eng, ap, val, writes):
        P.op(eng, lambda h: h.memset(ap, val), (), writes)

    def MM(lst, reads, writes):
        def fn(h):
            ins = None
            for (out, lhsT, rhs, start, stop) in lst:
                ins = h.matmul(out, lhsT=lhsT, rhs=rhs, start=start, stop=stop)
            return ins
        P.op("pe", fn, reads, writes)

    def TR(lst, reads, writes):
        def fn(h):
            ins = None
            for (out, in_, idn) in lst:
                ins = h.transpose(out=out, in_=in_, identity=idn)
            return ins
        P.op("pe", fn, reads, writes)

    def DMA(q, out, in_, reads, writes, noncontig=False):
        def fn(h):
            if noncontig:
                with nc.allow_non_contiguous_dma(reason="small strided constant load"):
                    return h.dma_start(out=out, in_=in_)
            return h.dma_start(out=out, in_=in_)
        P.op(q, fn, reads, writes, dma=True)

    MS("pool", identf[:], 1.0, [B["identf"]])
    P.op("pool", lambda h: h.affine_select(out=identf[:], in_=identf[:], pattern=[[-1, 128]], compare_op=ALU.is_equal,
                                           fill=0.0, base=0, channel_multiplier=1), [B["identf"]], [B["identf"]])
    CP("dve", ident[:], identf[:], [B["identf"]], [B["ident"]])
    MS("pool", tri[:], 1.0, [B["tri"]])
    P.op("pool", lambda h: h.affine_select(out=tri[:], in_=tri[:], pattern=[[1, 128]], compare_op=ALU.is_ge,
                                           fill=0.0, base=0, channel_multiplier=-1), [B["tri"]], [B["tri"]])
    DMA("sp", wn_pre[:], w_pre.rearrange("(kc p) -> p kc", p=128), [], [B["wn_pre"]], noncontig=True)
    DMA("sp", wn_ffn[:], w_ffnpre.rearrange("(kc p) -> p kc", p=128), [], [B["wn_ffn"]], noncontig=True)
    DMA("sp", wcv[:], w_conv.rearrange("j (kc p) -> p j kc", p=128), [], [B["wcv"]], noncontig=True)
    DMA("sp", wgn_bc[:], w_gn.partition_broadcast(128), [], [B["wgn_bc"]])
    DMA("sp", wgk[0:16, :], w_gk, [], [B["wgk"]])
    DMA("sp", wgk[16:17, :], b_gk.rearrange("(o n) -> o n", o=1), [], [B["wgk"]])
    MS("dve", gkl[:], 1.0, [B["gkl"]])
    MS("dve", uhist[:], 0.0, [B["uhist"]])
    MS("dve", S[:], 0.0, [B["S"]])
    MS("dve", Sbf[0], 0.0, [BSbf[0]])
    MS("dve", stat[:], 1.0, statB)
    MS("dve", maskrow[:], 0.0, [B["maskrow"]])
    for j in range(NS):
        MS("dve", maskrow[:, j, :, j:j + 1], 1.0, [B["maskrow"]])
    for t in range(2):
        bt, bb = big_alloc()
        DMA("sp", bt[0:NS, :], sconv[:, t, :], [], [bb])
        if t == 1:
            DMA("sp", ncs[:, 0, :], bt[0:NS, :], [bb], [])
        base, pbs = ps_alloc(1)
        TR([(psum[:, base + kc * NS:base + (kc + 1) * NS], bt[0:NS, kc * 128:(kc + 1) * 128], identf[0:NS, 0:NS])
            for kc in range(8)], [bb, B["identf"]], pbs)
        ACT(scT[:, t, :, :], psum[:, base:base + 8 * NS].rearrange("p (k c) -> p k c", k=8), AF.Copy, pbs, [B["scT"]])

    wbc_state = {"i": 0}

    def wbc_load(src):
        i = wbc_state["i"]
        wbc_state["i"] = 1 - i
        DMA("sp", wbc[i][:], src.partition_broadcast(128), [], [Bwbc[i]])
        return wbc[i], Bwbc[i]

    def build_groups():
        g = []
        for st in range(NST):
            for i in range(2):
                g.append(("q", w_in, 0, 8, 3072 + i * SLOTW, SLOTW))
            for i in range(2):
                g.append(("k", w_in, 0, 8, 3584 + i * SLOTW, SLOTW))
            g.append(("gklr", w_in, 0, 8, 6144, 16))
            for i in range(4):
                g.append(("v", w_in, 0, 8, 4096 + i * SLOTW, SLOTW))
            for i in range(4):
                g.append(("g", w_in, 0, 8, 5120 + i * SLOTW, SLOTW))
            for qd in range(4):
                g.append(("c", w_in, 0, 8, 1024 + qd * SLOTW, SLOTW))
                g.append(("x", w_in, 0, 8, 2048 + qd * SLOTW, SLOTW))
                g.append(("b", w_in, 0, 8, 0 + qd * SLOTW, SLOTW))
            for qd in range(4):
                g.append(("wa", w_a, 0, 8, qd * SLOTW, SLOTW))
                g.append(("ga", w_in, 0, 8, 6160 + qd * SLOTW, SLOTW))
            for qd in range(4):
                g.append(("wb", w_b, 0, 8, qd * SLOTW, SLOTW))
                g.append(("gb", w_in, 0, 8, 7184 + qd * SLOTW, SLOTW))
            for i in range(4):
                g.append(("wo", w_o, 0, 8, i * SLOTW, SLOTW))
            for gi in range(11):
                g.append(("fg", w_fg, 0, 8, gi * SLOTW, SLOTW))
                g.append(("fu", w_fu, 0, 8, gi * SLOTW, SLOTW))
                if gi < 6:
                    k0 = (gi // 2) * 8
                    nk = min(8, NFC - k0)
                    g.append(("WD", w_fd, k0, nk, (gi % 2) * 512, 512))
            g.append(("pp", w_pp, 0, 2, 0, 1024))
            for i in range(4):
                g.append(("pg", w_pg, 0, 8, i * SLOTW, SLOTW))
        return g

    groups = build_groups()
    GPS = len(groups) // NST
    ring_idx = {}
    wd_idx = {}
    for li in range(GPS):
        if groups[li][0] == "WD":
            wd_idx[li] = len(wd_idx)
        else:
            ring_idx[li] = len(ring_idx)
    wscr = nc.dram_tensor("wscr", [len(ring_idx), 128, 8 * SLOTW], BF16).ap()
    wdscr = nc.dram_tensor("wdscr", [6, 128, 8 * 512], BF16).ap()
    Bscr = [Buf(f"scr{i}") for i in range(len(ring_idx))]
    Bwdscr = [Buf(f"wdscr{i}") for i in range(6)]
    WS = {"next_load": 0, "next_use": 0, "free": list(range(NSLOT)), "slot_of": {}, "wd_ok": False, "wd_idx": 0}

    def w_prefetch():
        while WS["next_load"] < len(groups):
            gi = WS["next_load"]
            name, mat, k0, nk, c0, nco = groups[gi]
            st_, li = gi // GPS, gi % GPS
            src = mat[k0 * 128:(k0 + nk) * 128, c0:c0 + nco].rearrange("(kc p) n -> p kc n", p=128)
            if name == "WD":
                if not WS["wd_ok"]:
                    return
                piece = wd_idx[li]
                dst = wdn[:, k0:k0 + nk, c0:c0 + nco]
                scr = wdscr[piece][:, 0:nk * 512].rearrange("p (k c) -> p k c", k=nk)
                conv_st = piece % 3
                if st_ <= conv_st:
                    DMA("pool", dst, src, [], [Bwdn[piece]])
                    if st_ == conv_st:
                        DMA("sp", scr, dst, [Bwdn[piece]], [Bwdscr[piece]])
                else:
                    DMA("pool", dst, scr, [Bwdscr[piece]], [Bwdn[piece]])
                WS["next_load"] += 1
                continue
            if not WS["free"]:
                return
            s = WS["free"].pop(0)
            ri = ring_idx[li]
            img = wslot[s][:, :, :].rearrange("p k c -> p (k c)")
            conv_st = ri % 3
            if st_ <= conv_st:
                if name == "pp":
                    dst = wslot[s][:, :, :].rearrange("p (a b) n -> p a (b n)", a=2)
                else:
                    dst = wslot[s][:, 0:nk, 0:nco]
                if name == "gklr":
                    MS("pool", wslot[s][:, :, :], 0.0, [Bws[s]])
                DMA("pool", dst, src, [], [Bws[s]])
                if st_ == conv_st:
                    DMA("sp", wscr[ri], img, [Bws[s]], [Bscr[ri]])
            else:
                DMA("pool", img, wscr[ri], [Bscr[ri]], [Bws[s]])
            WS["slot_of"][gi] = s
            WS["next_load"] += 1

    def w_acquire(name):
        while groups[WS["next_use"]][0] == "WD":
            WS["next_use"] += 1
        gi = WS["next_use"]
        assert groups[gi][0] == name, (groups[gi][0], name)
        if gi not in WS["slot_of"]:
            w_prefetch()
        assert gi in WS["slot_of"], ("weight ring deadlock", name, gi)
        WS["next_use"] += 1
        return WS["slot_of"][gi]

    def w_release(*slots):
        for s in slots:
            WS["free"].append(s)
        w_prefetch()

    def mm_fm(base, pbs, slot, j, src, srcB, W, M=128):
        lst = []
        blocks = [(0, TP)] + ([(TP, W - TP)] if W > TP else [])
        for (c0, cn) in blocks:
            for kc in range(8):
                lst.append((psum[0:M, base + c0:base + c0 + cn], wslot[slot][:, kc, j * 128:j * 128 + M],
                            src[:, kc, c0:c0 + cn], kc == 0, kc == 7))
        MM(lst, [Bws[slot]] + srcB, pbs)

    def rstd_from(ssap, sB, R, n):
        ACT(ssap[0:R, 1:2], ssap[0:R, 0:1], AF.Ln, [sB], [sB], scale=1.0 / n, bias=EPS)
        ACT(ssap[0:R, 2:3], ssap[0:R, 1:2], AF.Exp, [sB], [sB], scale=-0.5)

    def fm_tile(ti, tt, col0, R, wn, wnB, norm=True):
        par = ti % 2
        if norm:
            sa, sB = stat_alloc()
            ACT(junk[0:R, :], ht[0:R, tt, :], AF.Square, [Bht[tt]], [Bjunk, sB], accum=sa[0:R, 0:1])
            rstd_from(sa, sB, R, 1024)
            TS("dve", hs[par][0:R, :], ht[0:R, tt, :], sa[0:R, 2:3], ALU.mult, [Bht[tt], sB], [Bhs[par]])
        else:
            CP("dve", hs[par][0:R, :], ht[0:R, tt, :], [Bht[tt]], [Bhs[par]])
        base, pbs = ps_alloc(1)
        pv = psum[:, base:base + 512].bitcast(BF16).rearrange("p (k c) -> p k c", k=8)
        TR([(pv[:, kc, 0:R], hs[par][0:R, kc * 128:(kc + 1) * 128], ident[0:R, 0:R]) for kc in range(8)],
           [Bhs[par], B["ident"]], pbs)
        if wn is not None:
            TT("dve", hT[:, :, col0:col0 + R], pv[:, :, 0:R], wn[:, :].unsqueeze(2).to_broadcast([128, 8, R]),
               ALU.mult, pbs + [wnB], [BhT[tt]])
        else:
            ACT(hT[:, :, col0:col0 + R], pv[:, :, 0:R], AF.Copy, pbs, [BhT[tt]])

    def norm_resid(base, pbs, R, tt, wb_, wbB):
        src = psum[0:R, base:base + 1024]
        sa, sB = stat_alloc()
        ACT(junk[0:R, :], src, AF.Square, pbs, [Bjunk, sB], accum=sa[0:R, 0:1])
        rstd_from(sa, sB, R, 1024)
        bt, bb = big_alloc()
        TT("dve", bt[0:R, :], src, wb_[0:R, :], ALU.mult, pbs + [wbB], [bb])
        STT("dve", ht[0:R, tt, :], bt[0:R, :], sa[0:R, 2:3], ht[0:R, tt, :], ALU.mult, ALU.add, [bb, sB, Bht[tt]], [Bht[tt]])

    preA = set()
    for st in range(NST):
        last = st == NST - 1
        W = WM if last else TP
        nbW = 2 if last else 1
        tiles = [(tt, tt * 128, 128) for tt in range(4)] + ([(4, TP, NS)] if last else [])
        allhT = [BhT[t[0]] for t in tiles]

        if st > 0:
            handoff(Bwdn, mixB_bufs)
            handoff(BactT, Bma + BmT)
            WS["wd_ok"] = False

        if st == 0:
            for tt in range(4):
                DMA("sp", ht[:, tt, :], xp[tt * 128:(tt + 1) * 128, :], [], [Bht[tt]])
            DMA("sp", ht[0:NS, 4, :], xs, [], [Bht[4]])
        w_prefetch()
        for ti, (tt, col0, R) in enumerate(tiles):
            if (st, tt) not in preA:
                fm_tile(ti, tt, col0, R, wn_pre, B["wn_pre"])
        wb_mix, Bwb_mix = wbc_load(w_mixpost)

        for (nm, dstT, dB, scl) in (("q", qT, Bq, float(128 ** -0.5)), ("k", kT, Bk, None)):
            for i in range(2):
                s_ = w_acquire(nm)
                for j in range(2):
                    h_ = i * 2 + j
                    b_, p_ = ps_alloc(nbW)
                    mm_fm(b_, p_, s_, j, hT, allhT, W)
                    ACT(dstT[:, h_, 0:W], psum[:, b_:b_ + W], AF.Copy, p_, [dB], scale=scl)
                w_release(s_)
        sg_ = w_acquire("gklr")
        b_, p_ = ps_alloc(nbW)
        mm_fm(b_, p_, sg_, 0, hT, allhT, W, M=16)
        ACT(gkl[0:16, 0:W], psum[0:16, b_:b_ + W], AF.Copy, p_, [B["gkl"]])
        w_release(sg_)
        for (nm, dstt, dB, fn) in (("v", v_tok, Bv, AF.Copy), ("g", sg_tok, Bsg, AF.Silu)):
            for half in range(2):
                s0 = w_acquire(nm)
                s1 = w_acquire(nm)
                for (tt, col0, R) in tiles:
                    b_, p_ = ps_alloc(1)
                    lst = []
                    for i, s_ in enumerate((s0, s1)):
                        for kc in range(8):
                            lst.append((psum[0:R, b_ + i * SLOTW:b_ + (i + 1) * SLOTW], hT[:, kc, col0:col0 + R],
                                        wslot[s_][:, kc, :], kc == 0, kc == 7))
                    MM(lst, [BhT[tt], Bws[s0], Bws[s1]], p_)
                    ACT(dstt[0:R, tt, half * 512:(half + 1) * 512], psum[0:R, b_:b_ + 512], fn, p_, [dB[tt]])
                w_release(s0, s1)


        def gla_post(src_h, srcB, R, tt, col0, par):
            sg_ap = ssg[:, par, :]
            sgB = B[f"ssg{par}"]
            for h_ in range(4):
                ACT(junk[0:R, 0:256], src_h(h_), AF.Square, srcB, [Bjunk, sgB], accum=sg_ap[0:R, h_:h_ + 1])
            ACT(sg_ap[0:R, 4:8], sg_ap[0:R, 0:4], AF.Ln, [sgB], [sgB], scale=1.0 / 256, bias=EPS)
            ACT(sg_ap[0:R, 4:8], sg_ap[0:R, 4:8], AF.Exp, [sgB], [sgB], scale=-0.5)
            for h_ in range(4):
                hp = h_ % 2
                STT("dve", onb[hp][0:R, :], src_h(h_), sg_ap[0:R, 4 + h_:5 + h_],
                    wgn_bc[0:R, :], ALU.mult, ALU.mult, srcB + [sgB, B["wgn_bc"]], [Bon[hp]])
                TT("dve", ogt[0:R, h_ * 256:(h_ + 1) * 256], onb[hp][0:R, :], sg_tok[0:R, tt, h_ * 256:(h_ + 1) * 256],
                   ALU.mult, [Bon[hp], Bsg[tt]], [Bogt])
            base, pbs = G(0, 1)
            pv = psum[:, base:base + 512].bitcast(BF16).rearrange("p (k c) -> p k c", k=8)
            TR([(pv[:, kc, 0:R], ogt[0:R, kc * 128:(kc + 1) * 128], ident[0:R, 0:R]) for kc in range(8)],
               [Bogt, B["ident"]], pbs)
            ACT(goT[:, :, col0:col0 + R], pv[:, :, 0:R], AF.Copy, pbs, [BgoTc[tt]] + BgoT)

        def sample_prep():
            bgk, pgk = G(0, 1)
            MM([(psum[:, bgk + h_ * NS:bgk + (h_ + 1) * NS], wgk[0:17, h_ * 128:(h_ + 1) * 128], gkl[0:17, TP:TP + NS], True, True)
                for h_ in range(4)], [B["gkl"], B["wgk"]], pgk)
            pgv = psum[:, bgk:bgk + 4 * NS].rearrange("p (k c) -> p k c", k=4)
            ACT(aStmp[:, :, :], pgv, AF.Exp, pgk, [B["aStmp"]], scale=-1.0)
            ACT(aStmp[:, :, :], aStmp[:, :, :], AF.Ln, [B["aStmp"]], [B["aStmp"]], bias=1.0)
            ACT(aS[:, :, :], aStmp[:, :, :], AF.Exp, [B["aStmp"]], [B["aS"]], scale=-1.0 / 16)
            TT("dve", QA[:, :, :], qT[:, :, TP:TP + NS], aS[:, :, :], ALU.mult, [Bq, B["aS"]], [BQA])
            bkt, pkt = G(1, 1)
            pktv = psum[:, bkt:bkt + 512].bitcast(BF16)
            TR([(pktv[0:NS, h_ * 128:(h_ + 1) * 128], kT[:, h_, TP:TP + NS], ident[:, :]) for h_ in range(4)] +
               [(pktv[0:NS, 512 + h_ * 128:512 + (h_ + 1) * 128], qT[:, h_, TP:TP + NS], ident[:, :]) for h_ in range(4)],
               [Bk, Bq, B["ident"]], pkt)
            ACT(kS_tok, pktv[0:NS, 0:512], AF.Copy, pkt, [BkS])
            bt, bb = big_alloc()
            TT("dve", bt[0:NS, 0:512], pktv[0:NS, 512:1024], kS_tok, ALU.mult, pkt + [BkS], [bb])
            for h_ in range(4):
                ACT(junk[0:NS, 0:128], bt[0:NS, h_ * 128:(h_ + 1) * 128], AF.Copy, [bb], [Bjunk, B["aStmp"]],
                    accum=sqk[0:NS, h_:h_ + 1])
            for h_ in range(4):
                TS("dve", oS[:, h_ * 256:(h_ + 1) * 256], v_tok[0:NS, 4, h_ * 256:(h_ + 1) * 256], sqk[0:NS, h_:h_ + 1],
                   ALU.mult, [Bv[4], B["aStmp"]], [BoS])

        def sample_load(j):
            DMA("sp", SS[j % 2][:, :, :], sgla[j].rearrange("h k v -> k h v"), [], [BSS[j % 2]])

        def sample_iter(j):
            jp = j % 2
            if j == 0:
                sample_load(0)
            if j + 1 < NS:
                sample_load(j + 1)
            ACT(SSbf, SS[jp][:, :, :], AF.Copy, [BSS[jp]], [Bjunk])
            TS("dve", KSj, kS_tok, identf[0:NS, j:j + 1], ALU.mult, [BkS, B["identf"]], [BKSj])
            TT("dve", QmJ[jp][:, :, :], QA[:, :, :], maskrow[:, j, :, :], ALU.mult, [BQA, B["maskrow"]], [BQmJ[jp]])
            bkv, pkv = G(0, 2)
            MM([(psum[:, bkv + h_ * 256:bkv + (h_ + 1) * 256], KSj[:, h_ * 128:(h_ + 1) * 128],
                 v_tok[0:NS, 4, h_ * 256:(h_ + 1) * 256], True, True) for h_ in range(4)], [BKSj, Bv[4]], pkv)
            bo, pbo = G(2, 2)
            MM([(psum[0:NS, bo + h_ * 256:bo + (h_ + 1) * 256], QmJ[jp][:, h_, :], SSbf[:, h_, :], True, True)
                for h_ in range(4)], [BQmJ[jp], Bjunk], pbo)
            for h_ in range(4):
                STT("dve", SS[jp][:, h_, :], SS[jp][:, h_, :], aS[:, h_, j:j + 1], psum[:, bkv + h_ * 256:bkv + (h_ + 1) * 256],
                    ALU.mult, ALU.add, [BSS[jp], B["aS"]] + pkv, [BSS[jp]])
            DMA("sp", ngs[j].rearrange("h k v -> k h v"), SS[jp][:, :, :], [BSS[jp]], [])
            TT("dve", oS, oS, psum[0:NS, bo:bo + 1024], ALU.add, pbo + [BoS], [BoS])

        def conv_gen():
            for qd in range(4):
                sc = w_acquire("c")
                sx = w_acquire("x")
                sbk = w_acquire("b")
                for j in range(2):
                    kc = qd * 2 + j
                    par = kc % 2
                    u = ubuf[par]
                    bc, pc = ps_alloc(nbW)
                    mm_fm(bc, pc, sc, j, hT, allhT, W)
                    ACT(cS[:, 0:W], psum[:, bc:bc + W], AF.Copy, pc, [BcS])
                    bx, px = ps_alloc(nbW)
                    mm_fm(bx, px, sx, j, hT, allhT, W)
                    TT("dve", u[:, 2:2 + W], psum[:, bx:bx + W], cS[:, 0:W], ALU.mult, px + [BcS], [Bu[par]])
                    CP("dve", u[:, 0:2], uhist[:, kc, :], [B["uhist"]], [Bu[par]])
                    CP("dve", uhist[:, kc, :], u[:, TP:TP + 2], [Bu[par]], [B["uhist"]])
                    TS("dve", ycv[:, 0:TP], u[:, 0:TP], wcv[:, 0, kc:kc + 1], ALU.mult, [Bu[par], B["wcv"]], [Byc])
                    STT("dve", ycv[:, 0:TP], u[:, 1:TP + 1], wcv[:, 1, kc:kc + 1], ycv[:, 0:TP], ALU.mult, ALU.add,
                        [Bu[par], B["wcv"], Byc], [Byc])
                    STT("dve", ycv[:, 0:TP], u[:, 2:TP + 2], wcv[:, 2, kc:kc + 1], ycv[:, 0:TP], ALU.mult, ALU.add,
                        [Bu[par], B["wcv"], Byc], [Byc])
                    if last:
                        TS("dve", ycv[:, TP:W], scT[:, 0, kc, :], wcv[:, 0, kc:kc + 1], ALU.mult, [B["scT"], B["wcv"]], [Byc])
                        STT("dve", ycv[:, TP:W], scT[:, 1, kc, :], wcv[:, 1, kc:kc + 1], ycv[:, TP:W], ALU.mult, ALU.add,
                            [B["scT"], B["wcv"], Byc], [Byc])
                        STT("dve", ycv[:, TP:W], u[:, TP + 2:W + 2], wcv[:, 2, kc:kc + 1], ycv[:, TP:W], ALU.mult, ALU.add,
                            [Bu[par], B["wcv"], Byc], [Byc])
                        CP("dve", utail[:, kc, :], u[:, TP:TP + 18], [Bu[par]], [B["utail"]])
                    bb_, pb_ = ps_alloc(nbW)
                    mm_fm(bb_, pb_, sbk, j, hT, allhT, W)
                    TT("dve", mergedT[:, kc, 0:W], psum[:, bb_:bb_ + W], ycv[:, 0:W], ALU.mult, pb_ + [Byc], [BmT[kc]])
                    yield
                w_release(sc, sx, sbk)
            if last:
                base, pbs = ps_alloc(2)
                TR([(psum[0:18, base + kc * 128:base + (kc + 1) * 128], utail[:, kc, :], identf[:, :]) for kc in range(8)],
                   [B["utail"], B["identf"]], pbs)
                bt, bb = big_alloc()
                ACT(bt[0:18, :], psum[0:18, base:base + 1024], AF.Copy, pbs, [bb])
                DMA("sp", ncp[:, :], bt[0:2, :], [bb], [])
                DMA("sp", ncs[:, 1, :], bt[2:18, :], [bb], [])

            for qd in range(4):
                swa = w_acquire("wa")
                sga = w_acquire("ga")
                for j in range(2):
                    kc = qd * 2 + j
                    by, py = ps_alloc(nbW)
                    mm_fm(by, py, swa, j, mergedT, BmT, W)
                    bg, pg = ps_alloc(nbW)
                    mm_fm(bg, pg, sga, j, hT, allhT, W)
                    bt, bb = big_alloc()
                    ACT(bt[:, 0:W], psum[:, bg:bg + W], AF.Tanh, pg, [bb], scale=0.5)
                    ACT(bt[:, 0:W], bt[:, 0:W], AF.Identity, [bb], [bb], scale=0.5, bias=0.5)
                    TT("dve", merged_a[:, kc, 0:W], psum[:, by:by + W], bt[:, 0:W], ALU.mult, py + [bb], [Bma[kc]])
                    yield
                w_release(swa, sga)


        def gla_gen():
            if last:
                sample_prep()
                yield
            def stage1(c):
                par_ = (st * 4 + c) % 2
                cols_ = slice(c * 128, (c + 1) * 128)
                bgk, pgk = G(0, 1)
                MM([(psum[:, bgk + q_ * 128:bgk + (q_ + 1) * 128], gkl[0:17, cols_], wgk[0:17, q_ * 128:(q_ + 1) * 128], True, True)
                    for q_ in range(4)], [B["gkl"], B["wgk"]], pgk)
                ACT(e1, psum[:, bgk:bgk + 512], AF.Exp, pgk, [Be1], scale=-1.0)
                ACT(lt[par_], e1, AF.Ln, [Be1], [Blt[par_]], bias=1.0)

            stage1(0)
            yield
            for c in range(4):
                cidx = st * 4 + c
                par = cidx % 2
                cols = slice(c * 128, (c + 1) * 128)
                bpb, ppb = G(1, 1)
                MM([(psum[:, bpb + h_ * 128:bpb + (h_ + 1) * 128], lt[par][:, h_ * 128:(h_ + 1) * 128], tri[:, :], True, True)
                    for h_ in range(4)], [Blt[par], B["tri"]], ppb)
                pbv = psum[:, bpb:bpb + 512].rearrange("p (h c) -> p h c", h=4)
                ACT(eb_a, pbv, AF.Exp, ppb, [Beb], scale=-1.0 / 16)
                ACT(einv_a, pbv, AF.Exp, ppb, [Beinv], scale=1.0 / 16)
                TT("dve", ed_a, einv_a, eb_a[:, :, 127:128].to_broadcast([128, 4, 128]), ALU.mult, [Beinv, Beb], [Bed])
                TT("dve", qe_a, qT[:, :, cols], eb_a, ALU.mult, [Bq, Beb], [Bqe])
                TT("dve", ke_a, kT[:, :, cols], einv_a, ALU.mult, [Bk, Beinv], [Bke])
                TT("dve", kd_a, kT[:, :, cols], ed_a, ALU.mult, [Bk, Bed], [Bkd])
                if c + 1 < 4:
                    stage1(c + 1)
                yield
                bkd, pkd = G(0, 1)
                pkdv = psum[:, bkd:bkd + 512].bitcast(BF16)
                TR([(pkdv[:, h_ * 128:(h_ + 1) * 128], kd_a[:, h_, :], ident[:, :]) for h_ in range(4)], [Bkd, B["ident"]], pkd)
                ACT(kdt_a, pkdv[:, 0:512], AF.Copy, pkd, [Bkdt])
                bsc, psc_ = G(1, 1)
                MM([(psum[:, bsc + h_ * 128:bsc + (h_ + 1) * 128], ke_a[:, h_, :], qe_a[:, h_, :], True, True) for h_ in range(4)],
                   [Bke, Bqe], psc_)
                TT("dve", scm_a, psum[:, bsc:bsc + 512].rearrange("p (h c) -> p h c", h=4),
                   tri[:, :].unsqueeze(1).to_broadcast([128, 4, 128]), ALU.mult, psc_ + [B["tri"]], [Bscm])
                yield
                po, ppo = G(2, 2)
                lst = []
                for h_ in range(4):
                    vh = v_tok[:, c, h_ * 256:(h_ + 1) * 256]
                    lst.append((psum[:, po + h_ * 256:po + (h_ + 1) * 256], scm_a[:, h_, :], vh, True, False))
                    lst.append((psum[:, po + h_ * 256:po + (h_ + 1) * 256], qe_a[:, h_, :], Sbf[par][:, h_, :], False, True))
                MM(lst, [Bscm, Bv[c], Bqe, BSbf[par]], ppo)
                bD, pD_ = G(0, 2)
                MM([(psum[:, bD + h_ * 256:bD + (h_ + 1) * 256], kdt_a[:, h_ * 128:(h_ + 1) * 128],
                     v_tok[:, c, h_ * 256:(h_ + 1) * 256], True, True) for h_ in range(4)], [Bkdt, Bv[c]], pD_)
                for h_ in range(4):
                    STT("dve", S[:, h_, :], S[:, h_, :], eb_a[:, h_, 127:128], psum[:, bD + h_ * 256:bD + (h_ + 1) * 256],
                        ALU.mult, ALU.add, [B["S"], Beb] + pD_, [B["S"]])
                ACT(Sbf[1 - par], S[:, :, :], AF.Copy, [B["S"]], [BSbf[1 - par]])
                yield
                gla_post(lambda h_, po=po: psum[:, po + h_ * 256:po + (h_ + 1) * 256], ppo, 128, c, c * 128, par)
                yield
                if last:
                    for j in range(c * 4, c * 4 + 4):
                        sample_iter(j)
                        yield
            if last:
                DMA("sp", ngp.rearrange("h k v -> k h v"), S[:, :, :], [B["S"]], [])
                gla_post(lambda h_: oS[:, h_ * 256:(h_ + 1) * 256], [BoS], NS, 4, TP, 0)


        ps_state["lo"], ps_state["hi"] = 4, 8
        g_gla, g_conv = gla_gen(), conv_gen()
        n_gla = 17 + (17 if last else 0)
        n_conv = 16
        done_g = 0
        for i_ in range(n_conv):
            while done_g * n_conv < (i_ + 1) * n_gla:
                if next(g_gla, "end") == "end":
                    break
                done_g += 1
            next(g_conv, None)
        for _ in g_gla:
            pass
        for _ in g_conv:
            pass
        ps_state["lo"], ps_state["hi"] = 0, 8

        for qd in range(4):
            swb = w_acquire("wb")
            sgb = w_acquire("gb")
            for j in range(2):
                kc = qd * 2 + j
                by, py = ps_alloc(nbW)
                mm_fm(by, py, swb, j, goT, BgoT + BgoTc[0:len(tiles)], W)
                bg, pg = ps_alloc(nbW)
                mm_fm(bg, pg, sgb, j, hT, allhT, W)
                bt, bb = big_alloc()
                ACT(bt[:, 0:W], psum[:, bg:bg + W], AF.Tanh, pg, [bb], scale=0.5)
                ACT(bt[:, 0:W], bt[:, 0:W], AF.Identity, [bb], [bb], scale=0.5, bias=0.5)
                TT("dve", bt[:, 0:W], psum[:, by:by + W], bt[:, 0:W], ALU.mult, py + [bb], [bb])
                TT("dve", mergedT[:, kc, 0:W], bt[:, 0:W], merged_a[:, kc, 0:W], ALU.add, [bb, Bma[kc]], [BmT[kc]])
            w_release(swb, sgb)
        handoff(mixB_bufs, Bwdn)
        WS["wd_ok"] = True
        wb_ffn, Bwb_ffn = wbc_load(w_ffnpost)

        so = [w_acquire("wo") for _ in range(4)]
        for idx, (tt, col0, R) in enumerate(tiles):
            b_, p_ = ps_alloc(2)
            lst = []
            for i, s_ in enumerate(so):
                for kc in range(8):
                    lst.append((psum[0:R, b_ + i * SLOTW:b_ + (i + 1) * SLOTW], mergedT[:, kc, col0:col0 + R],
                                wslot[s_][:, kc, :], kc == 0, kc == 7))
            MM(lst, BmT + [Bws[s_] for s_ in so], p_)
            norm_resid(b_, p_, R, tt, wb_mix, Bwb_mix)
            if idx >= LAG:
                fm_tile(idx - LAG, *tiles[idx - LAG], wn_ffn, B["wn_ffn"])
        for i_ in range(max(0, len(tiles) - LAG), len(tiles)):
            fm_tile(i_, *tiles[i_], wn_ffn, B["wn_ffn"])
        w_release(*so)
        handoff(Bma + BmT, BactT)
        wb_ple, Bwb_ple = wbc_load(w_plepost)

        for gi in range(11):
            sfg = w_acquire("fg")
            sfu = w_acquire("fu")
            for j in range(2):
                ci = gi * 2 + j
                bg, pg = ps_alloc(nbW)
                mm_fm(bg, pg, sfg, j, hT, allhT, W)
                bu, pu = ps_alloc(nbW)
                mm_fm(bu, pu, sfu, j, hT, allhT, W)
                bt, bb = big_alloc()
                ACT(bt[:, 0:W], psum[:, bg:bg + W], AF.Silu, pg, [bb])
                TT("dve", actT[:, ci, 0:W], psum[:, bu:bu + W], bt[:, 0:W], ALU.mult, pu + [bb], [BactT[ci]])
            w_release(sfg, sfu)
        def ple_prep(ti, tt, col0, R):
            par = ti % 2
            src = pp_d[st * TP + tt * 128: st * TP + tt * 128 + 128, :] if tt < 4 else ps_d
            bt, bb = big_alloc()
            DMA("sp", bt[0:R, 0:256], src, [], [bb])
            CP("dve", pb16[par][0:R, :], bt[0:R, 0:256], [bb], [Bpb16[par]])
            base, pbs = ps_alloc(1)
            pv = psum[:, base:base + 512].bitcast(BF16).rearrange("p (k c) -> p k c", k=8)
            TR([(pv[:, kc, 0:R], pb16[par][0:R, kc * 128:(kc + 1) * 128], ident[0:R, 0:R]) for kc in range(2)],
               [Bpb16[par], B["ident"]], pbs)
            ACT(pT[:, :, col0:col0 + R], pv[:, 0:2, 0:R], AF.Copy, pbs, [BpT[tt]])
            fm_tile(ti, tt, col0, R, None, None, norm=False)

        for idx, (tt, col0, R) in enumerate(tiles):
            b_, p_ = ps_alloc(2)
            lst = []
            for half in range(2):
                for kc in range(NFC):
                    lst.append((psum[0:R, b_ + half * 512:b_ + (half + 1) * 512], actT[:, kc, col0:col0 + R],
                                wdn[:, kc, half * 512:(half + 1) * 512], kc == 0, kc == NFC - 1))
            MM(lst, BactT + Bwdn, p_)
            norm_resid(b_, p_, R, tt, wb_ffn, Bwb_ffn)
            if idx >= LAG:
                ple_prep(idx - LAG, *tiles[idx - LAG])
        for i_ in range(max(0, len(tiles) - LAG), len(tiles)):
            ple_prep(i_, *tiles[i_])

        spp = w_acquire("pp")
        spg = [w_acquire("pg") for _ in range(4)]
        wppv = wslot[spp][:, :, :].rearrange("p (a b) n -> p a (b n)", a=2)
        for idx, (tt, col0, R) in enumerate(tiles):
            bp, ppp = ps_alloc(2)
            lst = []
            for half in range(2):
                for kc in range(2):
                    lst.append((psum[0:R, bp + half * 512:bp + (half + 1) * 512], pT[:, kc, col0:col0 + R],
                                wppv[:, kc, half * 512:(half + 1) * 512], kc == 0, kc == 1))
            MM(lst, [BpT[tt], Bws[spp]], ppp)
            bg, pg = ps_alloc(2)
            lst = []
            for i, s_ in enumerate(spg):
                for kc in range(8):
                    lst.append((psum[0:R, bg + i * SLOTW:bg + (i + 1) * SLOTW], hT[:, kc, col0:col0 + R],
                                wslot[s_][:, kc, :], kc == 0, kc == 7))
            MM(lst, [BhT[tt]] + [Bws[s_] for s_ in spg], pg)
            bt, bb = big_alloc()
            ACT(bt[0:R, :], psum[0:R, bg:bg + 1024], AF.Sigmoid, pg, [bb])
            TT("dve", bt[0:R, :], psum[0:R, bp:bp + 1024], bt[0:R, :], ALU.mult, ppp + [bb], [bb])
            sa, sB = stat_alloc()
            ACT(junk[0:R, :], bt[0:R, :], AF.Square, [bb], [Bjunk, sB], accum=sa[0:R, 0:1])
            rstd_from(sa, sB, R, 1024)
            TT("dve", bt[0:R, :], bt[0:R, :], wb_ple[0:R, :], ALU.mult, [bb, Bwb_ple], [bb])
            STT("dve", ht[0:R, tt, :], bt[0:R, :], sa[0:R, 2:3], ht[0:R, tt, :], ALU.mult, ALU.add, [bb, sB, Bht[tt]], [Bht[tt]])
            dst = yp[st * TP + tt * 128: st * TP + tt * 128 + 128, :] if tt < 4 else ys
            DMA("sp", dst, ht[0:R, tt, :], [Bht[tt]], [])
            if not last:
                r0 = (st + 1) * TP + tt * 128
                DMA("sp", ht[:, tt, :], xp[r0:r0 + 128, :], [], [Bht[tt]])
                if idx >= LAG:
                    fm_tile(idx - LAG, *tiles[idx - LAG], wn_pre, B["wn_pre"])
                    preA.add((st + 1, tiles[idx - LAG][0]))
        w_release(spp, *spg)

    P.emit(nc, es)
    es.close()
    return nc


_NC_CACHE = {}


def kernel(x_prompt, x_sample, state_conv, state_gla, p_prompt, p_sample,
           w_norm_mix_pre, w_in, w_conv, w_a_out, w_gk, b_gk, w_gla_norm, w_b_out, w_o,
           w_norm_mix_post, w_norm_ffn_pre, w_ffn_gate, w_ffn_up, w_ffn_down, w_norm_ffn_post,
           w_ple_proj, w_ple_gate, w_norm_ple_post):
    f = lambda a: np.ascontiguousarray(np.asarray(a, dtype=np.float32))
    if "nc" not in _NC_CACHE:
        _NC_CACHE["nc"] = build_program()
    nc = _NC_CACHE["nc"]
    shared = {
        "w_pre": f(w_norm_mix_pre[0]), "w_in": f(w_in[0]), "w_conv": f(w_conv[0]), "w_a": f(w_a_out[0]),
        "w_gk": f(w_gk[0]), "b_gk": f(b_gk[0]), "w_gn": f(w_gla_norm[0]), "w_b": f(w_b_out[0]), "w_o": f(w_o[0]),
        "w_mixpost": f(w_norm_mix_post[0]), "w_ffnpre": f(w_norm_ffn_pre[0]), "w_fg": f(w_ffn_gate[0]),
        "w_fu": f(w_ffn_up[0]), "w_fd": f(w_ffn_down[0]), "w_ffnpost": f(w_norm_ffn_post[0]),
        "w_pp": f(w_ple_proj[0]), "w_pg": f(w_ple_gate[0]), "w_plepost": f(w_norm_ple_post[0]),
    }
    x_prompt = np.asarray(x_prompt); x_sample = np.asarray(x_sample)
    state_conv = np.asarray(state_conv); state_gla = np.asarray(state_gla)
    p_prompt = np.asarray(p_prompt); p_sample = np.asarray(p_sample)
    in_maps = []
    for c in range(8):
        m = dict(shared)
        m["xp"] = f(x_prompt[c])
        m["xs"] = f(x_sample[c * NS:(c + 1) * NS, 0, :])
        m["sconv"] = f(state_conv[0, c * NS:(c + 1) * NS])
        m["sgla"] = f(state_gla[0, c * NS:(c + 1) * NS])
        m["pp"] = f(p_prompt[0, c])
        m["psm"] = f(p_sample[0, c * NS:(c + 1) * NS, 0, :])
        in_maps.append(m)
    res = run_bass_kernel_spmd(nc, in_maps, core_ids=list(range(8)))
    r = res.results
    y_prompt = np.stack([r[c]["yp"] for c in range(8)], axis=0).astype(np.float32)
    y_sample = np.concatenate([r[c]["ys"] for c in range(8)], axis=0)[:, None, :].astype(np.float32)
    new_conv_prompt = np.stack([r[c]["ncp"] for c in range(8)], axis=0)[None].astype(np.float32)
    new_gla_prompt = np.stack([r[c]["ngp"] for c in range(8)], axis=0)[None].astype(np.float32)
    new_conv_sample = np.concatenate([r[c]["ncs"] for c in range(8)], axis=0)[None].astype(np.float32)
    new_gla_sample = np.concatenate([r[c]["ngs"] for c in range(8)], axis=0)[None].astype(np.float32)
    return (y_prompt, y_sample, new_conv_prompt, new_gla_prompt, new_conv_sample, new_gla_sample)
```

```python
import contextlib
import numpy as np
import concourse.bass as bass
import concourse.mybir as mybir
from concourse.bass_utils import run_bass_kernel_spmd

F32 = mybir.dt.float32
BF16 = mybir.dt.bfloat16
AF = mybir.ActivationFunctionType
ALU = mybir.AluOpType

ENGS = ("sp", "pe", "act", "dve", "pool")
EPS = 1e-6
NST = 4
TP = 512
NS = 16
DFF = 2816
NFC = DFF // 128
import os
STRICT = os.environ.get("KSTRICT", "0") == "1"


class Buf:
    __slots__ = ("name", "last_w", "readers", "excl")

    def __init__(self, name, excl=False):
        self.name = name
        self.last_w = None
        self.readers = []
        self.excl = excl


class Op:
    __slots__ = ("eng", "fn", "deps", "is_dma", "signal", "sig_val", "sem", "prev_same_sem")

    def __init__(self, eng, fn, is_dma):
        self.eng = eng
        self.fn = fn
        self.is_dma = is_dma
        self.deps = []
        self.signal = is_dma
        self.sig_val = None
        self.sem = None
        self.prev_same_sem = None


class Prog:
    def __init__(self):
        self.ops = {e: [] for e in ENGS}
        self.n_dma_sems = {"sp": 12, "pool": 8}

    def op(self, eng, fn, reads=(), writes=(), dma=False):
        o = Op(eng, fn, dma)
        deps = {}

        def add(w, kind):
            cur = deps.get(id(w))
            if cur is None or kind < cur[1]:
                deps[id(w)] = (w, kind)

        for b in reads:
            if b.last_w is not None:
                add(b.last_w, 0)
            if b.excl:
                for r in b.readers:
                    add(r, 2)
        for b in writes:
            if b.last_w is not None:
                add(b.last_w, 1)
            for r in b.readers:
                add(r, 1)
        for w, kind in deps.values():
            if w is o:
                continue
            if (not w.is_dma) and (not dma) and w.eng == eng:
                if eng == "pe" or kind == 2 or (kind == 1 and not STRICT):
                    continue
            o.deps.append(w)
            w.signal = True
        for b in reads:
            b.readers.append(o)
        for b in writes:
            b.last_w = o
            b.readers = []
        self.ops[eng].append(o)
        return o

    def emit(self, nc, es):
        sems = {e: es.enter_context(nc.semaphore("s_" + e)) for e in ("pe", "act", "dve", "pool")}
        dsems = {q: [es.enter_context(nc.semaphore(f"d_{q}{i}")) for i in range(n)]
                 for q, n in self.n_dma_sems.items()}
        for e in ENGS:
            cnt = 0
            dcnt = 0
            last_on_sem = {}
            for o in self.ops[e]:
                if o.is_dma:
                    pool = dsems[e]
                    k = dcnt % len(pool)
                    o.sem = pool[k]
                    o.sig_val = 16 * (dcnt // len(pool) + 1)
                    o.prev_same_sem = last_on_sem.get(k)
                    last_on_sem[k] = o
                    dcnt += 1
                elif o.signal:
                    cnt += 1
                    o.sem = sems[e]
                    o.sig_val = cnt
        finals = []
        for q in dsems:
            last = {}
            for o in self.ops[q]:
                if o.is_dma:
                    last[id(o.sem)] = o
            finals.extend(last.values())
        block = es.enter_context(nc.Block())

        def run(e, h):
            waited = {}

            def wait(sem, val):
                if waited.get(id(sem), 0) >= val:
                    return
                h.wait_ge(sem, val)
                waited[id(sem)] = val

            for o in self.ops[e]:
                for d in o.deps:
                    wait(d.sem, d.sig_val)
                if o.is_dma and o.prev_same_sem is not None:
                    wait(o.prev_same_sem.sem, o.prev_same_sem.sig_val)
                ins = o.fn(h)
                if o.signal:
                    ins.then_inc(o.sem, 16 if o.is_dma else 1)
            if e == "sp":
                for o in finals:
                    wait(o.sem, o.sig_val)

        @block.sync
        def _(h):
            run("sp", h)

        @block.tensor
        def _(h):
            run("pe", h)

        @block.scalar
        def _(h):
            run("act", h)

        @block.vector
        def _(h):
            run("dve", h)

        @block.gpsimd
        def _(h):
            run("pool", h)


def handoff(src, dst):
    users = []
    for s in src:
        users.extend(s.readers)
        if s.last_w is not None:
            users.append(s.last_w)
    for d in dst:
        d.readers = list(d.readers) + users


SLOTW = 256
NSLOT = 8
LAG = 9


def build_program():
    nc = bass.Bass("TRN2", target_bir_lowering=False)
    P = Prog()
    es = contextlib.ExitStack()

    def din(name, shape):
        return nc.dram_tensor(name, shape, F32, kind="ExternalInput").ap()

    def dout(name, shape):
        return nc.dram_tensor(name, shape, F32, kind="ExternalOutput").ap()

    xp = din("xp", [NST * TP, 1024])
    xs = din("xs", [NS, 1024])
    sconv = din("sconv", [NS, 2, 1024])
    sgla = din("sgla", [NS, 4, 128, 256])
    pp_d = din("pp", [NST * TP, 256])
    ps_d = din("psm", [NS, 256])
    w_pre = din("w_pre", [1024])
    w_in = din("w_in", [1024, 8208])
    w_conv = din("w_conv", [3, 1024])
    w_a = din("w_a", [1024, 1024])
    w_gk = din("w_gk", [16, 512])
    b_gk = din("b_gk", [512])
    w_gn = din("w_gn", [256])
    w_b = din("w_b", [1024, 1024])
    w_o = din("w_o", [1024, 1024])
    w_mixpost = din("w_mixpost", [1024])
    w_ffnpre = din("w_ffnpre", [1024])
    w_fg = din("w_fg", [1024, DFF])
    w_fu = din("w_fu", [1024, DFF])
    w_fd = din("w_fd", [DFF, 1024])
    w_ffnpost = din("w_ffnpost", [1024])
    w_pp = din("w_pp", [256, 1024])
    w_pg = din("w_pg", [1024, 1024])
    w_plepost = din("w_plepost", [1024])
    yp = dout("yp", [NST * TP, 1024])
    ys = dout("ys", [NS, 1024])
    ncp = dout("ncp", [2, 1024])
    ngp = dout("ngp", [4, 128, 256])
    ncs = dout("ncs", [NS, 2, 1024])
    ngs = dout("ngs", [NS, 4, 128, 256])

    def sb(name, shape, dt):
        return es.enter_context(nc.sbuf_tensor(name, shape, dt))

    WM = TP + NS

    ident = sb("ident", [128, 128], BF16)
    identf = sb("identf", [128, 128], F32)
    tri = sb("tri", [128, 128], F32)
    wn_pre = sb("wn_pre", [128, 8], F32)
    wn_ffn = sb("wn_ffn", [128, 8], F32)
    wcv = sb("wcv", [128, 3, 8], F32)
    wgn_bc = sb("wgn_bc", [128, 256], F32)
    wbc = [sb(f"wbc{i}", [128, 1024], F32) for i in range(2)]
    Bwbc = [Buf("wbc0"), Buf("wbc1")]
    wgk = sb("wgk", [32, 512], F32)
    gkl = sb("gkl", [32, WM], F32)
    maskrow = sb("maskrow", [128, NS, 4, NS], BF16)
    uhist = sb("uhist", [128, 8, 2], F32)
    scT = sb("scT", [128, 2, 8, NS], F32)
    utail = sb("utail", [128, 8, 18], F32)
    S = sb("S", [128, 4, 256], F32)
    stat = sb("stat", [128, 16, 4], F32)
    ssg = sb("ssg", [128, 2, 8], F32)
    aS = sb("aS", [128, 4, NS], F32)
    aStmp = sb("aStmp", [128, 4, NS], F32)
    sqk = sb("sqk", [NS, 4], F32)
    QA = sb("QA", [128, 4, NS], BF16)
    BQA = Buf("QA")
    B = {}

    def mk(*names):
        for n in names:
            B[n] = Buf(n)

    mk("ident", "identf", "tri", "wn_pre", "wn_ffn", "wcv", "wgn_bc",
       "wgk", "gkl", "maskrow", "uhist", "scT", "utail", "S", "aS", "aStmp", "ssg0", "ssg1")
    statB = [Buf(f"stat{i}") for i in range(16)]

    ht = sb("ht", [128, 5, 1024], F32)
    hT = sb("hT", [128, 8, WM], BF16)
    Bht = [Buf(f"ht{i}") for i in range(5)]
    BhT = [Buf(f"hT{i}") for i in range(5)]

    regA = sb("regA", [128, 12672], BF16)
    merged_a = regA[:, 0:8448].bitcast(F32).rearrange("p (k c) -> p k c", k=8)
    mergedT = regA[:, 8448:12672].rearrange("p (k c) -> p k c", k=8)
    actT = regA[:, 0:NFC * WM].rearrange("p (k c) -> p k c", k=NFC)
    Bma = [Buf(f"ma{i}") for i in range(8)]
    BmT = [Buf(f"mT{i}") for i in range(8)]
    BactT = [Buf(f"actT{i}") for i in range(NFC)]

    regB = sb("regB", [128, NFC * 1024], BF16)
    o = 0
    goT = regB[:, o:o + 8 * WM].rearrange("p (k c) -> p k c", k=8); o += 8 * WM
    qT = regB[:, o:o + 4 * WM].rearrange("p (k c) -> p k c", k=4); o += 4 * WM
    kT = regB[:, o:o + 4 * WM].rearrange("p (k c) -> p k c", k=4); o += 4 * WM
    v_tok = regB[:, o:o + 5 * 1024].rearrange("p (k c) -> p k c", k=5); o += 5 * 1024
    sg_tok = regB[:, o:o + 5 * 1024].rearrange("p (k c) -> p k c", k=5); o += 5 * 1024
    gtmp = []
    for i in range(5):
        gtmp.append(regB[:, o:o + 512]); o += 512
    qe_a, ke_a, kd_a, scm_a = [t.rearrange("p (h c) -> p h c", h=4) for t in gtmp[0:4]]
    kdt_a = gtmp[4]
    assert o <= NFC * 1024, o
    wdn = regB[:, 0:NFC * 1024].rearrange("p (k c) -> p k c", k=NFC)
    Sbf_t = [sb(f"Sbf{i}", [128, 4, 256], BF16) for i in range(2)]
    Sbf = [t[:, :, :] for t in Sbf_t]
    BgoT = [Buf(f"goT{i}") for i in range(8)]
    BgoTc = [Buf(f"goTc{i}") for i in range(5)]
    Bq, Bk = Buf("qT"), Buf("kT")
    Bv = [Buf(f"v{i}") for i in range(5)]
    Bsg = [Buf(f"sg{i}") for i in range(5)]
    BSbf = [Buf("Sbf0"), Buf("Sbf1")]
    Bqe, Bke, Bkd, Bscm, Bkdt = [Buf(n) for n in ("qe", "ke", "kd", "scm", "kdt")]
    Bwdn = [Buf(f"wdn{i}") for i in range(6)]
    mixB_bufs = BgoT + BgoTc + [Bq, Bk] + Bv + Bsg + [Bqe, Bke, Bkd, Bscm, Bkdt]

    regC = sb("regC", [128, 4608], F32)
    regCc = sb("regCc", [128, 2116], F32)
    cS = regCc[:, 0:528]
    ubuf = [regCc[:, 528:1058], regCc[:, 1058:1588]]
    ycv = regCc[:, 1588:2116]
    e1 = regC[:, 0:512]
    lt = [regC[:, 512:1024], regC[:, 1024:1536]]
    eb_a = regC[:, 1536:2048].rearrange("p (h c) -> p h c", h=4)
    einv_a = regC[:, 2048:2560].rearrange("p (h c) -> p h c", h=4)
    ed_a = regC[:, 2560:3072].rearrange("p (h c) -> p h c", h=4)
    onb = [regC[:, 3072:3328], regC[:, 3328:3584]]
    oS = regC[0:NS, 3584:4608]
    BcS, Bu, Byc = Buf("cS"), [Buf("u0"), Buf("u1")], Buf("yc")
    Be1, Blt = Buf("e1"), [Buf("lt0"), Buf("lt1")]
    Beb, Beinv, Bed = Buf("eb"), Buf("einv"), Buf("ed")
    Bon = [Buf("on0"), Buf("on1")]
    BoS = Buf("oS")
    convC = [BcS, Byc] + Bu
    glaC = [Be1] + Blt + [Beb, Beinv, Bed] + Bon

    NBT = 2
    bigt = [sb(f"bigt{i}", [128, 1024], F32) for i in range(NBT)]
    Bbig = [Buf(f"bigt{i}") for i in range(NBT)]
    junk = sb("junk", [128, 1024], BF16)
    Bjunk = Buf("junk")
    SSbf = junk[:, :].rearrange("p (h c) -> p h c", h=4)
    hs = [sb(f"hs{i}", [128, 1024], BF16) for i in range(2)]
    Bhs = [Buf("hs0"), Buf("hs1")]
    ogt, Bogt = hs[0], Bhs[0]
    kS_tok = hs[1][0:NS, 0:512]
    KSj = hs[1][0:NS, 512:1024]
    BkS, BKSj = Buf("kS"), Buf("KSj")
    SS = [sb(f"SS{i}", [128, 4, 256], F32) for i in range(2)]
    BSS = [Buf("SS0"), Buf("SS1")]
    QmJ = [sb(f"QmJ{i}", [128, 4, NS], BF16) for i in range(2)]
    BQmJ = [Buf("QmJ0"), Buf("QmJ1")]
    pb16 = [sb(f"pb16{i}", [128, 256], BF16) for i in range(2)]
    Bpb16 = [Buf("pb160"), Buf("pb161")]
    pT = sb("pT", [128, 2, WM], BF16)
    BpT = [Buf(f"pT{i}") for i in range(5)]

    wslot = [sb(f"wslot{i}", [128, 8, SLOTW], BF16) for i in range(NSLOT)]
    Bws = [Buf(f"wslot{i}") for i in range(NSLOT)]

    psum = es.enter_context(nc.psum_tensor("psum", [128, 4096], F32))
    Bps = [Buf(f"psb{i}", excl=True) for i in range(8)]
    ps_state = {"next": 0, "lo": 0, "hi": 8}

    def ps_alloc(n):
        p = ps_state["next"]
        lo, hi = ps_state["lo"], ps_state["hi"]
        if p < lo or p >= hi:
            p = lo
        if n == 2 and (p % 2):
            p += 1
        if p + n > hi:
            p = lo
        ps_state["next"] = p + n
        return p * 512, Bps[p:p + n]

    def G(b, n):
        return b * 512, Bps[b:b + n]

    big_state = {"next": 0}

    def big_alloc():
        i = big_state["next"]
        big_state["next"] = (i + 1) % NBT
        return bigt[i], Bbig[i]

    stat_state = {"next": 0}

    def stat_alloc():
        i = stat_state["next"]
        stat_state["next"] = (i + 1) % 16
        return stat[:, i, :], statB[i]

    def ACT(out, in_, func, reads, writes, scale=None, bias=None, accum=None):
        kw = {}
        if scale is not None:
            kw["scale"] = scale
        if bias is not None:
            kw["bias"] = bias
        if accum is not None:
            kw["accum_out"] = accum
        P.op("act", lambda h: h.activation(out=out, in_=in_, func=func, **kw), reads, writes)

    def TT(eng, out, in0, in1, op, reads, writes):
        P.op(eng, lambda h: h.tensor_tensor(out=out, in0=in0, in1=in1, op=op), reads, writes)

    def TS(eng, out, in0, s1, op0, reads, writes):
        P.op(eng, lambda h: h.tensor_scalar(out=out, in0=in0, scalar1=s1, scalar2=None, op0=op0), reads, writes)

    def STT(eng, out, in0, scalar, in1, op0, op1, reads, writes):
        P.op(eng, lambda h: h.scalar_tensor_tensor(out=out, in0=in0, scalar=scalar, in1=in1, op0=op0, op1=op1), reads, writes)

    def CP(eng, out, in_, reads, writes):
        P.op(eng, lambda h: h.tensor_copy(out=out, in_=in_), reads, writes)

    def MS(eng, ap, val, writes):
        P.op(eng, lambda h: h.memset(ap, val), (), writes)

    def MM(lst, reads, writes):
        def fn(h):
            ins = None
            for (out, lhsT, rhs, start, stop) in lst:
                ins = h.matmul(out, lhsT=lhsT, rhs=rhs, start=start, stop=stop)
            return ins
        P.op("pe", fn, reads, writes)

    def TR(lst, reads, writes):
        def fn(h):
            ins = None
            for (out, in_, idn) in lst:
                ins = h.transpose(out=out, in_=in_, identity=idn)
            return ins
        P.op("pe", fn, reads, writes)

    def DMA(q, out, in_, reads, writes, noncontig=False):
        def fn(h):
            if noncontig:
                with nc.allow_non_contiguous_dma(reason="small strided constant load"):
                    return h.dma_start(out=out, in_=in_)
            return h.dma_start(out=out, in_=in_)
        P.op(q, fn, reads, writes, dma=True)

    MS("pool", identf[:], 1.0, [B["identf"]])
    P.op("pool", lambda h: h.affine_select(out=identf[:], in_=identf[:], pattern=[[-1, 128]], compare_op=ALU.is_equal,
                                           fill=0.0, base=0, channel_multiplier=1), [B["identf"]], [B["identf"]])
    CP("dve", ident[:], identf[:], [B["identf"]], [B["ident"]])
    MS("pool", tri[:], 1.0, [B["tri"]])
    P.op("pool", lambda h: h.affine_select(out=tri[:], in_=tri[:], pattern=[[1, 128]], compare_op=ALU.is_ge,
                                           fill=0.0, base=0, channel_multiplier=-1), [B["tri"]], [B["tri"]])
    for tt in range(4):
        DMA("sp", ht[:, tt, :], xp[tt * 128:(tt + 1) * 128, :], [], [Bht[tt]])
    DMA("sp", wn_pre[:], w_pre.rearrange("(kc p) -> p kc", p=128), [], [B["wn_pre"]], noncontig=True)
    DMA("sp", wn_ffn[:], w_ffnpre.rearrange("(kc p) -> p kc", p=128), [], [B["wn_ffn"]], noncontig=True)
    DMA("sp", wcv[:], w_conv.rearrange("j (kc p) -> p j kc", p=128), [], [B["wcv"]], noncontig=True)
    DMA("sp", wgn_bc[:], w_gn.partition_broadcast(128), [], [B["wgn_bc"]])
    DMA("sp", wgk[0:16, :], w_gk, [], [B["wgk"]])
    DMA("sp", wgk[16:17, :], b_gk.rearrange("(o n) -> o n", o=1), [], [B["wgk"]])
    MS("dve", gkl[:], 1.0, [B["gkl"]])
    MS("dve", uhist[:], 0.0, [B["uhist"]])
    MS("dve", S[:], 0.0, [B["S"]])
    MS("dve", Sbf[0], 0.0, [BSbf[0]])
    MS("dve", stat[:], 1.0, statB)
    MS("dve", maskrow[:], 0.0, [B["maskrow"]])
    for j in range(NS):
        MS("dve", maskrow[:, j, :, j:j + 1], 1.0, [B["maskrow"]])
    for t in range(2):
        bt, bb = big_alloc()
        DMA("sp", bt[0:NS, :], sconv[:, t, :], [], [bb])
        if t == 1:
            DMA("sp", ncs[:, 0, :], bt[0:NS, :], [bb], [])
        base, pbs = ps_alloc(1)
        TR([(psum[:, base + kc * NS:base + (kc + 1) * NS], bt[0:NS, kc * 128:(kc + 1) * 128], identf[0:NS, 0:NS])
            for kc in range(8)], [bb, B["identf"]], pbs)
        ACT(scT[:, t, :, :], psum[:, base:base + 8 * NS].rearrange("p (k c) -> p k c", k=8), AF.Copy, pbs, [B["scT"]])

    wbc_state = {"i": 0}

    def wbc_load(src):
        i = wbc_state["i"]
        wbc_state["i"] = 1 - i
        DMA("sp", wbc[i][:], src.partition_broadcast(128), [], [Bwbc[i]])
        return wbc[i], Bwbc[i]

    def build_groups():
        g = []
        for st in range(NST):
            for i in range(2):
                g.append(("q", w_in, 0, 8, 3072 + i * SLOTW, SLOTW))
            for i in range(2):
                g.append(("k", w_in, 0, 8, 3584 + i * SLOTW, SLOTW))
            g.append(("gklr", w_in, 0, 8, 6144, 16))
            for i in range(4):
                g.append(("v", w_in, 0, 8, 4096 + i * SLOTW, SLOTW))
            for i in range(4):
                g.append(("g", w_in, 0, 8, 5120 + i * SLOTW, SLOTW))
            for qd in range(4):
                g.append(("c", w_in, 0, 8, 1024 + qd * SLOTW, SLOTW))
                g.append(("x", w_in, 0, 8, 2048 + qd * SLOTW, SLOTW))
                g.append(("b", w_in, 0, 8, 0 + qd * SLOTW, SLOTW))
            for qd in range(4):
                g.append(("wa", w_a, 0, 8, qd * SLOTW, SLOTW))
                g.append(("ga", w_in, 0, 8, 6160 + qd * SLOTW, SLOTW))
            for qd in range(4):
                g.append(("wb", w_b, 0, 8, qd * SLOTW, SLOTW))
                g.append(("gb", w_in, 0, 8, 7184 + qd * SLOTW, SLOTW))
            for i in range(4):
                g.append(("wo", w_o, 0, 8, i * SLOTW, SLOTW))
            for gi in range(11):
                g.append(("fg", w_fg, 0, 8, gi * SLOTW, SLOTW))
                g.append(("fu", w_fu, 0, 8, gi * SLOTW, SLOTW))
                if gi < 6:
                    k0 = (gi // 2) * 8
                    nk = min(8, NFC - k0)
                    g.append(("WD", w_fd, k0, nk, (gi % 2) * 512, 512))
            g.append(("pp", w_pp, 0, 2, 0, 1024))
            for i in range(4):
                g.append(("pg", w_pg, 0, 8, i * SLOTW, SLOTW))
        return g

    groups = build_groups()
    GPS = len(groups) // NST
    ring_idx = {}
    wd_idx = {}
    for li in range(GPS):
        if groups[li][0] == "WD":
            wd_idx[li] = len(wd_idx)
        else:
            ring_idx[li] = len(ring_idx)
    wscr = nc.dram_tensor("wscr", [len(ring_idx), 128, 8 * SLOTW], BF16).ap()
    wdscr = nc.dram_tensor("wdscr", [6, 128, 8 * 512], BF16).ap()
    Bscr = [Buf(f"scr{i}") for i in range(len(ring_idx))]
    Bwdscr = [Buf(f"wdscr{i}") for i in range(6)]
    WS = {"next_load": 0, "next_use": 0, "free": list(range(NSLOT)), "slot_of": {}, "wd_ok": False, "wd_idx": 0}

    def w_prefetch():
        while WS["next_load"] < len(groups):
            gi = WS["next_load"]
            name, mat, k0, nk, c0, nco = groups[gi]
            st_, li = gi // GPS, gi % GPS
            src = mat[k0 * 128:(k0 + nk) * 128, c0:c0 + nco].rearrange("(kc p) n -> p kc n", p=128)
            if name == "WD":
                if not WS["wd_ok"]:
                    return
                piece = wd_idx[li]
                dst = wdn[:, k0:k0 + nk, c0:c0 + nco]
                scr = wdscr[piece][:, 0:nk * 512].rearrange("p (k c) -> p k c", k=nk)
                conv_st = piece % 2
                if st_ <= conv_st:
                    DMA("pool", dst, src, [], [Bwdn[piece]])
                    if st_ == conv_st:
                        DMA("sp", scr, dst, [Bwdn[piece]], [Bwdscr[piece]])
                else:
                    DMA("pool", dst, scr, [Bwdscr[piece]], [Bwdn[piece]])
                WS["next_load"] += 1
                continue
            if not WS["free"]:
                return
            s = WS["free"].pop(0)
            ri = ring_idx[li]
            img = wslot[s][:, :, :].rearrange("p k c -> p (k c)")
            conv_st = ri % 2
            if st_ <= conv_st:
                if name == "pp":
                    dst = wslot[s][:, :, :].rearrange("p (a b) n -> p a (b n)", a=2)
                else:
                    dst = wslot[s][:, 0:nk, 0:nco]
                if name == "gklr":
                    MS("pool", wslot[s][:, :, :], 0.0, [Bws[s]])
                DMA("pool", dst, src, [], [Bws[s]])
                if st_ == conv_st:
                    DMA("sp", wscr[ri], img, [Bws[s]], [Bscr[ri]])
            else:
                DMA("pool", img, wscr[ri], [Bscr[ri]], [Bws[s]])
            WS["slot_of"][gi] = s
            WS["next_load"] += 1

    def w_acquire(name):
        while groups[WS["next_use"]][0] == "WD":
            WS["next_use"] += 1
        gi = WS["next_use"]
        assert groups[gi][0] == name, (groups[gi][0], name)
        if gi not in WS["slot_of"]:
            w_prefetch()
        assert gi in WS["slot_of"], ("weight ring deadlock", name, gi)
        WS["next_use"] += 1
        return WS["slot_of"][gi]

    def w_release(*slots):
        for s in slots:
            WS["free"].append(s)
        w_prefetch()

    def mm_fm(base, pbs, slot, j, src, srcB, W, M=128):
        lst = []
        blocks = [(0, TP)] + ([(TP, W - TP)] if W > TP else [])
        for (c0, cn) in blocks:
            for kc in range(8):
                lst.append((psum[0:M, base + c0:base + c0 + cn], wslot[slot][:, kc, j * 128:j * 128 + M],
                            src[:, kc, c0:c0 + cn], kc == 0, kc == 7))
        MM(lst, [Bws[slot]] + srcB, pbs)

    def rstd_from(ssap, sB, R, n):
        ACT(ssap[0:R, 1:2], ssap[0:R, 0:1], AF.Ln, [sB], [sB], scale=1.0 / n, bias=EPS)
        ACT(ssap[0:R, 2:3], ssap[0:R, 1:2], AF.Exp, [sB], [sB], scale=-0.5)

    def fm_prep(ti, tt, R, norm=True):
        par = ti % 2
        if norm:
            sa, sB = stat_alloc()
            ACT(junk[0:R, :], ht[0:R, tt, :], AF.Square, [Bht[tt]], [Bjunk, sB], accum=sa[0:R, 0:1])
            rstd_from(sa, sB, R, 1024)
            TS("dve", hs[par][0:R, :], ht[0:R, tt, :], sa[0:R, 2:3], ALU.mult, [Bht[tt], sB], [Bhs[par]])
        else:
            CP("dve", hs[par][0:R, :], ht[0:R, tt, :], [Bht[tt]], [Bhs[par]])

    def fm_tr(ti, tt, col0, R, wn, wnB):
        par = ti % 2
        base, pbs = ps_alloc(1)
        pv = psum[:, base:base + 512].bitcast(BF16).rearrange("p (k c) -> p k c", k=8)
        TR([(pv[:, kc, 0:R], hs[par][0:R, kc * 128:(kc + 1) * 128], ident[0:R, 0:R]) for kc in range(8)],
           [Bhs[par], B["ident"]], pbs)
        if wn is not None:
            TT("dve", hT[:, :, col0:col0 + R], pv[:, :, 0:R], wn[:, :].unsqueeze(2).to_broadcast([128, 8, R]),
               ALU.mult, pbs + [wnB], [BhT[tt]])
        else:
            ACT(hT[:, :, col0:col0 + R], pv[:, :, 0:R], AF.Copy, pbs, [BhT[tt]])

    def fm_tile(ti, tt, col0, R, wn, wnB, norm=True):
        fm_prep(ti, tt, R, norm)
        fm_tr(ti, tt, col0, R, wn, wnB)

    def norm_resid(base, pbs, R, tt, wb_, wbB):
        src = psum[0:R, base:base + 1024]
        sa, sB = stat_alloc()
        ACT(junk[0:R, :], src, AF.Square, pbs, [Bjunk, sB], accum=sa[0:R, 0:1])
        rstd_from(sa, sB, R, 1024)
        bt, bb = big_alloc()
        TT("dve", bt[0:R, :], src, wb_[0:R, :], ALU.mult, pbs + [wbB], [bb])
        STT("dve", ht[0:R, tt, :], bt[0:R, :], sa[0:R, 2:3], ht[0:R, tt, :], ALU.mult, ALU.add, [bb, sB, Bht[tt]], [Bht[tt]])

    preA = set()
    for st in range(NST):
        last = st == NST - 1
        W = WM if last else TP
        nbW = 2 if last else 1
        tiles = [(tt, tt * 128, 128) for tt in range(4)] + ([(4, TP, NS)] if last else [])
        allhT = [BhT[t[0]] for t in tiles]

        if st > 0:
            handoff(Bwdn, mixB_bufs)
            handoff(BactT, Bma + BmT)
            WS["wd_ok"] = False

        if st == 0:
            DMA("sp", ht[0:NS, 4, :], xs, [], [Bht[4]])
        w_prefetch()
        for ti, (tt, col0, R) in enumerate(tiles):
            if (st, tt) in preA:
                fm_tr(ti, tt, col0, R, wn_pre, B["wn_pre"])
            else:
                fm_tile(ti, tt, col0, R, wn_pre, B["wn_pre"])
        wb_mix, Bwb_mix = wbc_load(w_mixpost)

        for (nm, dstT, dB, scl) in (("q", qT, Bq, float(128 ** -0.5)), ("k", kT, Bk, None)):
            for i in range(2):
                s_ = w_acquire(nm)
                for j in range(2):
                    h_ = i * 2 + j
                    b_, p_ = ps_alloc(nbW)
                    mm_fm(b_, p_, s_, j, hT, allhT, W)
                    ACT(dstT[:, h_, 0:W], psum[:, b_:b_ + W], AF.Copy, p_, [dB], scale=scl)
                w_release(s_)
        sg_ = w_acquire("gklr")
        b_, p_ = ps_alloc(nbW)
        mm_fm(b_, p_, sg_, 0, hT, allhT, W, M=16)
        ACT(gkl[0:16, 0:W], psum[0:16, b_:b_ + W], AF.Copy, p_, [B["gkl"]])
        w_release(sg_)
        for (nm, dstt, dB, fn) in (("v", v_tok, Bv, AF.Copy), ("g", sg_tok, Bsg, AF.Silu)):
            for half in range(2):
                s0 = w_acquire(nm)
                s1 = w_acquire(nm)
                for (tt, col0, R) in tiles:
                    b_, p_ = ps_alloc(1)
                    lst = []
                    for i, s_ in enumerate((s0, s1)):
                        for kc in range(8):
                            lst.append((psum[0:R, b_ + i * SLOTW:b_ + (i + 1) * SLOTW], hT[:, kc, col0:col0 + R],
                                        wslot[s_][:, kc, :], kc == 0, kc == 7))
                    MM(lst, [BhT[tt], Bws[s0], Bws[s1]], p_)
                    ACT(dstt[0:R, tt, half * 512:(half + 1) * 512], psum[0:R, b_:b_ + 512], fn, p_, [dB[tt]])
                w_release(s0, s1)


        def gla_post(src_h, srcB, R, tt, col0, par):
            sg_ap = ssg[:, par, :]
            sgB = B[f"ssg{par}"]
            for h_ in range(4):
                ACT(junk[0:R, 0:256], src_h(h_), AF.Square, srcB, [Bjunk, sgB], accum=sg_ap[0:R, h_:h_ + 1])
            ACT(sg_ap[0:R, 4:8], sg_ap[0:R, 0:4], AF.Ln, [sgB], [sgB], scale=1.0 / 256, bias=EPS)
            ACT(sg_ap[0:R, 4:8], sg_ap[0:R, 4:8], AF.Exp, [sgB], [sgB], scale=-0.5)
            for h_ in range(4):
                hp = h_ % 2
                STT("dve", onb[hp][0:R, :], src_h(h_), sg_ap[0:R, 4 + h_:5 + h_],
                    wgn_bc[0:R, :], ALU.mult, ALU.mult, srcB + [sgB, B["wgn_bc"]], [Bon[hp]])
                TT("dve", ogt[0:R, h_ * 256:(h_ + 1) * 256], onb[hp][0:R, :], sg_tok[0:R, tt, h_ * 256:(h_ + 1) * 256],
                   ALU.mult, [Bon[hp], Bsg[tt]], [Bogt])
            base, pbs = G(0, 1)
            pv = psum[:, base:base + 512].bitcast(BF16).rearrange("p (k c) -> p k c", k=8)
            TR([(pv[:, kc, 0:R], ogt[0:R, kc * 128:(kc + 1) * 128], ident[0:R, 0:R]) for kc in range(8)],
               [Bogt, B["ident"]], pbs)
            ACT(goT[:, :, col0:col0 + R], pv[:, :, 0:R], AF.Copy, pbs, [BgoTc[tt]] + BgoT)

        def sample_prep():
            bgk, pgk = G(0, 1)
            MM([(psum[:, bgk + h_ * NS:bgk + (h_ + 1) * NS], wgk[0:17, h_ * 128:(h_ + 1) * 128], gkl[0:17, TP:TP + NS], True, True)
                for h_ in range(4)], [B["gkl"], B["wgk"]], pgk)
            pgv = psum[:, bgk:bgk + 4 * NS].rearrange("p (k c) -> p k c", k=4)
            ACT(aStmp[:, :, :], pgv, AF.Exp, pgk, [B["aStmp"]], scale=-1.0)
            ACT(aStmp[:, :, :], aStmp[:, :, :], AF.Ln, [B["aStmp"]], [B["aStmp"]], bias=1.0)
            ACT(aS[:, :, :], aStmp[:, :, :], AF.Exp, [B["aStmp"]], [B["aS"]], scale=-1.0 / 16)
            TT("dve", QA[:, :, :], qT[:, :, TP:TP + NS], aS[:, :, :], ALU.mult, [Bq, B["aS"]], [BQA])
            bkt, pkt = G(1, 1)
            pktv = psum[:, bkt:bkt + 512].bitcast(BF16)
            TR([(pktv[0:NS, h_ * 128:(h_ + 1) * 128], kT[:, h_, TP:TP + NS], ident[:, :]) for h_ in range(4)] +
               [(pktv[0:NS, 512 + h_ * 128:512 + (h_ + 1) * 128], qT[:, h_, TP:TP + NS], ident[:, :]) for h_ in range(4)],
               [Bk, Bq, B["ident"]], pkt)
            ACT(kS_tok, pktv[0:NS, 0:512], AF.Copy, pkt, [BkS])
            bt, bb = big_alloc()
            TT("dve", bt[0:NS, 0:512], pktv[0:NS, 512:1024], kS_tok, ALU.mult, pkt + [BkS], [bb])
            for h_ in range(4):
                ACT(junk[0:NS, 0:128], bt[0:NS, h_ * 128:(h_ + 1) * 128], AF.Copy, [bb], [Bjunk, B["aStmp"]],
                    accum=sqk[0:NS, h_:h_ + 1])
            for h_ in range(4):
                TS("dve", oS[:, h_ * 256:(h_ + 1) * 256], v_tok[0:NS, 4, h_ * 256:(h_ + 1) * 256], sqk[0:NS, h_:h_ + 1],
                   ALU.mult, [Bv[4], B["aStmp"]], [BoS])

        def sample_load(j):
            DMA("sp", SS[j % 2][:, :, :], sgla[j].rearrange("h k v -> k h v"), [], [BSS[j % 2]])

        def sample_iter(j):
            jp = j % 2
            if j == 0:
                sample_load(0)
            if j + 1 < NS:
                sample_load(j + 1)
            ACT(SSbf, SS[jp][:, :, :], AF.Copy, [BSS[jp]], [Bjunk])
            TS("dve", KSj, kS_tok, identf[0:NS, j:j + 1], ALU.mult, [BkS, B["identf"]], [BKSj])
            TT("dve", QmJ[jp][:, :, :], QA[:, :, :], maskrow[:, j, :, :], ALU.mult, [BQA, B["maskrow"]], [BQmJ[jp]])
            bkv, pkv = G(0, 2)
            MM([(psum[:, bkv + h_ * 256:bkv + (h_ + 1) * 256], KSj[:, h_ * 128:(h_ + 1) * 128],
                 v_tok[0:NS, 4, h_ * 256:(h_ + 1) * 256], True, True) for h_ in range(4)], [BKSj, Bv[4]], pkv)
            bo, pbo = G(2, 2)
            MM([(psum[0:NS, bo + h_ * 256:bo + (h_ + 1) * 256], QmJ[jp][:, h_, :], SSbf[:, h_, :], True, True)
                for h_ in range(4)], [BQmJ[jp], Bjunk], pbo)
            for h_ in range(4):
                STT("dve", SS[jp][:, h_, :], SS[jp][:, h_, :], aS[:, h_, j:j + 1], psum[:, bkv + h_ * 256:bkv + (h_ + 1) * 256],
                    ALU.mult, ALU.add, [BSS[jp], B["aS"]] + pkv, [BSS[jp]])
            DMA("sp", ngs[j].rearrange("h k v -> k h v"), SS[jp][:, :, :], [BSS[jp]], [])
            TT("dve", oS, oS, psum[0:NS, bo:bo + 1024], ALU.add, pbo + [BoS], [BoS])

        def conv_gen():
            for qd in range(4):
                sc = w_acquire("c")
                sx = w_acquire("x")
                sbk = w_acquire("b")
                for j in range(2):
                    kc = qd * 2 + j
                    par = kc % 2
                    u = ubuf[par]
                    bc, pc = ps_alloc(nbW)
                    mm_fm(bc, pc, sc, j, hT, allhT, W)
                    ACT(cS[:, 0:W], psum[:, bc:bc + W], AF.Copy, pc, [BcS])
                    bx, px = ps_alloc(nbW)
                    mm_fm(bx, px, sx, j, hT, allhT, W)
                    TT("dve", u[:, 2:2 + W], psum[:, bx:bx + W], cS[:, 0:W], ALU.mult, px + [BcS], [Bu[par]])
                    CP("dve", u[:, 0:2], uhist[:, kc, :], [B["uhist"]], [Bu[par]])
                    CP("dve", uhist[:, kc, :], u[:, TP:TP + 2], [Bu[par]], [B["uhist"]])
                    TS("dve", ycv[:, 0:TP], u[:, 0:TP], wcv[:, 0, kc:kc + 1], ALU.mult, [Bu[par], B["wcv"]], [Byc])
                    STT("dve", ycv[:, 0:TP], u[:, 1:TP + 1], wcv[:, 1, kc:kc + 1], ycv[:, 0:TP], ALU.mult, ALU.add,
                        [Bu[par], B["wcv"], Byc], [Byc])
                    STT("dve", ycv[:, 0:TP], u[:, 2:TP + 2], wcv[:, 2, kc:kc + 1], ycv[:, 0:TP], ALU.mult, ALU.add,
                        [Bu[par], B["wcv"], Byc], [Byc])
                    if last:
                        TS("dve", ycv[:, TP:W], scT[:, 0, kc, :], wcv[:, 0, kc:kc + 1], ALU.mult, [B["scT"], B["wcv"]], [Byc])
                        STT("dve", ycv[:, TP:W], scT[:, 1, kc, :], wcv[:, 1, kc:kc + 1], ycv[:, TP:W], ALU.mult, ALU.add,
                            [B["scT"], B["wcv"], Byc], [Byc])
                        STT("dve", ycv[:, TP:W], u[:, TP + 2:W + 2], wcv[:, 2, kc:kc + 1], ycv[:, TP:W], ALU.mult, ALU.add,
                            [Bu[par], B["wcv"], Byc], [Byc])
                        CP("dve", utail[:, kc, :], u[:, TP:TP + 18], [Bu[par]], [B["utail"]])
                    bb_, pb_ = ps_alloc(nbW)
                    mm_fm(bb_, pb_, sbk, j, hT, allhT, W)
                    TT("dve", mergedT[:, kc, 0:W], psum[:, bb_:bb_ + W], ycv[:, 0:W], ALU.mult, pb_ + [Byc], [BmT[kc]])
                    yield
                w_release(sc, sx, sbk)
            if last:
                base, pbs = ps_alloc(2)
                TR([(psum[0:18, base + kc * 128:base + (kc + 1) * 128], utail[:, kc, :], identf[:, :]) for kc in range(8)],
                   [B["utail"], B["identf"]], pbs)
                bt, bb = big_alloc()
                ACT(bt[0:18, :], psum[0:18, base:base + 1024], AF.Copy, pbs, [bb])
                DMA("sp", ncp[:, :], bt[0:2, :], [bb], [])
                DMA("sp", ncs[:, 1, :], bt[2:18, :], [bb], [])

            for qd in range(4):
                swa = w_acquire("wa")
                sga = w_acquire("ga")
                for j in range(2):
                    kc = qd * 2 + j
                    by, py = ps_alloc(nbW)
                    mm_fm(by, py, swa, j, mergedT, BmT, W)
                    bg, pg = ps_alloc(nbW)
                    mm_fm(bg, pg, sga, j, hT, allhT, W)
                    bt, bb = big_alloc()
                    ACT(bt[:, 0:W], psum[:, bg:bg + W], AF.Tanh, pg, [bb], scale=0.5)
                    ACT(bt[:, 0:W], bt[:, 0:W], AF.Identity, [bb], [bb], scale=0.5, bias=0.5)
                    TT("dve", merged_a[:, kc, 0:W], psum[:, by:by + W], bt[:, 0:W], ALU.mult, py + [bb], [Bma[kc]])
                    yield
                w_release(swa, sga)


        def gla_gen():
            if last:
                sample_prep()
                yield
            def stage1(c):
                par_ = (st * 4 + c) % 2
                cols_ = slice(c * 128, (c + 1) * 128)
                bgk, pgk = G(0, 1)
                MM([(psum[:, bgk + q_ * 128:bgk + (q_ + 1) * 128], gkl[0:17, cols_], wgk[0:17, q_ * 128:(q_ + 1) * 128], True, True)
                    for q_ in range(4)], [B["gkl"], B["wgk"]], pgk)
                ACT(e1, psum[:, bgk:bgk + 512], AF.Exp, pgk, [Be1], scale=-1.0)
                ACT(lt[par_], e1, AF.Ln, [Be1], [Blt[par_]], bias=1.0)

            stage1(0)
            yield
            for c in range(4):
                cidx = st * 4 + c
                par = cidx % 2
                cols = slice(c * 128, (c + 1) * 128)
                bpb, ppb = G(1, 1)
                MM([(psum[:, bpb + h_ * 128:bpb + (h_ + 1) * 128], lt[par][:, h_ * 128:(h_ + 1) * 128], tri[:, :], True, True)
                    for h_ in range(4)], [Blt[par], B["tri"]], ppb)
                pbv = psum[:, bpb:bpb + 512].rearrange("p (h c) -> p h c", h=4)
                ACT(eb_a, pbv, AF.Exp, ppb, [Beb], scale=-1.0 / 16)
                ACT(einv_a, pbv, AF.Exp, ppb, [Beinv], scale=1.0 / 16)
                TT("dve", ed_a, einv_a, eb_a[:, :, 127:128].to_broadcast([128, 4, 128]), ALU.mult, [Beinv, Beb], [Bed])
                TT("dve", qe_a, qT[:, :, cols], eb_a, ALU.mult, [Bq, Beb], [Bqe])
                TT("dve", ke_a, kT[:, :, cols], einv_a, ALU.mult, [Bk, Beinv], [Bke])
                TT("dve", kd_a, kT[:, :, cols], ed_a, ALU.mult, [Bk, Bed], [Bkd])
                if c + 1 < 4:
                    stage1(c + 1)
                yield
                bkd, pkd = G(0, 1)
                pkdv = psum[:, bkd:bkd + 512].bitcast(BF16)
                TR([(pkdv[:, h_ * 128:(h_ + 1) * 128], kd_a[:, h_, :], ident[:, :]) for h_ in range(4)], [Bkd, B["ident"]], pkd)
                ACT(kdt_a, pkdv[:, 0:512], AF.Copy, pkd, [Bkdt])
                bsc, psc_ = G(1, 1)
                MM([(psum[:, bsc + h_ * 128:bsc + (h_ + 1) * 128], ke_a[:, h_, :], qe_a[:, h_, :], True, True) for h_ in range(4)],
                   [Bke, Bqe], psc_)
                TT("dve", scm_a, psum[:, bsc:bsc + 512].rearrange("p (h c) -> p h c", h=4),
                   tri[:, :].unsqueeze(1).to_broadcast([128, 4, 128]), ALU.mult, psc_ + [B["tri"]], [Bscm])
                yield
                po, ppo = G(2, 2)
                lst = []
                for h_ in range(4):
                    vh = v_tok[:, c, h_ * 256:(h_ + 1) * 256]
                    lst.append((psum[:, po + h_ * 256:po + (h_ + 1) * 256], scm_a[:, h_, :], vh, True, False))
                    lst.append((psum[:, po + h_ * 256:po + (h_ + 1) * 256], qe_a[:, h_, :], Sbf[par][:, h_, :], False, True))
                MM(lst, [Bscm, Bv[c], Bqe, BSbf[par]], ppo)
                bD, pD_ = G(0, 2)
                MM([(psum[:, bD + h_ * 256:bD + (h_ + 1) * 256], kdt_a[:, h_ * 128:(h_ + 1) * 128],
                     v_tok[:, c, h_ * 256:(h_ + 1) * 256], True, True) for h_ in range(4)], [Bkdt, Bv[c]], pD_)
                for h_ in range(4):
                    STT("dve", S[:, h_, :], S[:, h_, :], eb_a[:, h_, 127:128], psum[:, bD + h_ * 256:bD + (h_ + 1) * 256],
                        ALU.mult, ALU.add, [B["S"], Beb] + pD_, [B["S"]])
                ACT(Sbf[1 - par], S[:, :, :], AF.Copy, [B["S"]], [BSbf[1 - par]])
                yield
                gla_post(lambda h_, po=po: psum[:, po + h_ * 256:po + (h_ + 1) * 256], ppo, 128, c, c * 128, par)
                yield
                if last:
                    for j in range(c * 4, c * 4 + 4):
                        sample_iter(j)
                        yield
            if last:
                DMA("sp", ngp.rearrange("h k v -> k h v"), S[:, :, :], [B["S"]], [])
                gla_post(lambda h_: oS[:, h_ * 256:(h_ + 1) * 256], [BoS], NS, 4, TP, 0)


        ps_state["lo"], ps_state["hi"] = 4, 8
        g_gla, g_conv = gla_gen(), conv_gen()
        n_gla = 17 + (17 if last else 0)
        n_conv = 16
        done_g = 0
        for i_ in range(n_conv):
            while done_g * n_conv < (i_ + 1) * n_gla:
                if next(g_gla, "end") == "end":
                    break
                done_g += 1
            next(g_conv, None)
        for _ in g_gla:
            pass
        for _ in g_conv:
            pass
        ps_state["lo"], ps_state["hi"] = 0, 8

        for qd in range(4):
            swb = w_acquire("wb")
            sgb = w_acquire("gb")
            for j in range(2):
                kc = qd * 2 + j
                by, py = ps_alloc(nbW)
                mm_fm(by, py, swb, j, goT, BgoT + BgoTc[0:len(tiles)], W)
                bg, pg = ps_alloc(nbW)
                mm_fm(bg, pg, sgb, j, hT, allhT, W)
                bt, bb = big_alloc()
                ACT(bt[:, 0:W], psum[:, bg:bg + W], AF.Sigmoid, pg, [bb])
                TT("dve", bt[:, 0:W], psum[:, by:by + W], bt[:, 0:W], ALU.mult, py + [bb], [bb])
                TT("dve", mergedT[:, kc, 0:W], bt[:, 0:W], merged_a[:, kc, 0:W], ALU.add, [bb, Bma[kc]], [BmT[kc]])
            w_release(swb, sgb)
        handoff(mixB_bufs, Bwdn)
        WS["wd_ok"] = True
        wb_ffn, Bwb_ffn = wbc_load(w_ffnpost)

        so = [w_acquire("wo") for _ in range(4)]
        for idx, (tt, col0, R) in enumerate(tiles):
            b_, p_ = ps_alloc(2)
            lst = []
            for i, s_ in enumerate(so):
                for kc in range(8):
                    lst.append((psum[0:R, b_ + i * SLOTW:b_ + (i + 1) * SLOTW], mergedT[:, kc, col0:col0 + R],
                                wslot[s_][:, kc, :], kc == 0, kc == 7))
            MM(lst, BmT + [Bws[s_] for s_ in so], p_)
            norm_resid(b_, p_, R, tt, wb_mix, Bwb_mix)
            if idx >= LAG:
                fm_tile(idx - LAG, *tiles[idx - LAG], wn_ffn, B["wn_ffn"])
        for i_ in range(max(0, len(tiles) - LAG), len(tiles)):
            fm_tile(i_, *tiles[i_], wn_ffn, B["wn_ffn"])
        w_release(*so)
        handoff(Bma + BmT, BactT)
        wb_ple, Bwb_ple = wbc_load(w_plepost)

        for gi in range(11):
            sfg = w_acquire("fg")
            sfu = w_acquire("fu")
            for j in range(2):
                ci = gi * 2 + j
                bg, pg = ps_alloc(nbW)
                mm_fm(bg, pg, sfg, j, hT, allhT, W)
                bu, pu = ps_alloc(nbW)
                mm_fm(bu, pu, sfu, j, hT, allhT, W)
                bt, bb = big_alloc()
                ACT(bt[:, 0:W], psum[:, bg:bg + W], AF.Silu, pg, [bb])
                TT("dve", actT[:, ci, 0:W], psum[:, bu:bu + W], bt[:, 0:W], ALU.mult, pu + [bb], [BactT[ci]])
            w_release(sfg, sfu)
        def ple_prep(ti, tt, col0, R):
            par = ti % 2
            src = pp_d[st * TP + tt * 128: st * TP + tt * 128 + 128, :] if tt < 4 else ps_d
            bt, bb = big_alloc()
            DMA("sp", bt[0:R, 0:256], src, [], [bb])
            CP("dve", pb16[par][0:R, :], bt[0:R, 0:256], [bb], [Bpb16[par]])
            base, pbs = ps_alloc(1)
            pv = psum[:, base:base + 512].bitcast(BF16).rearrange("p (k c) -> p k c", k=8)
            TR([(pv[:, kc, 0:R], pb16[par][0:R, kc * 128:(kc + 1) * 128], ident[0:R, 0:R]) for kc in range(2)],
               [Bpb16[par], B["ident"]], pbs)
            ACT(pT[:, :, col0:col0 + R], pv[:, 0:2, 0:R], AF.Copy, pbs, [BpT[tt]])
            fm_tile(ti, tt, col0, R, None, None, norm=False)

        for idx, (tt, col0, R) in enumerate(tiles):
            b_, p_ = ps_alloc(2)
            lst = []
            for half in range(2):
                for kc in range(NFC):
                    lst.append((psum[0:R, b_ + half * 512:b_ + (half + 1) * 512], actT[:, kc, col0:col0 + R],
                                wdn[:, kc, half * 512:(half + 1) * 512], kc == 0, kc == NFC - 1))
            MM(lst, BactT + Bwdn, p_)
            norm_resid(b_, p_, R, tt, wb_ffn, Bwb_ffn)
            if idx >= LAG:
                ple_prep(idx - LAG, *tiles[idx - LAG])
        for i_ in range(max(0, len(tiles) - LAG), len(tiles)):
            ple_prep(i_, *tiles[i_])

        spp = w_acquire("pp")
        spg = [w_acquire("pg") for _ in range(4)]
        wppv = wslot[spp][:, :, :].rearrange("p (a b) n -> p a (b n)", a=2)
        for idx, (tt, col0, R) in enumerate(tiles):
            bp, ppp = ps_alloc(2)
            lst = []
            for half in range(2):
                for kc in range(2):
                    lst.append((psum[0:R, bp + half * 512:bp + (half + 1) * 512], pT[:, kc, col0:col0 + R],
                                wppv[:, kc, half * 512:(half + 1) * 512], kc == 0, kc == 1))
            MM(lst, [BpT[tt], Bws[spp]], ppp)
            bg, pg = ps_alloc(2)
            lst = []
            for i, s_ in enumerate(spg):
                for kc in range(8):
                    lst.append((psum[0:R, bg + i * SLOTW:bg + (i + 1) * SLOTW], hT[:, kc, col0:col0 + R],
                                wslot[s_][:, kc, :], kc == 0, kc == 7))
            MM(lst, [BhT[tt]] + [Bws[s_] for s_ in spg], pg)
            bt, bb = big_alloc()
            ACT(bt[0:R, :], psum[0:R, bg:bg + 1024], AF.Sigmoid, pg, [bb])
            TT("dve", bt[0:R, :], psum[0:R, bp:bp + 1024], bt[0:R, :], ALU.mult, ppp + [bb], [bb])
            sa, sB = stat_alloc()
            ACT(junk[0:R, :], bt[0:R, :], AF.Square, [bb], [Bjunk, sB], accum=sa[0:R, 0:1])
            rstd_from(sa, sB, R, 1024)
            TT("dve", bt[0:R, :], bt[0:R, :], wb_ple[0:R, :], ALU.mult, [bb, Bwb_ple], [bb])
            STT("dve", ht[0:R, tt, :], bt[0:R, :], sa[0:R, 2:3], ht[0:R, tt, :], ALU.mult, ALU.add, [bb, sB, Bht[tt]], [Bht[tt]])
            dst = yp[st * TP + tt * 128: st * TP + tt * 128 + 128, :] if tt < 4 else ys
            DMA("sp", dst, ht[0:R, tt, :], [Bht[tt]], [])
            if not last:
                r0 = (st + 1) * TP + tt * 128
                DMA("sp", ht[:, tt, :], xp[r0:r0 + 128, :], [], [Bht[tt]])
                if idx >= 2 and idx - 2 < 2:
                    fm_prep(idx - 2, tiles[idx - 2][0], 128)
                    preA.add((st + 1, tiles[idx - 2][0]))
        w_release(spp, *spg)

    P.emit(nc, es)
    es.close()
    return nc


_NC_CACHE = {}


def kernel(x_prompt, x_sample, state_conv, state_gla, p_prompt, p_sample,
           w_norm_mix_pre, w_in, w_conv, w_a_out, w_gk, b_gk, w_gla_norm, w_b_out, w_o,
           w_norm_mix_post, w_norm_ffn_pre, w_ffn_gate, w_ffn_up, w_ffn_down, w_norm_ffn_post,
           w_ple_proj, w_ple_gate, w_norm_ple_post):
    f = lambda a: np.ascontiguousarray(np.asarray(a, dtype=np.float32))
    if "nc" not in _NC_CACHE:
        _NC_CACHE["nc"] = build_program()
    nc = _NC_CACHE["nc"]
    shared = {
        "w_pre": f(w_norm_mix_pre[0]), "w_in": f(w_in[0]), "w_conv": f(w_conv[0]), "w_a": f(w_a_out[0]),
        "w_gk": f(w_gk[0]), "b_gk": f(b_gk[0]), "w_gn": f(w_gla_norm[0]), "w_b": f(w_b_out[0]), "w_o": f(w_o[0]),
        "w_mixpost": f(w_norm_mix_post[0]), "w_ffnpre": f(w_norm_ffn_pre[0]), "w_fg": f(w_ffn_gate[0]),
        "w_fu": f(w_ffn_up[0]), "w_fd": f(w_ffn_down[0]), "w_ffnpost": f(w_norm_ffn_post[0]),
        "w_pp": f(w_ple_proj[0]), "w_pg": f(w_ple_gate[0]), "w_plepost": f(w_norm_ple_post[0]),
    }
    x_prompt = np.asarray(x_prompt); x_sample = np.asarray(x_sample)
    state_conv = np.asarray(state_conv); state_gla = np.asarray(state_gla)
    p_prompt = np.asarray(p_prompt); p_sample = np.asarray(p_sample)
    in_maps = []
    for c in range(8):
        m = dict(shared)
        m["xp"] = f(x_prompt[c])
        m["xs"] = f(x_sample[c * NS:(c + 1) * NS, 0, :])
        m["sconv"] = f(state_conv[0, c * NS:(c + 1) * NS])
        m["sgla"] = f(state_gla[0, c * NS:(c + 1) * NS])
        m["pp"] = f(p_prompt[0, c])
        m["psm"] = f(p_sample[0, c * NS:(c + 1) * NS, 0, :])
        in_maps.append(m)
    res = run_bass_kernel_spmd(nc, in_maps, core_ids=list(range(8)))
    r = res.results
    y_prompt = np.stack([r[c]["yp"] for c in range(8)], axis=0).astype(np.float32)
    y_sample = np.concatenate([r[c]["ys"] for c in range(8)], axis=0)[:, None, :].astype(np.float32)
    new_conv_prompt = np.stack([r[c]["ncp"] for c in range(8)], axis=0)[None].astype(np.float32)
    new_gla_prompt = np.stack([r[c]["ngp"] for c in range(8)], axis=0)[None].astype(np.float32)
    new_conv_sample = np.concatenate([r[c]["ncs"] for c in range(8)], axis=0)[None].astype(np.float32)
    new_gla_sample = np.concatenate([r[c]["ngs"] for c in range(8)], axis=0)[None].astype(np.float32)
    return (y_prompt, y_sample, new_conv_prompt, new_gla_prompt, new_conv_sample, new_gla_sample)
```

```python
import contextlib
import numpy as np
import concourse.bass as bass
import concourse.mybir as mybir
from concourse.bass_utils import run_bass_kernel_spmd

F32 = mybir.dt.float32
BF16 = mybir.dt.bfloat16
AF = mybir.ActivationFunctionType
ALU = mybir.AluOpType

ENGS = ("sp", "pe", "act", "dve", "pool")
EPS = 1e-6
NST = 4
TP = 512
NS = 16
DFF = 2816
NFC = DFF // 128
import os
STRICT = os.environ.get("KSTRICT", "0") == "1"


class Buf:
    __slots__ = ("name", "last_w", "readers", "excl")

    def __init__(self, name, excl=False):
        self.name = name
        self.last_w = None
        self.readers = []
        self.excl = excl


class Op:
    __slots__ = ("eng", "fn", "deps", "is_dma", "signal", "sig_val", "sem", "prev_same_sem")

    def __init__(self, eng, fn, is_dma):
        self.eng = eng
        self.fn = fn
        self.is_dma = is_dma
        self.deps = []
        self.signal = is_dma
        self.sig_val = None
        self.sem = None
        self.prev_same_sem = None


class Prog:
    def __init__(self):
        self.ops = {e: [] for e in ENGS}
        self.n_dma_sems = {"sp": 12, "pool": 8}

    def op(self, eng, fn, reads=(), writes=(), dma=False):
        o = Op(eng, fn, dma)
        deps = {}

        def add(w, kind):
            cur = deps.get(id(w))
            if cur is None or kind < cur[1]:
                deps[id(w)] = (w, kind)

        for b in reads:
            if b.last_w is not None:
                add(b.last_w, 0)
            if b.excl:
                for r in b.readers:
                    add(r, 2)
        for b in writes:
            if b.last_w is not None:
                add(b.last_w, 1)
            for r in b.readers:
                add(r, 1)
        for w, kind in deps.values():
            if w is o:
                continue
            if (not w.is_dma) and (not dma) and w.eng == eng:
                if eng == "pe" or kind == 2 or (kind == 1 and not STRICT):
                    continue
            o.deps.append(w)
            w.signal = True
        for b in reads:
            b.readers.append(o)
        for b in writes:
            b.last_w = o
            b.readers = []
        self.ops[eng].append(o)
        return o

    def emit(self, nc, es):
        sems = {e: es.enter_context(nc.semaphore("s_" + e)) for e in ("pe", "act", "dve", "pool")}
        dsems = {q: [es.enter_context(nc.semaphore(f"d_{q}{i}")) for i in range(n)]
                 for q, n in self.n_dma_sems.items()}
        for e in ENGS:
            cnt = 0
            dcnt = 0
            last_on_sem = {}
            for o in self.ops[e]:
                if o.is_dma:
                    pool = dsems[e]
                    k = dcnt % len(pool)
                    o.sem = pool[k]
                    o.sig_val = 16 * (dcnt // len(pool) + 1)
                    o.prev_same_sem = last_on_sem.get(k)
                    last_on_sem[k] = o
                    dcnt += 1
                elif o.signal:
                    cnt += 1
                    o.sem = sems[e]
                    o.sig_val = cnt
        finals = []
        for q in dsems:
            last = {}
            for o in self.ops[q]:
                if o.is_dma:
                    last[id(o.sem)] = o
            finals.extend(last.values())
        block = es.enter_context(nc.Block())

        def run(e, h):
            waited = {}

            def wait(sem, val):
                if waited.get(id(sem), 0) >= val:
                    return
                h.wait_ge(sem, val)
                waited[id(sem)] = val

            for o in self.ops[e]:
                for d in o.deps:
                    wait(d.sem, d.sig_val)
                if o.is_dma and o.prev_same_sem is not None:
                    wait(o.prev_same_sem.sem, o.prev_same_sem.sig_val)
                ins = o.fn(h)
                if o.signal:
                    ins.then_inc(o.sem, 16 if o.is_dma else 1)
            if e == "sp":
                for o in finals:
                    wait(o.sem, o.sig_val)

        @block.sync
        def _(h):
            run("sp", h)

        @block.tensor
        def _(h):
            run("pe", h)

        @block.scalar
        def _(h):
            run("act", h)

        @block.vector
        def _(h):
            run("dve", h)

        @block.gpsimd
        def _(h):
            run("pool", h)


def handoff(src, dst):
    users = []
    for s in src:
        users.extend(s.readers)
        if s.last_w is not None:
            users.append(s.last_w)
    for d in dst:
        d.readers = list(d.readers) + users


SLOTW = 256
NSLOT = 8
LAG = 9


def build_program():
    nc = bass.Bass("TRN2", target_bir_lowering=False)
    P = Prog()
    es = contextlib.ExitStack()

    def din(name, shape):
        return nc.dram_tensor(name, shape, F32, kind="ExternalInput").ap()

    def dout(name, shape):
        return nc.dram_tensor(name, shape, F32, kind="ExternalOutput").ap()

    xp = din("xp", [NST * TP, 1024])
    xs = din("xs", [NS, 1024])
    sconv = din("sconv", [NS, 2, 1024])
    sgla = din("sgla", [NS, 4, 128, 256])
    pp_d = din("pp", [NST * TP, 256])
    ps_d = din("psm", [NS, 256])
    w_pre = din("w_pre", [1024])
    w_in = din("w_in", [1024, 8208])
    w_conv = din("w_conv", [3, 1024])
    w_a = din("w_a", [1024, 1024])
    w_gk = din("w_gk", [16, 512])
    b_gk = din("b_gk", [512])
    w_gn = din("w_gn", [256])
    w_b = din("w_b", [1024, 1024])
    w_o = din("w_o", [1024, 1024])
    w_mixpost = din("w_mixpost", [1024])
    w_ffnpre = din("w_ffnpre", [1024])
    w_fg = din("w_fg", [1024, DFF])
    w_fu = din("w_fu", [1024, DFF])
    w_fd = din("w_fd", [DFF, 1024])
    w_ffnpost = din("w_ffnpost", [1024])
    w_pp = din("w_pp", [256, 1024])
    w_pg = din("w_pg", [1024, 1024])
    w_plepost = din("w_plepost", [1024])
    yp = dout("yp", [NST * TP, 1024])
    ys = dout("ys", [NS, 1024])
    ncp = dout("ncp", [2, 1024])
    ngp = dout("ngp", [4, 128, 256])
    ncs = dout("ncs", [NS, 2, 1024])
    ngs = dout("ngs", [NS, 4, 128, 256])

    def sb(name, shape, dt):
        return es.enter_context(nc.sbuf_tensor(name, shape, dt))

    WM = TP + NS

    ident = sb("ident", [128, 128], BF16)
    identf = sb("identf", [128, 128], F32)
    tri = sb("tri", [128, 128], F32)
    wn_pre = sb("wn_pre", [128, 8], F32)
    wn_ffn = sb("wn_ffn", [128, 8], F32)
    wcv = sb("wcv", [128, 3, 8], F32)
    wgn_bc = sb("wgn_bc", [128, 256], F32)
    wbc = [sb(f"wbc{i}", [128, 1024], F32) for i in range(2)]
    Bwbc = [Buf("wbc0"), Buf("wbc1")]
    wgk = sb("wgk", [32, 512], F32)
    gkl = sb("gkl", [32, WM], F32)
    maskrow = sb("maskrow", [128, NS, 4, NS], BF16)
    uhist = sb("uhist", [128, 8, 2], F32)
    scT = sb("scT", [128, 2, 8, NS], F32)
    utail = sb("utail", [128, 8, 18], F32)
    S = sb("S", [128, 4, 256], F32)
    stat = sb("stat", [128, 16, 4], F32)
    ssg = sb("ssg", [128, 2, 8], F32)
    aS = sb("aS", [128, 4, NS], F32)
    aStmp = sb("aStmp", [128, 4, NS], F32)
    sqk = sb("sqk", [NS, 4], F32)
    QA = sb("QA", [128, 4, NS], BF16)
    BQA = Buf("QA")
    B = {}

    def mk(*names):
        for n in names:
            B[n] = Buf(n)

    mk("ident", "identf", "tri", "wn_pre", "wn_ffn", "wcv", "wgn_bc",
       "wgk", "gkl", "maskrow", "uhist", "scT", "utail", "S", "aS", "aStmp", "ssg0", "ssg1")
    statB = [Buf(f"stat{i}") for i in range(16)]

    ht = sb("ht", [128, 5, 1024], F32)
    hT = sb("hT", [128, 8, WM], BF16)
    Bht = [Buf(f"ht{i}") for i in range(5)]
    BhT = [Buf(f"hT{i}") for i in range(5)]

    regA = sb("regA", [128, 12672], BF16)
    merged_a = regA[:, 0:8448].bitcast(F32).rearrange("p (k c) -> p k c", k=8)
    mergedT = regA[:, 8448:12672].rearrange("p (k c) -> p k c", k=8)
    actT = regA[:, 0:NFC * WM].rearrange("p (k c) -> p k c", k=NFC)
    Bma = [Buf(f"ma{i}") for i in range(8)]
    BmT = [Buf(f"mT{i}") for i in range(8)]
    BactT = [Buf(f"actT{i}") for i in range(NFC)]

    regB = sb("regB", [128, NFC * 1024], BF16)
    o = 0
    goT = regB[:, o:o + 8 * WM].rearrange("p (k c) -> p k c", k=8); o += 8 * WM
    qT = regB[:, o:o + 4 * WM].rearrange("p (k c) -> p k c", k=4); o += 4 * WM
    kT = regB[:, o:o + 4 * WM].rearrange("p (k c) -> p k c", k=4); o += 4 * WM
    v_tok = regB[:, o:o + 5 * 1024].rearrange("p (k c) -> p k c", k=5); o += 5 * 1024
    sg_tok = regB[:, o:o + 5 * 1024].rearrange("p (k c) -> p k c", k=5); o += 5 * 1024
    gtmp = []
    for i in range(5):
        gtmp.append(regB[:, o:o + 512]); o += 512
    qe_a, ke_a, kd_a, scm_a = [t.rearrange("p (h c) -> p h c", h=4) for t in gtmp[0:4]]
    kdt_a = gtmp[4]
    assert o <= NFC * 1024, o
    wdn = regB[:, 0:NFC * 1024].rearrange("p (k c) -> p k c", k=NFC)
    Sbf_t = [sb(f"Sbf{i}", [128, 4, 256], BF16) for i in range(2)]
    Sbf = [t[:, :, :] for t in Sbf_t]
    BgoT = [Buf(f"goT{i}") for i in range(8)]
    BgoTc = [Buf(f"goTc{i}") for i in range(5)]
    Bq, Bk = Buf("qT"), Buf("kT")
    Bv = [Buf(f"v{i}") for i in range(5)]
    Bsg = [Buf(f"sg{i}") for i in range(5)]
    BSbf = [Buf("Sbf0"), Buf("Sbf1")]
    Bqe, Bke, Bkd, Bscm, Bkdt = [Buf(n) for n in ("qe", "ke", "kd", "scm", "kdt")]
    Bwdn = [Buf(f"wdn{i}") for i in range(6)]
    mixB_bufs = BgoT + BgoTc + [Bq, Bk] + Bv + Bsg + [Bqe, Bke, Bkd, Bscm, Bkdt]

    regC = sb("regC", [128, 4608], F32)
    regCc = sb("regCc", [128, 2116], F32)
    cS = regCc[:, 0:528]
    ubuf = [regCc[:, 528:1058], regCc[:, 1058:1588]]
    ycv = regCc[:, 1588:2116]
    e1 = regC[:, 0:512]
    lt = [regC[:, 512:1024], regC[:, 1024:1536]]
    eb_a = regC[:, 1536:2048].rearrange("p (h c) -> p h c", h=4)
    einv_a = regC[:, 2048:2560].rearrange("p (h c) -> p h c", h=4)
    ed_a = regC[:, 2560:3072].rearrange("p (h c) -> p h c", h=4)
    onb = [regC[:, 3072:3328], regC[:, 3328:3584]]
    oS = regC[0:NS, 3584:4608]
    BcS, Bu, Byc = Buf("cS"), [Buf("u0"), Buf("u1")], Buf("yc")
    Be1, Blt = Buf("e1"), [Buf("lt0"), Buf("lt1")]
    Beb, Beinv, Bed = Buf("eb"), Buf("einv"), Buf("ed")
    Bon = [Buf("on0"), Buf("on1")]
    BoS = Buf("oS")
    convC = [BcS, Byc] + Bu
    glaC = [Be1] + Blt + [Beb, Beinv, Bed] + Bon

    NBT = 2
    bigt = [sb(f"bigt{i}", [128, 1024], F32) for i in range(NBT)]
    Bbig = [Buf(f"bigt{i}") for i in range(NBT)]
    junk = sb("junk", [128, 1024], BF16)
    Bjunk = Buf("junk")
    SSbf = junk[:, :].rearrange("p (h c) -> p h c", h=4)
    hs = [sb(f"hs{i}", [128, 1024], BF16) for i in range(2)]
    Bhs = [Buf("hs0"), Buf("hs1")]
    ogt, Bogt = hs[0], Bhs[0]
    kS_tok = hs[1][0:NS, 0:512]
    KSj = hs[1][0:NS, 512:1024]
    BkS, BKSj = Buf("kS"), Buf("KSj")
    SS = [sb(f"SS{i}", [128, 4, 256], F32) for i in range(2)]
    BSS = [Buf("SS0"), Buf("SS1")]
    QmJ = [sb(f"QmJ{i}", [128, 4, NS], BF16) for i in range(2)]
    BQmJ = [Buf("QmJ0"), Buf("QmJ1")]
    pb16 = [sb(f"pb16{i}", [128, 256], BF16) for i in range(2)]
    Bpb16 = [Buf("pb160"), Buf("pb161")]
    pT = sb("pT", [128, 2, WM], BF16)
    BpT = [Buf(f"pT{i}") for i in range(5)]

    wslot = [sb(f"wslot{i}", [128, 8, SLOTW], BF16) for i in range(NSLOT)]
    Bws = [Buf(f"wslot{i}") for i in range(NSLOT)]

    psum = es.enter_context(nc.psum_tensor("psum", [128, 4096], F32))
    Bps = [Buf(f"psb{i}", excl=True) for i in range(8)]
    ps_state = {"next": 0, "lo": 0, "hi": 8}

    def ps_alloc(n):
        p = ps_state["next"]
        lo, hi = ps_state["lo"], ps_state["hi"]
        if p < lo or p >= hi:
            p = lo
        if n == 2 and (p % 2):
            p += 1
        if p + n > hi:
            p = lo
        ps_state["next"] = p + n
        return p * 512, Bps[p:p + n]

    def G(b, n):
        return b * 512, Bps[b:b + n]

    big_state = {"next": 0}

    def big_alloc():
        i = big_state["next"]
        big_state["next"] = (i + 1) % NBT
        return bigt[i], Bbig[i]

    stat_state = {"next": 0}

    def stat_alloc():
        i = stat_state["next"]
        stat_state["next"] = (i + 1) % 16
        return stat[:, i, :], statB[i]

    def ACT(out, in_, func, reads, writes, scale=None, bias=None, accum=None):
        kw = {}
        if scale is not None:
            kw["scale"] = scale
        if bias is not None:
            kw["bias"] = bias
        if accum is not None:
            kw["accum_out"] = accum
        P.op("act", lambda h: h.activation(out=out, in_=in_, func=func, **kw), reads, writes)

    def TT(eng, out, in0, in1, op, reads, writes):
        P.op(eng, lambda h: h.tensor_tensor(out=out, in0=in0, in1=in1, op=op), reads, writes)

    def TS(eng, out, in0, s1, op0, reads, writes):
        P.op(eng, lambda h: h.tensor_scalar(out=out, in0=in0, scalar1=s1, scalar2=None, op0=op0), reads, writes)

    def STT(eng, out, in0, scalar, in1, op0, op1, reads, writes):
        P.op(eng, lambda h: h.scalar_tensor_tensor(out=out, in0=in0, scalar=scalar, in1=in1, op0=op0, op1=op1), reads, writes)

    def CP(eng, out, in_, reads, writes):
        P.op(eng, lambda h: h.tensor_copy(out=out, in_=in_), reads, writes)

    def MS(eng, ap, val, writes):
        P.op(eng, lambda h: h.memset(ap, val), (), writes)

    def MM(lst, reads, writes):
        def fn(h):
            ins = None
            for (out, lhsT, rhs, start, stop) in lst:
                ins = h.matmul(out, lhsT=lhsT, rhs=rhs, start=start, stop=stop)
            return ins
        P.op("pe", fn, reads, writes)

    def TR(lst, reads, writes):
        def fn(h):
            ins = None
            for (out, in_, idn) in lst:
                ins = h.transpose(out=out, in_=in_, identity=idn)
            return ins
        P.op("pe", fn, reads, writes)

    def DMA(q, out, in_, reads, writes, noncontig=False):
        def fn(h):
            if noncontig:
                with nc.allow_non_contiguous_dma(reason="small strided constant load"):
                    return h.dma_start(out=out, in_=in_)
            return h.dma_start(out=out, in_=in_)
        P.op(q, fn, reads, writes, dma=True)

    MS("pool", identf[:], 1.0, [B["identf"]])
    P.op("pool", lambda h: h.affine_select(out=identf[:], in_=identf[:], pattern=[[-1, 128]], compare_op=ALU.is_equal,
                                           fill=0.0, base=0, channel_multiplier=1), [B["identf"]], [B["identf"]])
    CP("dve", ident[:], identf[:], [B["identf"]], [B["ident"]])
    MS("pool", tri[:], 1.0, [B["tri"]])
    P.op("pool", lambda h: h.affine_select(out=tri[:], in_=tri[:], pattern=[[1, 128]], compare_op=ALU.is_ge,
                                           fill=0.0, base=0, channel_multiplier=-1), [B["tri"]], [B["tri"]])
    DMA("sp", wn_pre[:], w_pre.rearrange("(kc p) -> p kc", p=128), [], [B["wn_pre"]], noncontig=True)
    DMA("sp", wn_ffn[:], w_ffnpre.rearrange("(kc p) -> p kc", p=128), [], [B["wn_ffn"]], noncontig=True)
    DMA("sp", wcv[:], w_conv.rearrange("j (kc p) -> p j kc", p=128), [], [B["wcv"]], noncontig=True)
    DMA("sp", wgn_bc[:], w_gn.partition_broadcast(128), [], [B["wgn_bc"]])
    DMA("sp", wgk[0:16, :], w_gk, [], [B["wgk"]])
    DMA("sp", wgk[16:17, :], b_gk.rearrange("(o n) -> o n", o=1), [], [B["wgk"]])
    MS("dve", gkl[:], 1.0, [B["gkl"]])
    MS("dve", uhist[:], 0.0, [B["uhist"]])
    MS("dve", S[:], 0.0, [B["S"]])
    MS("dve", Sbf[0], 0.0, [BSbf[0]])
    MS("dve", stat[:], 1.0, statB)
    MS("dve", maskrow[:], 0.0, [B["maskrow"]])
    for j in range(NS):
        MS("dve", maskrow[:, j, :, j:j + 1], 1.0, [B["maskrow"]])
    for t in range(2):
        bt, bb = big_alloc()
        DMA("sp", bt[0:NS, :], sconv[:, t, :], [], [bb])
        if t == 1:
            DMA("sp", ncs[:, 0, :], bt[0:NS, :], [bb], [])
        base, pbs = ps_alloc(1)
        TR([(psum[:, base + kc * NS:base + (kc + 1) * NS], bt[0:NS, kc * 128:(kc + 1) * 128], identf[0:NS, 0:NS])
            for kc in range(8)], [bb, B["identf"]], pbs)
        ACT(scT[:, t, :, :], psum[:, base:base + 8 * NS].rearrange("p (k c) -> p k c", k=8), AF.Copy, pbs, [B["scT"]])

    wbc_state = {"i": 0}

    def wbc_load(src):
        i = wbc_state["i"]
        wbc_state["i"] = 1 - i
        DMA("sp", wbc[i][:], src.partition_broadcast(128), [], [Bwbc[i]])
        return wbc[i], Bwbc[i]

    def build_groups():
        g = []
        for st in range(NST):
            for i in range(2):
                g.append(("q", w_in, 0, 8, 3072 + i * SLOTW, SLOTW))
            for i in range(2):
                g.append(("k", w_in, 0, 8, 3584 + i * SLOTW, SLOTW))
            g.append(("gklr", w_in, 0, 8, 6144, 16))
            for i in range(4):
                g.append(("v", w_in, 0, 8, 4096 + i * SLOTW, SLOTW))
            for i in range(4):
                g.append(("g", w_in, 0, 8, 5120 + i * SLOTW, SLOTW))
            for qd in range(4):
                g.append(("c", w_in, 0, 8, 1024 + qd * SLOTW, SLOTW))
                g.append(("x", w_in, 0, 8, 2048 + qd * SLOTW, SLOTW))
                g.append(("b", w_in, 0, 8, 0 + qd * SLOTW, SLOTW))
            for qd in range(4):
                g.append(("wa", w_a, 0, 8, qd * SLOTW, SLOTW))
                g.append(("ga", w_in, 0, 8, 6160 + qd * SLOTW, SLOTW))
            for qd in range(4):
                g.append(("wb", w_b, 0, 8, qd * SLOTW, SLOTW))
                g.append(("gb", w_in, 0, 8, 7184 + qd * SLOTW, SLOTW))
            for i in range(4):
                g.append(("wo", w_o, 0, 8, i * SLOTW, SLOTW))
            for gi in range(11):
                g.append(("fg", w_fg, 0, 8, gi * SLOTW, SLOTW))
                g.append(("fu", w_fu, 0, 8, gi * SLOTW, SLOTW))
                if gi < 6:
                    k0 = (gi // 2) * 8
                    nk = min(8, NFC - k0)
                    g.append(("WD", w_fd, k0, nk, (gi % 2) * 512, 512))
            g.append(("pp", w_pp, 0, 2, 0, 1024))
            for i in range(4):
                g.append(("pg", w_pg, 0, 8, i * SLOTW, SLOTW))
        return g

    groups = build_groups()
    GPS = len(groups) // NST
    ring_idx = {}
    wd_idx = {}
    for li in range(GPS):
        if groups[li][0] == "WD":
            wd_idx[li] = len(wd_idx)
        else:
            ring_idx[li] = len(ring_idx)
    wscr = nc.dram_tensor("wscr", [len(ring_idx), 128, 8 * SLOTW], BF16).ap()
    wdscr = nc.dram_tensor("wdscr", [6, 128, 8 * 512], BF16).ap()
    Bscr = [Buf(f"scr{i}") for i in range(len(ring_idx))]
    Bwdscr = [Buf(f"wdscr{i}") for i in range(6)]
    WS = {"next_load": 0, "next_use": 0, "free": list(range(NSLOT)), "slot_of": {}, "wd_ok": False, "wd_idx": 0}

    def w_prefetch():
        while WS["next_load"] < len(groups):
            gi = WS["next_load"]
            name, mat, k0, nk, c0, nco = groups[gi]
            st_, li = gi // GPS, gi % GPS
            src = mat[k0 * 128:(k0 + nk) * 128, c0:c0 + nco].rearrange("(kc p) n -> p kc n", p=128)
            if name == "WD":
                if not WS["wd_ok"]:
                    return
                piece = wd_idx[li]
                dst = wdn[:, k0:k0 + nk, c0:c0 + nco]
                scr = wdscr[piece][:, 0:nk * 512].rearrange("p (k c) -> p k c", k=nk)
                conv_st = piece % 2
                if st_ <= conv_st:
                    DMA("pool", dst, src, [], [Bwdn[piece]])
                    if st_ == conv_st:
                        DMA("sp", scr, dst, [Bwdn[piece]], [Bwdscr[piece]])
                else:
                    DMA("pool", dst, scr, [Bwdscr[piece]], [Bwdn[piece]])
                WS["next_load"] += 1
                continue
            if not WS["free"]:
                return
            s = WS["free"].pop(0)
            ri = ring_idx[li]
            img = wslot[s][:, :, :].rearrange("p k c -> p (k c)")
            conv_st = ri % 2
            if st_ <= conv_st:
                if name == "pp":
                    dst = wslot[s][:, :, :].rearrange("p (a b) n -> p a (b n)", a=2)
                else:
                    dst = wslot[s][:, 0:nk, 0:nco]
                if name == "gklr":
                    MS("pool", wslot[s][:, :, :], 0.0, [Bws[s]])
                DMA("pool", dst, src, [], [Bws[s]])
                if st_ == conv_st:
                    DMA("sp", wscr[ri], img, [Bws[s]], [Bscr[ri]])
            else:
                DMA("pool", img, wscr[ri], [Bscr[ri]], [Bws[s]])
            WS["slot_of"][gi] = s
            WS["next_load"] += 1

    def w_acquire(name):
        while groups[WS["next_use"]][0] == "WD":
            WS["next_use"] += 1
        gi = WS["next_use"]
        assert groups[gi][0] == name, (groups[gi][0], name)
        if gi not in WS["slot_of"]:
            w_prefetch()
        assert gi in WS["slot_of"], ("weight ring deadlock", name, gi)
        WS["next_use"] += 1
        return WS["slot_of"][gi]

    def w_release(*slots):
        for s in slots:
            WS["free"].append(s)
        w_prefetch()

    def mm_fm(base, pbs, slot, j, src, srcB, W, M=128):
        lst = []
        blocks = [(0, TP)] + ([(TP, W - TP)] if W > TP else [])
        for (c0, cn) in blocks:
            for kc in range(8):
                lst.append((psum[0:M, base + c0:base + c0 + cn], wslot[slot][:, kc, j * 128:j * 128 + M],
                            src[:, kc, c0:c0 + cn], kc == 0, kc == 7))
        MM(lst, [Bws[slot]] + srcB, pbs)

    def rstd_from(ssap, sB, R, n):
        ACT(ssap[0:R, 1:2], ssap[0:R, 0:1], AF.Ln, [sB], [sB], scale=1.0 / n, bias=EPS)
        ACT(ssap[0:R, 2:3], ssap[0:R, 1:2], AF.Exp, [sB], [sB], scale=-0.5)

    def fm_prep(ti, tt, R, norm=True):
        par = ti % 2
        if norm:
            sa, sB = stat_alloc()
            ACT(junk[0:R, :], ht[0:R, tt, :], AF.Square, [Bht[tt]], [Bjunk, sB], accum=sa[0:R, 0:1])
            rstd_from(sa, sB, R, 1024)
            TS("dve", hs[par][0:R, :], ht[0:R, tt, :], sa[0:R, 2:3], ALU.mult, [Bht[tt], sB], [Bhs[par]])
        else:
            CP("dve", hs[par][0:R, :], ht[0:R, tt, :], [Bht[tt]], [Bhs[par]])

    def fm_tr(ti, tt, col0, R, wn, wnB):
        par = ti % 2
        base, pbs = ps_alloc(1)
        pv = psum[:, base:base + 512].bitcast(BF16).rearrange("p (k c) -> p k c", k=8)
        TR([(pv[:, kc, 0:R], hs[par][0:R, kc * 128:(kc + 1) * 128], ident[0:R, 0:R]) for kc in range(8)],
           [Bhs[par], B["ident"]], pbs)
        if wn is not None:
            TT("dve", hT[:, :, col0:col0 + R], pv[:, :, 0:R], wn[:, :].unsqueeze(2).to_broadcast([128, 8, R]),
               ALU.mult, pbs + [wnB], [BhT[tt]])
        else:
            ACT(hT[:, :, col0:col0 + R], pv[:, :, 0:R], AF.Copy, pbs, [BhT[tt]])

    def fm_tile(ti, tt, col0, R, wn, wnB, norm=True):
        fm_prep(ti, tt, R, norm)
        fm_tr(ti, tt, col0, R, wn, wnB)

    def norm_resid(base, pbs, R, tt, wb_, wbB):
        src = psum[0:R, base:base + 1024]
        sa, sB = stat_alloc()
        ACT(junk[0:R, :], src, AF.Square, pbs, [Bjunk, sB], accum=sa[0:R, 0:1])
        rstd_from(sa, sB, R, 1024)
        bt, bb = big_alloc()
        TT("dve", bt[0:R, :], src, wb_[0:R, :], ALU.mult, pbs + [wbB], [bb])
        STT("dve", ht[0:R, tt, :], bt[0:R, :], sa[0:R, 2:3], ht[0:R, tt, :], ALU.mult, ALU.add, [bb, sB, Bht[tt]], [Bht[tt]])

    preA = set()
    for st in range(NST):
        last = st == NST - 1
        W = WM if last else TP
        nbW = 2 if last else 1
        tiles = [(tt, tt * 128, 128) for tt in range(4)] + ([(4, TP, NS)] if last else [])
        allhT = [BhT[t[0]] for t in tiles]

        if st > 0:
            handoff(Bwdn, mixB_bufs)
            handoff(BactT, Bma + BmT)
            WS["wd_ok"] = False

        if st == 0:
            for tt in range(4):
                DMA("sp", ht[:, tt, :], xp[tt * 128:(tt + 1) * 128, :], [], [Bht[tt]])
            DMA("sp", ht[0:NS, 4, :], xs, [], [Bht[4]])
        w_prefetch()
        for ti, (tt, col0, R) in enumerate(tiles):
            if (st, tt) in preA:
                fm_tr(ti, tt, col0, R, wn_pre, B["wn_pre"])
            else:
                fm_tile(ti, tt, col0, R, wn_pre, B["wn_pre"])
        wb_mix, Bwb_mix = wbc_load(w_mixpost)

        for (nm, dstT, dB, scl) in (("q", qT, Bq, float(128 ** -0.5)), ("k", kT, Bk, None)):
            for i in range(2):
                s_ = w_acquire(nm)
                for j in range(2):
                    h_ = i * 2 + j
                    b_, p_ = ps_alloc(nbW)
                    mm_fm(b_, p_, s_, j, hT, allhT, W)
                    ACT(dstT[:, h_, 0:W], psum[:, b_:b_ + W], AF.Copy, p_, [dB], scale=scl)
                w_release(s_)
        sg_ = w_acquire("gklr")
        b_, p_ = ps_alloc(nbW)
        mm_fm(b_, p_, sg_, 0, hT, allhT, W, M=16)
        ACT(gkl[0:16, 0:W], psum[0:16, b_:b_ + W], AF.Copy, p_, [B["gkl"]])
        w_release(sg_)
        for (nm, dstt, dB, fn) in (("v", v_tok, Bv, AF.Copy), ("g", sg_tok, Bsg, AF.Silu)):
            for half in range(2):
                s0 = w_acquire(nm)
                s1 = w_acquire(nm)
                for (tt, col0, R) in tiles:
                    b_, p_ = ps_alloc(1)
                    lst = []
                    for i, s_ in enumerate((s0, s1)):
                        for kc in range(8):
                            lst.append((psum[0:R, b_ + i * SLOTW:b_ + (i + 1) * SLOTW], hT[:, kc, col0:col0 + R],
                                        wslot[s_][:, kc, :], kc == 0, kc == 7))
                    MM(lst, [BhT[tt], Bws[s0], Bws[s1]], p_)
                    ACT(dstt[0:R, tt, half * 512:(half + 1) * 512], psum[0:R, b_:b_ + 512], fn, p_, [dB[tt]])
                w_release(s0, s1)


        def gla_post(src_h, srcB, R, tt, col0, par):
            sg_ap = ssg[:, par, :]
            sgB = B[f"ssg{par}"]
            for h_ in range(4):
                ACT(junk[0:R, 0:256], src_h(h_), AF.Square, srcB, [Bjunk, sgB], accum=sg_ap[0:R, h_:h_ + 1])
            ACT(sg_ap[0:R, 4:8], sg_ap[0:R, 0:4], AF.Ln, [sgB], [sgB], scale=1.0 / 256, bias=EPS)
            ACT(sg_ap[0:R, 4:8], sg_ap[0:R, 4:8], AF.Exp, [sgB], [sgB], scale=-0.5)
            for h_ in range(4):
                hp = h_ % 2
                STT("dve", onb[hp][0:R, :], src_h(h_), sg_ap[0:R, 4 + h_:5 + h_],
                    wgn_bc[0:R, :], ALU.mult, ALU.mult, srcB + [sgB, B["wgn_bc"]], [Bon[hp]])
                TT("dve", ogt[0:R, h_ * 256:(h_ + 1) * 256], onb[hp][0:R, :], sg_tok[0:R, tt, h_ * 256:(h_ + 1) * 256],
                   ALU.mult, [Bon[hp], Bsg[tt]], [Bogt])
            base, pbs = G(0, 1)
            pv = psum[:, base:base + 512].bitcast(BF16).rearrange("p (k c) -> p k c", k=8)
            TR([(pv[:, kc, 0:R], ogt[0:R, kc * 128:(kc + 1) * 128], ident[0:R, 0:R]) for kc in range(8)],
               [Bogt, B["ident"]], pbs)
            ACT(goT[:, :, col0:col0 + R], pv[:, :, 0:R], AF.Copy, pbs, [BgoTc[tt]] + BgoT)

        def sample_prep():
            bgk, pgk = G(0, 1)
            MM([(psum[:, bgk + h_ * NS:bgk + (h_ + 1) * NS], wgk[0:17, h_ * 128:(h_ + 1) * 128], gkl[0:17, TP:TP + NS], True, True)
                for h_ in range(4)], [B["gkl"], B["wgk"]], pgk)
            pgv = psum[:, bgk:bgk + 4 * NS].rearrange("p (k c) -> p k c", k=4)
            ACT(aStmp[:, :, :], pgv, AF.Exp, pgk, [B["aStmp"]], scale=-1.0)
            ACT(aStmp[:, :, :], aStmp[:, :, :], AF.Ln, [B["aStmp"]], [B["aStmp"]], bias=1.0)
            ACT(aS[:, :, :], aStmp[:, :, :], AF.Exp, [B["aStmp"]], [B["aS"]], scale=-1.0 / 16)
            TT("dve", QA[:, :, :], qT[:, :, TP:TP + NS], aS[:, :, :], ALU.mult, [Bq, B["aS"]], [BQA])
            bkt, pkt = G(1, 1)
            pktv = psum[:, bkt:bkt + 512].bitcast(BF16)
            TR([(pktv[0:NS, h_ * 128:(h_ + 1) * 128], kT[:, h_, TP:TP + NS], ident[:, :]) for h_ in range(4)] +
               [(pktv[0:NS, 512 + h_ * 128:512 + (h_ + 1) * 128], qT[:, h_, TP:TP + NS], ident[:, :]) for h_ in range(4)],
               [Bk, Bq, B["ident"]], pkt)
            ACT(kS_tok, pktv[0:NS, 0:512], AF.Copy, pkt, [BkS])
            bt, bb = big_alloc()
            TT("dve", bt[0:NS, 0:512], pktv[0:NS, 512:1024], kS_tok, ALU.mult, pkt + [BkS], [bb])
            for h_ in range(4):
                ACT(junk[0:NS, 0:128], bt[0:NS, h_ * 128:(h_ + 1) * 128], AF.Copy, [bb], [Bjunk, B["aStmp"]],
                    accum=sqk[0:NS, h_:h_ + 1])
            for h_ in range(4):
                TS("dve", oS[:, h_ * 256:(h_ + 1) * 256], v_tok[0:NS, 4, h_ * 256:(h_ + 1) * 256], sqk[0:NS, h_:h_ + 1],
                   ALU.mult, [Bv[4], B["aStmp"]], [BoS])

        def sample_load(j):
            DMA("sp", SS[j % 2][:, :, :], sgla[j].rearrange("h k v -> k h v"), [], [BSS[j % 2]])

        def sample_iter(j):
            jp = j % 2
            if j == 0:
                sample_load(0)
            if j + 1 < NS:
                sample_load(j + 1)
            ACT(SSbf, SS[jp][:, :, :], AF.Copy, [BSS[jp]], [Bjunk])
            TS("dve", KSj, kS_tok, identf[0:NS, j:j + 1], ALU.mult, [BkS, B["identf"]], [BKSj])
            TT("dve", QmJ[jp][:, :, :], QA[:, :, :], maskrow[:, j, :, :], ALU.mult, [BQA, B["maskrow"]], [BQmJ[jp]])
            bkv, pkv = G(0, 2)
            MM([(psum[:, bkv + h_ * 256:bkv + (h_ + 1) * 256], KSj[:, h_ * 128:(h_ + 1) * 128],
                 v_tok[0:NS, 4, h_ * 256:(h_ + 1) * 256], True, True) for h_ in range(4)], [BKSj, Bv[4]], pkv)
            bo, pbo = G(2, 2)
            MM([(psum[0:NS, bo + h_ * 256:bo + (h_ + 1) * 256], QmJ[jp][:, h_, :], SSbf[:, h_, :], True, True)
                for h_ in range(4)], [BQmJ[jp], Bjunk], pbo)
            for h_ in range(4):
                STT("dve", SS[jp][:, h_, :], SS[jp][:, h_, :], aS[:, h_, j:j + 1], psum[:, bkv + h_ * 256:bkv + (h_ + 1) * 256],
                    ALU.mult, ALU.add, [BSS[jp], B["aS"]] + pkv, [BSS[jp]])
            DMA("sp", ngs[j].rearrange("h k v -> k h v"), SS[jp][:, :, :], [BSS[jp]], [])
            TT("dve", oS, oS, psum[0:NS, bo:bo + 1024], ALU.add, pbo + [BoS], [BoS])

        def conv_gen():
            for qd in range(4):
                sc = w_acquire("c")
                sx = w_acquire("x")
                sbk = w_acquire("b")
                for j in range(2):
                    kc = qd * 2 + j
                    par = kc % 2
                    u = ubuf[par]
                    bc, pc = ps_alloc(nbW)
                    mm_fm(bc, pc, sc, j, hT, allhT, W)
                    ACT(cS[:, 0:W], psum[:, bc:bc + W], AF.Copy, pc, [BcS])
                    bx, px = ps_alloc(nbW)
                    mm_fm(bx, px, sx, j, hT, allhT, W)
                    TT("dve", u[:, 2:2 + W], psum[:, bx:bx + W], cS[:, 0:W], ALU.mult, px + [BcS], [Bu[par]])
                    CP("dve", u[:, 0:2], uhist[:, kc, :], [B["uhist"]], [Bu[par]])
                    CP("dve", uhist[:, kc, :], u[:, TP:TP + 2], [Bu[par]], [B["uhist"]])
                    TS("dve", ycv[:, 0:TP], u[:, 0:TP], wcv[:, 0, kc:kc + 1], ALU.mult, [Bu[par], B["wcv"]], [Byc])
                    STT("dve", ycv[:, 0:TP], u[:, 1:TP + 1], wcv[:, 1, kc:kc + 1], ycv[:, 0:TP], ALU.mult, ALU.add,
                        [Bu[par], B["wcv"], Byc], [Byc])
                    STT("dve", ycv[:, 0:TP], u[:, 2:TP + 2], wcv[:, 2, kc:kc + 1], ycv[:, 0:TP], ALU.mult, ALU.add,
                        [Bu[par], B["wcv"], Byc], [Byc])
                    if last:
                        TS("dve", ycv[:, TP:W], scT[:, 0, kc, :], wcv[:, 0, kc:kc + 1], ALU.mult, [B["scT"], B["wcv"]], [Byc])
                        STT("dve", ycv[:, TP:W], scT[:, 1, kc, :], wcv[:, 1, kc:kc + 1], ycv[:, TP:W], ALU.mult, ALU.add,
                            [B["scT"], B["wcv"], Byc], [Byc])
                        STT("dve", ycv[:, TP:W], u[:, TP + 2:W + 2], wcv[:, 2, kc:kc + 1], ycv[:, TP:W], ALU.mult, ALU.add,
                            [Bu[par], B["wcv"], Byc], [Byc])
                        CP("dve", utail[:, kc, :], u[:, TP:TP + 18], [Bu[par]], [B["utail"]])
                    bb_, pb_ = ps_alloc(nbW)
                    mm_fm(bb_, pb_, sbk, j, hT, allhT, W)
                    TT("dve", mergedT[:, kc, 0:W], psum[:, bb_:bb_ + W], ycv[:, 0:W], ALU.mult, pb_ + [Byc], [BmT[kc]])
                    yield
                w_release(sc, sx, sbk)
            if last:
                base, pbs = ps_alloc(2)
                TR([(psum[0:18, base + kc * 128:base + (kc + 1) * 128], utail[:, kc, :], identf[:, :]) for kc in range(8)],
                   [B["utail"], B["identf"]], pbs)
                bt, bb = big_alloc()
                ACT(bt[0:18, :], psum[0:18, base:base + 1024], AF.Copy, pbs, [bb])
                DMA("sp", ncp[:, :], bt[0:2, :], [bb], [])
                DMA("sp", ncs[:, 1, :], bt[2:18, :], [bb], [])

            for qd in range(4):
                swa = w_acquire("wa")
                sga = w_acquire("ga")
                for j in range(2):
                    kc = qd * 2 + j
                    by, py = ps_alloc(nbW)
                    mm_fm(by, py, swa, j, mergedT, BmT, W)
                    bg, pg = ps_alloc(nbW)
                    mm_fm(bg, pg, sga, j, hT, allhT, W)
                    bt, bb = big_alloc()
                    ACT(bt[:, 0:W], psum[:, bg:bg + W], AF.Tanh, pg, [bb], scale=0.5)
                    ACT(bt[:, 0:W], bt[:, 0:W], AF.Identity, [bb], [bb], scale=0.5, bias=0.5)
                    TT("dve", merged_a[:, kc, 0:W], psum[:, by:by + W], bt[:, 0:W], ALU.mult, py + [bb], [Bma[kc]])
                    yield
                w_release(swa, sga)


        def gla_gen():
            if last:
                sample_prep()
                yield
            def stage1(c):
                par_ = (st * 4 + c) % 2
                cols_ = slice(c * 128, (c + 1) * 128)
                bgk, pgk = G(0, 1)
                MM([(psum[:, bgk + q_ * 128:bgk + (q_ + 1) * 128], gkl[0:17, cols_], wgk[0:17, q_ * 128:(q_ + 1) * 128], True, True)
                    for q_ in range(4)], [B["gkl"], B["wgk"]], pgk)
                ACT(e1, psum[:, bgk:bgk + 512], AF.Exp, pgk, [Be1], scale=-1.0)
                ACT(lt[par_], e1, AF.Ln, [Be1], [Blt[par_]], bias=1.0)

            stage1(0)
            yield
            for c in range(4):
                cidx = st * 4 + c
                par = cidx % 2
                cols = slice(c * 128, (c + 1) * 128)
                bpb, ppb = G(1, 1)
                MM([(psum[:, bpb + h_ * 128:bpb + (h_ + 1) * 128], lt[par][:, h_ * 128:(h_ + 1) * 128], tri[:, :], True, True)
                    for h_ in range(4)], [Blt[par], B["tri"]], ppb)
                pbv = psum[:, bpb:bpb + 512].rearrange("p (h c) -> p h c", h=4)
                ACT(eb_a, pbv, AF.Exp, ppb, [Beb], scale=-1.0 / 16)
                ACT(einv_a, pbv, AF.Exp, ppb, [Beinv], scale=1.0 / 16)
                TT("dve", ed_a, einv_a, eb_a[:, :, 127:128].to_broadcast([128, 4, 128]), ALU.mult, [Beinv, Beb], [Bed])
                TT("dve", qe_a, qT[:, :, cols], eb_a, ALU.mult, [Bq, Beb], [Bqe])
                TT("dve", ke_a, kT[:, :, cols], einv_a, ALU.mult, [Bk, Beinv], [Bke])
                TT("dve", kd_a, kT[:, :, cols], ed_a, ALU.mult, [Bk, Bed], [Bkd])
                if c + 1 < 4:
                    stage1(c + 1)
                yield
                bkd, pkd = G(0, 1)
                pkdv = psum[:, bkd:bkd + 512].bitcast(BF16)
                TR([(pkdv[:, h_ * 128:(h_ + 1) * 128], kd_a[:, h_, :], ident[:, :]) for h_ in range(4)], [Bkd, B["ident"]], pkd)
                ACT(kdt_a, pkdv[:, 0:512], AF.Copy, pkd, [Bkdt])
                bsc, psc_ = G(1, 1)
                MM([(psum[:, bsc + h_ * 128:bsc + (h_ + 1) * 128], ke_a[:, h_, :], qe_a[:, h_, :], True, True) for h_ in range(4)],
                   [Bke, Bqe], psc_)
                TT("dve", scm_a, psum[:, bsc:bsc + 512].rearrange("p (h c) -> p h c", h=4),
                   tri[:, :].unsqueeze(1).to_broadcast([128, 4, 128]), ALU.mult, psc_ + [B["tri"]], [Bscm])
                yield
                po, ppo = G(2, 2)
                lst = []
                for h_ in range(4):
                    vh = v_tok[:, c, h_ * 256:(h_ + 1) * 256]
                    lst.append((psum[:, po + h_ * 256:po + (h_ + 1) * 256], scm_a[:, h_, :], vh, True, False))
                    lst.append((psum[:, po + h_ * 256:po + (h_ + 1) * 256], qe_a[:, h_, :], Sbf[par][:, h_, :], False, True))
                MM(lst, [Bscm, Bv[c], Bqe, BSbf[par]], ppo)
                bD, pD_ = G(0, 2)
                MM([(psum[:, bD + h_ * 256:bD + (h_ + 1) * 256], kdt_a[:, h_ * 128:(h_ + 1) * 128],
                     v_tok[:, c, h_ * 256:(h_ + 1) * 256], True, True) for h_ in range(4)], [Bkdt, Bv[c]], pD_)
                for h_ in range(4):
                    STT("dve", S[:, h_, :], S[:, h_, :], eb_a[:, h_, 127:128], psum[:, bD + h_ * 256:bD + (h_ + 1) * 256],
                        ALU.mult, ALU.add, [B["S"], Beb] + pD_, [B["S"]])
                ACT(Sbf[1 - par], S[:, :, :], AF.Copy, [B["S"]], [BSbf[1 - par]])
                yield
                gla_post(lambda h_, po=po: psum[:, po + h_ * 256:po + (h_ + 1) * 256], ppo, 128, c, c * 128, par)
                yield
                if last:
                    for j in range(c * 4, c * 4 + 4):
                        sample_iter(j)
                        yield
            if last:
                DMA("sp", ngp.rearrange("h k v -> k h v"), S[:, :, :], [B["S"]], [])
                gla_post(lambda h_: oS[:, h_ * 256:(h_ + 1) * 256], [BoS], NS, 4, TP, 0)


        ps_state["lo"], ps_state["hi"] = 4, 8
        g_gla, g_conv = gla_gen(), conv_gen()
        n_gla = 17 + (17 if last else 0)
        n_conv = 16
        done_g = 0
        for i_ in range(n_conv):
            while done_g * n_conv < (i_ + 1) * n_gla:
                if next(g_gla, "end") == "end":
                    break
                done_g += 1
            next(g_conv, None)
        for _ in g_gla:
            pass
        for _ in g_conv:
            pass
        ps_state["lo"], ps_state["hi"] = 0, 8

        for qd in range(4):
            swb = w_acquire("wb")
            sgb = w_acquire("gb")
            for j in range(2):
                kc = qd * 2 + j
                by, py = ps_alloc(nbW)
                mm_fm(by, py, swb, j, goT, BgoT + BgoTc[0:len(tiles)], W)
                bg, pg = ps_alloc(nbW)
                mm_fm(bg, pg, sgb, j, hT, allhT, W)
                bt, bb = big_alloc()
                ACT(bt[:, 0:W], psum[:, bg:bg + W], AF.Sigmoid, pg, [bb])
                TT("dve", bt[:, 0:W], psum[:, by:by + W], bt[:, 0:W], ALU.mult, py + [bb], [bb])
                TT("dve", mergedT[:, kc, 0:W], bt[:, 0:W], merged_a[:, kc, 0:W], ALU.add, [bb, Bma[kc]], [BmT[kc]])
            w_release(swb, sgb)
        handoff(mixB_bufs, Bwdn)
        WS["wd_ok"] = True
        wb_ffn, Bwb_ffn = wbc_load(w_ffnpost)

        so = [w_acquire("wo") for _ in range(4)]
        for idx, (tt, col0, R) in enumerate(tiles):
            b_, p_ = ps_alloc(2)
            lst = []
            for i, s_ in enumerate(so):
                for kc in range(8):
                    lst.append((psum[0:R, b_ + i * SLOTW:b_ + (i + 1) * SLOTW], mergedT[:, kc, col0:col0 + R],
                                wslot[s_][:, kc, :], kc == 0, kc == 7))
            MM(lst, BmT + [Bws[s_] for s_ in so], p_)
            norm_resid(b_, p_, R, tt, wb_mix, Bwb_mix)
            if idx >= LAG:
                fm_tile(idx - LAG, *tiles[idx - LAG], wn_ffn, B["wn_ffn"])
        for i_ in range(max(0, len(tiles) - LAG), len(tiles)):
            fm_tile(i_, *tiles[i_], wn_ffn, B["wn_ffn"])
        w_release(*so)
        handoff(Bma + BmT, BactT)
        wb_ple, Bwb_ple = wbc_load(w_plepost)

        for gi in range(11):
            sfg = w_acquire("fg")
            sfu = w_acquire("fu")
            for j in range(2):
                ci = gi * 2 + j
                bg, pg = ps_alloc(nbW)
                mm_fm(bg, pg, sfg, j, hT, allhT, W)
                bu, pu = ps_alloc(nbW)
                mm_fm(bu, pu, sfu, j, hT, allhT, W)
                bt, bb = big_alloc()
                ACT(bt[:, 0:W], psum[:, bg:bg + W], AF.Silu, pg, [bb])
                TT("dve", actT[:, ci, 0:W], psum[:, bu:bu + W], bt[:, 0:W], ALU.mult, pu + [bb], [BactT[ci]])
            w_release(sfg, sfu)
        def ple_prep(ti, tt, col0, R):
            par = ti % 2
            src = pp_d[st * TP + tt * 128: st * TP + tt * 128 + 128, :] if tt < 4 else ps_d
            bt, bb = big_alloc()
            DMA("sp", bt[0:R, 0:256], src, [], [bb])
            CP("dve", pb16[par][0:R, :], bt[0:R, 0:256], [bb], [Bpb16[par]])
            base, pbs = ps_alloc(1)
            pv = psum[:, base:base + 512].bitcast(BF16).rearrange("p (k c) -> p k c", k=8)
            TR([(pv[:, kc, 0:R], pb16[par][0:R, kc * 128:(kc + 1) * 128], ident[0:R, 0:R]) for kc in range(2)],
               [Bpb16[par], B["ident"]], pbs)
            ACT(pT[:, :, col0:col0 + R], pv[:, 0:2, 0:R], AF.Copy, pbs, [BpT[tt]])
            fm_tile(ti, tt, col0, R, None, None, norm=False)

        for idx, (tt, col0, R) in enumerate(tiles):
            b_, p_ = ps_alloc(2)
            lst = []
            for half in range(2):
                for kc in range(NFC):
                    lst.append((psum[0:R, b_ + half * 512:b_ + (half + 1) * 512], actT[:, kc, col0:col0 + R],
                                wdn[:, kc, half * 512:(half + 1) * 512], kc == 0, kc == NFC - 1))
            MM(lst, BactT + Bwdn, p_)
            norm_resid(b_, p_, R, tt, wb_ffn, Bwb_ffn)
            if idx >= LAG:
                ple_prep(idx - LAG, *tiles[idx - LAG])
        for i_ in range(max(0, len(tiles) - LAG), len(tiles)):
            ple_prep(i_, *tiles[i_])

        spp = w_acquire("pp")
        spg = [w_acquire("pg") for _ in range(4)]
        wppv = wslot[spp][:, :, :].rearrange("p (a b) n -> p a (b n)", a=2)
        for idx, (tt, col0, R) in enumerate(tiles):
            bp, ppp = ps_alloc(2)
            lst = []
            for half in range(2):
                for kc in range(2):
                    lst.append((psum[0:R, bp + half * 512:bp + (half + 1) * 512], pT[:, kc, col0:col0 + R],
                                wppv[:, kc, half * 512:(half + 1) * 512], kc == 0, kc == 1))
            MM(lst, [BpT[tt], Bws[spp]], ppp)
            bg, pg = ps_alloc(2)
            lst = []
            for i, s_ in enumerate(spg):
                for kc in range(8):
                    lst.append((psum[0:R, bg + i * SLOTW:bg + (i + 1) * SLOTW], hT[:, kc, col0:col0 + R],
                                wslot[s_][:, kc, :], kc == 0, kc == 7))
            MM(lst, [BhT[tt]] + [Bws[s_] for s_ in spg], pg)
            bt, bb = big_alloc()
            ACT(bt[0:R, :], psum[0:R, bg:bg + 1024], AF.Sigmoid, pg, [bb])
            TT("dve", bt[0:R, :], psum[0:R, bp:bp + 1024], bt[0:R, :], ALU.mult, ppp + [bb], [bb])
            sa, sB = stat_alloc()
            ACT(junk[0:R, :], bt[0:R, :], AF.Square, [bb], [Bjunk, sB], accum=sa[0:R, 0:1])
            rstd_from(sa, sB, R, 1024)
            TT("dve", bt[0:R, :], bt[0:R, :], wb_ple[0:R, :], ALU.mult, [bb, Bwb_ple], [bb])
            STT("dve", ht[0:R, tt, :], bt[0:R, :], sa[0:R, 2:3], ht[0:R, tt, :], ALU.mult, ALU.add, [bb, sB, Bht[tt]], [Bht[tt]])
            dst = yp[st * TP + tt * 128: st * TP + tt * 128 + 128, :] if tt < 4 else ys
            DMA("sp", dst, ht[0:R, tt, :], [Bht[tt]], [])
            if not last:
                r0 = (st + 1) * TP + tt * 128
                DMA("sp", ht[:, tt, :], xp[r0:r0 + 128, :], [], [Bht[tt]])
                if idx >= 2 and idx - 2 < 2:
                    fm_prep(idx - 2, tiles[idx - 2][0], 128)
                    preA.add((st + 1, tiles[idx - 2][0]))
        w_release(spp, *spg)

    P.emit(nc, es)
    es.close()
    return nc


_NC_CACHE = {}


def kernel(x_prompt, x_sample, state_conv, state_gla, p_prompt, p_sample,
           w_norm_mix_pre, w_in, w_conv, w_a_out, w_gk, b_gk, w_gla_norm, w_b_out, w_o,
           w_norm_mix_post, w_norm_ffn_pre, w_ffn_gate, w_ffn_up, w_ffn_down, w_norm_ffn_post,
           w_ple_proj, w_ple_gate, w_norm_ple_post):
    f = lambda a: np.ascontiguousarray(np.asarray(a, dtype=np.float32))
    if "nc" not in _NC_CACHE:
        _NC_CACHE["nc"] = build_program()
    nc = _NC_CACHE["nc"]
    shared = {
        "w_pre": f(w_norm_mix_pre[0]), "w_in": f(w_in[0]), "w_conv": f(w_conv[0]), "w_a": f(w_a_out[0]),
        "w_gk": f(w_gk[0]), "b_gk": f(b_gk[0]), "w_gn": f(w_gla_norm[0]), "w_b": f(w_b_out[0]), "w_o": f(w_o[0]),
        "w_mixpost": f(w_norm_mix_post[0]), "w_ffnpre": f(w_norm_ffn_pre[0]), "w_fg": f(w_ffn_gate[0]),
        "w_fu": f(w_ffn_up[0]), "w_fd": f(w_ffn_down[0]), "w_ffnpost": f(w_norm_ffn_post[0]),
        "w_pp": f(w_ple_proj[0]), "w_pg": f(w_ple_gate[0]), "w_plepost": f(w_norm_ple_post[0]),
    }
    x_prompt = np.asarray(x_prompt); x_sample = np.asarray(x_sample)
    state_conv = np.asarray(state_conv); state_gla = np.asarray(state_gla)
    p_prompt = np.asarray(p_prompt); p_sample = np.asarray(p_sample)
    in_maps = []
    for c in range(8):
        m = dict(shared)
        m["xp"] = f(x_prompt[c])
        m["xs"] = f(x_sample[c * NS:(c + 1) * NS, 0, :])
        m["sconv"] = f(state_conv[0, c * NS:(c + 1) * NS])
        m["sgla"] = f(state_gla[0, c * NS:(c + 1) * NS])
        m["pp"] = f(p_prompt[0, c])
        m["psm"] = f(p_sample[0, c * NS:(c + 1) * NS, 0, :])
        in_maps.append(m)
    res = run_bass_kernel_spmd(nc, in_maps, core_ids=list(range(8)))
    r = res.results
    y_prompt = np.stack([r[c]["yp"] for c in range(8)], axis=0).astype(np.float32)
    y_sample = np.concatenate([r[c]["ys"] for c in range(8)], axis=0)[:, None, :].astype(np.float32)
    new_conv_prompt = np.stack([r[c]["ncp"] for c in range(8)], axis=0)[None].astype(np.float32)
    new_gla_prompt = np.stack([r[c]["ngp"] for c in range(8)], axis=0)[None].astype(np.float32)
    new_conv_sample = np.concatenate([r[c]["ncs"] for c in range(8)], axis=0)[None].astype(np.float32)
    new_gla_sample = np.concatenate([r[c]["ngs"] for c in range(8)], axis=0)[None].astype(np.float32)
    return (y_prompt, y_sample, new_conv_prompt, new_gla_prompt, new_conv_sample, new_gla_sample)
```

```python
import contextlib
import numpy as np
import concourse.bass as bass
import concourse.mybir as mybir
from concourse.bass_utils import run_bass_kernel_spmd

F32 = mybir.dt.float32
BF16 = mybir.dt.bfloat16
AF = mybir.ActivationFunctionType
ALU = mybir.AluOpType

ENGS = ("sp", "pe", "act", "dve", "pool")
EPS = 1e-6
NST = 4
TP = 512
NS = 16
DFF = 2816
NFC = DFF // 128
import os
STRICT = os.environ.get("KSTRICT", "0") == "1"


class Buf:
    __slots__ = ("name", "last_w", "readers", "excl")

    def __init__(self, name, excl=False):
        self.name = name
        self.last_w = None
        self.readers = []
        self.excl = excl


class Op:
    __slots__ = ("eng", "fn", "deps", "is_dma", "signal", "sig_val", "sem", "prev_same_sem")

    def __init__(self, eng, fn, is_dma):
        self.eng = eng
        self.fn = fn
        self.is_dma = is_dma
        self.deps = []
        self.signal = is_dma
        self.sig_val = None
        self.sem = None
        self.prev_same_sem = None


class Prog:
    def __init__(self):
        self.ops = {e: [] for e in ENGS}
        self.n_dma_sems = {"sp": 12, "pool": 8}

    def op(self, eng, fn, reads=(), writes=(), dma=False):
        o = Op(eng, fn, dma)
        deps = {}

        def add(w, kind):
            cur = deps.get(id(w))
            if cur is None or kind < cur[1]:
                deps[id(w)] = (w, kind)

        for b in reads:
            if b.last_w is not None:
                add(b.last_w, 0)
            if b.excl:
                for r in b.readers:
                    add(r, 2)
        for b in writes:
            if b.last_w is not None:
                add(b.last_w, 1)
            for r in b.readers:
                add(r, 1)
        for w, kind in deps.values():
            if w is o:
                continue
            if (not w.is_dma) and (not dma) and w.eng == eng:
                if eng == "pe" or kind == 2 or (kind == 1 and not STRICT):
                    continue
            o.deps.append(w)
            w.signal = True
        for b in reads:
            b.readers.append(o)
        for b in writes:
            b.last_w = o
            b.readers = []
        self.ops[eng].append(o)
        return o

    def emit(self, nc, es):
        sems = {e: es.enter_context(nc.semaphore("s_" + e)) for e in ("pe", "act", "dve", "pool")}
        dsems = {q: [es.enter_context(nc.semaphore(f"d_{q}{i}")) for i in range(n)]
                 for q, n in self.n_dma_sems.items()}
        for e in ENGS:
            cnt = 0
            dcnt = 0
            last_on_sem = {}
            for o in self.ops[e]:
                if o.is_dma:
                    pool = dsems[e]
                    k = dcnt % len(pool)
                    o.sem = pool[k]
                    o.sig_val = 16 * (dcnt // len(pool) + 1)
                    o.prev_same_sem = last_on_sem.get(k)
                    last_on_sem[k] = o
                    dcnt += 1
                elif o.signal:
                    cnt += 1
                    o.sem = sems[e]
                    o.sig_val = cnt
        finals = []
        for q in dsems:
            last = {}
            for o in self.ops[q]:
                if o.is_dma:
                    last[id(o.sem)] = o
            finals.extend(last.values())
        block = es.enter_context(nc.Block())

        def run(e, h):
            waited = {}

            def wait(sem, val):
                if waited.get(id(sem), 0) >= val:
                    return
                h.wait_ge(sem, val)
                waited[id(sem)] = val

            for o in self.ops[e]:
                for d in o.deps:
                    wait(d.sem, d.sig_val)
                if o.is_dma and o.prev_same_sem is not None:
                    wait(o.prev_same_sem.sem, o.prev_same_sem.sig_val)
                ins = o.fn(h)
                if o.signal:
                    ins.then_inc(o.sem, 16 if o.is_dma else 1)
            if e == "sp":
                for o in finals:
                    wait(o.sem, o.sig_val)

        @block.sync
        def _(h):
            run("sp", h)

        @block.tensor
        def _(h):
            run("pe", h)

        @block.scalar
        def _(h):
            run("act", h)

        @block.vector
        def _(h):
            run("dve", h)

        @block.gpsimd
        def _(h):
            run("pool", h)


def handoff(src, dst):
    users = []
    for s in src:
        users.extend(s.readers)
        if s.last_w is not None:
            users.append(s.last_w)
    for d in dst:
        d.readers = list(d.readers) + users


SLOTW = 256
NSLOT = 8
LAG = 9


def build_program():
    nc = bass.Bass("TRN2", target_bir_lowering=False)
    P = Prog()
    es = contextlib.ExitStack()

    def din(name, shape):
        return nc.dram_tensor(name, shape, F32, kind="ExternalInput").ap()

    def dout(name, shape):
        return nc.dram_tensor(name, shape, F32, kind="ExternalOutput").ap()

    xp = din("xp", [NST * TP, 1024])
    xs = din("xs", [NS, 1024])
    sconv = din("sconv", [NS, 2, 1024])
    sgla = din("sgla", [NS, 4, 128, 256])
    pp_d = din("pp", [NST * TP, 256])
    ps_d = din("psm", [NS, 256])
    w_pre = din("w_pre", [1024])
    w_in = din("w_in", [1024, 8208])
    w_conv = din("w_conv", [3, 1024])
    w_a = din("w_a", [1024, 1024])
    w_gk = din("w_gk", [16, 512])
    b_gk = din("b_gk", [512])
    w_gn = din("w_gn", [256])
    w_b = din("w_b", [1024, 1024])
    w_o = din("w_o", [1024, 1024])
    w_mixpost = din("w_mixpost", [1024])
    w_ffnpre = din("w_ffnpre", [1024])
    w_fg = din("w_fg", [1024, DFF])
    w_fu = din("w_fu", [1024, DFF])
    w_fd = din("w_fd", [DFF, 1024])
    w_ffnpost = din("w_ffnpost", [1024])
    w_pp = din("w_pp", [256, 1024])
    w_pg = din("w_pg", [1024, 1024])
    w_plepost = din("w_plepost", [1024])
    yp = dout("yp", [NST * TP, 1024])
    ys = dout("ys", [NS, 1024])
    ncp = dout("ncp", [2, 1024])
    ngp = dout("ngp", [4, 128, 256])
    ncs = dout("ncs", [NS, 2, 1024])
    ngs = dout("ngs", [NS, 4, 128, 256])

    def sb(name, shape, dt):
        return es.enter_context(nc.sbuf_tensor(name, shape, dt))

    WM = TP + NS

    ident = sb("ident", [128, 128], BF16)
    identf = sb("identf", [128, 128], F32)
    tri = sb("tri", [128, 128], F32)
    wn_pre = sb("wn_pre", [128, 8], F32)
    wn_ffn = sb("wn_ffn", [128, 8], F32)
    wcv = sb("wcv", [128, 3, 8], F32)
    wgn_bc = sb("wgn_bc", [128, 256], F32)
    wbc = [sb(f"wbc{i}", [128, 1024], F32) for i in range(2)]
    Bwbc = [Buf("wbc0"), Buf("wbc1")]
    wgk = sb("wgk", [32, 512], F32)
    gkl = sb("gkl", [32, WM], F32)
    maskrow = sb("maskrow", [128, NS, 4, NS], BF16)
    uhist = sb("uhist", [128, 8, 2], F32)
    scT = sb("scT", [128, 2, 8, NS], F32)
    utail = sb("utail", [128, 8, 18], F32)
    S = sb("S", [128, 4, 256], F32)
    stat = sb("stat", [128, 16, 4], F32)
    ssg = sb("ssg", [128, 2, 8], F32)
    aS = sb("aS", [128, 4, NS], F32)
    aStmp = sb("aStmp", [128, 4, NS], F32)
    sqk = sb("sqk", [NS, 4], F32)
    QA = sb("QA", [128, 4, NS], BF16)
    BQA = Buf("QA")
    B = {}

    def mk(*names):
        for n in names:
            B[n] = Buf(n)

    mk("ident", "identf", "tri", "wn_pre", "wn_ffn", "wcv", "wgn_bc",
       "wgk", "gkl", "maskrow", "uhist", "scT", "utail", "S", "aS", "aStmp", "ssg0", "ssg1")
    statB = [Buf(f"stat{i}") for i in range(16)]

    ht = sb("ht", [128, 5, 1024], F32)
    hT = sb("hT", [128, 8, WM], BF16)
    Bht = [Buf(f"ht{i}") for i in range(5)]
    BhT = [Buf(f"hT{i}") for i in range(5)]

    regA = sb("regA", [128, 12672], BF16)
    merged_a = regA[:, 0:8448].bitcast(F32).rearrange("p (k c) -> p k c", k=8)
    mergedT = regA[:, 8448:12672].rearrange("p (k c) -> p k c", k=8)
    actT = regA[:, 0:NFC * WM].rearrange("p (k c) -> p k c", k=NFC)
    Bma = [Buf(f"ma{i}") for i in range(8)]
    BmT = [Buf(f"mT{i}") for i in range(8)]
    BactT = [Buf(f"actT{i}") for i in range(NFC)]

    regB = sb("regB", [128, NFC * 1024], BF16)
    o = 0
    goT = regB[:, o:o + 8 * WM].rearrange("p (k c) -> p k c", k=8); o += 8 * WM
    qT = regB[:, o:o + 4 * WM].rearrange("p (k c) -> p k c", k=4); o += 4 * WM
    kT = regB[:, o:o + 4 * WM].rearrange("p (k c) -> p k c", k=4); o += 4 * WM
    v_tok = regB[:, o:o + 5 * 1024].rearrange("p (k c) -> p k c", k=5); o += 5 * 1024
    sg_tok = regB[:, o:o + 5 * 1024].rearrange("p (k c) -> p k c", k=5); o += 5 * 1024
    gtmp = []
    for i in range(5):
        gtmp.append(regB[:, o:o + 512]); o += 512
    qe_a, ke_a, kd_a, scm_a = [t.rearrange("p (h c) -> p h c", h=4) for t in gtmp[0:4]]
    kdt_a = gtmp[4]
    assert o <= NFC * 1024, o
    wdn = regB[:, 0:NFC * 1024].rearrange("p (k c) -> p k c", k=NFC)
    Sbf_t = [sb(f"Sbf{i}", [128, 4, 256], BF16) for i in range(2)]
    Sbf = [t[:, :, :] for t in Sbf_t]
    BgoT = [Buf(f"goT{i}") for i in range(8)]
    BgoTc = [Buf(f"goTc{i}") for i in range(5)]
    Bq, Bk = Buf("qT"), Buf("kT")
    Bv = [Buf(f"v{i}") for i in range(5)]
    Bsg = [Buf(f"sg{i}") for i in range(5)]
    BSbf = [Buf("Sbf0"), Buf("Sbf1")]
    Bqe, Bke, Bkd, Bscm, Bkdt = [Buf(n) for n in ("qe", "ke", "kd", "scm", "kdt")]
    Bwdn = [Buf(f"wdn{i}") for i in range(6)]
    mixB_bufs = BgoT + BgoTc + [Bq, Bk] + Bv + Bsg + [Bqe, Bke, Bkd, Bscm, Bkdt]

    regC = sb("regC", [128, 4608], F32)
    regCc = sb("regCc", [128, 2116], F32)
    cS = regCc[:, 0:528]
    ubuf = [regCc[:, 528:1058], regCc[:, 1058:1588]]
    ycv = regCc[:, 1588:2116]
    e1 = regC[:, 0:512]
    lt = [regC[:, 512:1024], regC[:, 1024:1536]]
    eb_a = regC[:, 1536:2048].rearrange("p (h c) -> p h c", h=4)
    einv_a = regC[:, 2048:2560].rearrange("p (h c) -> p h c", h=4)
    ed_a = regC[:, 2560:3072].rearrange("p (h c) -> p h c", h=4)
    onb = [regC[:, 3072:3328], regC[:, 3328:3584]]
    oS = regC[0:NS, 3584:4608]
    BcS, Bu, Byc = Buf("cS"), [Buf("u0"), Buf("u1")], Buf("yc")
    Be1, Blt = Buf("e1"), [Buf("lt0"), Buf("lt1")]
    Beb, Beinv, Bed = Buf("eb"), Buf("einv"), Buf("ed")
    Bon = [Buf("on0"), Buf("on1")]
    BoS = Buf("oS")
    convC = [BcS, Byc] + Bu
    glaC = [Be1] + Blt + [Beb, Beinv, Bed] + Bon

    NBT = 2
    bigt = [sb(f"bigt{i}", [128, 1024], F32) for i in range(NBT)]
    Bbig = [Buf(f"bigt{i}") for i in range(NBT)]
    junk = sb("junk", [128, 1024], BF16)
    Bjunk = Buf("junk")
    SSbf = junk[:, :].rearrange("p (h c) -> p h c", h=4)
    hs = [sb(f"hs{i}", [128, 1024], BF16) for i in range(2)]
    Bhs = [Buf("hs0"), Buf("hs1")]
    ogt, Bogt = hs[0], Bhs[0]
    kS_tok = hs[1][0:NS, 0:512]
    KSj = hs[1][0:NS, 512:1024]
    BkS, BKSj = Buf("kS"), Buf("KSj")
    SS = [sb(f"SS{i}", [128, 4, 256], F32) for i in range(2)]
    BSS = [Buf("SS0"), Buf("SS1")]
    QmJ = [sb(f"QmJ{i}", [128, 4, NS], BF16) for i in range(2)]
    BQmJ = [Buf("QmJ0"), Buf("QmJ1")]
    pb16 = [sb(f"pb16{i}", [128, 256], BF16) for i in range(2)]
    Bpb16 = [Buf("pb160"), Buf("pb161")]
    pT = sb("pT", [128, 2, WM], BF16)
    BpT = [Buf(f"pT{i}") for i in range(5)]

    wslot = [sb(f"wslot{i}", [128, 8, SLOTW], BF16) for i in range(NSLOT)]
    Bws = [Buf(f"wslot{i}") for i in range(NSLOT)]

    psum = es.enter_context(nc.psum_tensor("psum", [128, 4096], F32))
    Bps = [Buf(f"psb{i}", excl=True) for i in range(8)]
    ps_state = {"next": 0, "lo": 0, "hi": 8}

    def ps_alloc(n):
        p = ps_state["next"]
        lo, hi = ps_state["lo"], ps_state["hi"]
        if p < lo or p >= hi:
            p = lo
        if n == 2 and (p % 2):
            p += 1
        if p + n > hi:
            p = lo
        ps_state["next"] = p + n
        return p * 512, Bps[p:p + n]

    def G(b, n):
        return b * 512, Bps[b:b + n]

    big_state = {"next": 0}

    def big_alloc():
        i = big_state["next"]
        big_state["next"] = (i + 1) % NBT
        return bigt[i], Bbig[i]

    stat_state = {"next": 0}

    def stat_alloc():
        i = stat_state["next"]
        stat_state["next"] = (i + 1) % 16
        return stat[:, i, :], statB[i]

    def ACT(out, in_, func, reads, writes, scale=None, bias=None, accum=None):
        kw = {}
        if scale is not None:
            kw["scale"] = scale
        if bias is not None:
            kw["bias"] = bias
        if accum is not None:
            kw["accum_out"] = accum
        P.op("act", lambda h: h.activation(out=out, in_=in_, func=func, **kw), reads, writes)

    def TT(eng, out, in0, in1, op, reads, writes):
        P.op(eng, lambda h: h.tensor_tensor(out=out, in0=in0, in1=in1, op=op), reads, writes)

    def TS(eng, out, in0, s1, op0, reads, writes):
        P.op(eng, lambda h: h.tensor_scalar(out=out, in0=in0, scalar1=s1, scalar2=None, op0=op0), reads, writes)

    def STT(eng, out, in0, scalar, in1, op0, op1, reads, writes):
        P.op(eng, lambda h: h.scalar_tensor_tensor(out=out, in0=in0, scalar=scalar, in1=in1, op0=op0, op1=op1), reads, writes)

    def CP(eng, out, in_, reads, writes):
        P.op(eng, lambda h: h.tensor_copy(out=out, in_=in_), reads, writes)

    def MS(eng, ap, val, writes):
        P.op(eng, lambda h: h.memset(ap, val), (), writes)

    def MM(lst, reads, writes):
        def fn(h):
            ins = None
            for (out, lhsT, rhs, start, stop) in lst:
                ins = h.matmul(out, lhsT=lhsT, rhs=rhs, start=start, stop=stop)
            return ins
        P.op("pe", fn, reads, writes)

    def TR(lst, reads, writes):
        def fn(h):
            ins = None
            for (out, in_, idn) in lst:
                ins = h.transpose(out=out, in_=in_, identity=idn)
            return ins
        P.op("pe", fn, reads, writes)

    def DMA(q, out, in_, reads, writes, noncontig=False):
        def fn(h):
            if noncontig:
                with nc.allow_non_contiguous_dma(reason="small strided constant load"):
                    return h.dma_start(out=out, in_=in_)
            return h.dma_start(out=out, in_=in_)
        P.op(q, fn, reads, writes, dma=True)

    MS("pool", identf[:], 1.0, [B["identf"]])
    P.op("pool", lambda h: h.affine_select(out=identf[:], in_=identf[:], pattern=[[-1, 128]], compare_op=ALU.is_equal,
                                           fill=0.0, base=0, channel_multiplier=1), [B["identf"]], [B["identf"]])
    CP("dve", ident[:], identf[:], [B["identf"]], [B["ident"]])
    MS("pool", tri[:], 1.0, [B["tri"]])
    P.op("pool", lambda h: h.affine_select(out=tri[:], in_=tri[:], pattern=[[1, 128]], compare_op=ALU.is_ge,
                                           fill=0.0, base=0, channel_multiplier=-1), [B["tri"]], [B["tri"]])
    DMA("sp", wn_pre[:], w_pre.rearrange("(kc p) -> p kc", p=128), [], [B["wn_pre"]], noncontig=True)
    DMA("sp", wn_ffn[:], w_ffnpre.rearrange("(kc p) -> p kc", p=128), [], [B["wn_ffn"]], noncontig=True)
    DMA("sp", wcv[:], w_conv.rearrange("j (kc p) -> p j kc", p=128), [], [B["wcv"]], noncontig=True)
    DMA("sp", wgn_bc[:], w_gn.partition_broadcast(128), [], [B["wgn_bc"]])
    DMA("sp", wgk[0:16, :], w_gk, [], [B["wgk"]])
    DMA("sp", wgk[16:17, :], b_gk.rearrange("(o n) -> o n", o=1), [], [B["wgk"]])
    MS("dve", gkl[:], 1.0, [B["gkl"]])
    MS("dve", uhist[:], 0.0, [B["uhist"]])
    MS("dve", S[:], 0.0, [B["S"]])
    MS("dve", Sbf[0], 0.0, [BSbf[0]])
    MS("dve", stat[:], 1.0, statB)
    MS("dve", maskrow[:], 0.0, [B["maskrow"]])
    for j in range(NS):
        MS("dve", maskrow[:, j, :, j:j + 1], 1.0, [B["maskrow"]])
    for t in range(2):
        bt, bb = big_alloc()
        DMA("sp", bt[0:NS, :], sconv[:, t, :], [], [bb])
        if t == 1:
            DMA("sp", ncs[:, 0, :], bt[0:NS, :], [bb], [])
        base, pbs = ps_alloc(1)
        TR([(psum[:, base + kc * NS:base + (kc + 1) * NS], bt[0:NS, kc * 128:(kc + 1) * 128], identf[0:NS, 0:NS])
            for kc in range(8)], [bb, B["identf"]], pbs)
        ACT(scT[:, t, :, :], psum[:, base:base + 8 * NS].rearrange("p (k c) -> p k c", k=8), AF.Copy, pbs, [B["scT"]])

    wbc_state = {"i": 0}

    def wbc_load(src):
        i = wbc_state["i"]
        wbc_state["i"] = 1 - i
        DMA("sp", wbc[i][:], src.partition_broadcast(128), [], [Bwbc[i]])
        return wbc[i], Bwbc[i]

    def build_groups():
        g = []

        def blk(name, mat, c0):
            g.append((name, mat, 0, 4, c0, 512))
            g.append((name, mat, 4, 4, c0, 512))

        for st in range(NST):
            blk("q", w_in, 3072)
            blk("k", w_in, 3584)
            g.append(("gklr", w_in, 0, 8, 6144, 16))
            for i in range(2):
                blk("v", w_in, 4096 + i * 512)
            for i in range(2):
                blk("g", w_in, 5120 + i * 512)
            for qd in range(4):
                g.append(("c", w_in, 0, 8, 1024 + qd * SLOTW, SLOTW))
                g.append(("x", w_in, 0, 8, 2048 + qd * SLOTW, SLOTW))
                g.append(("b", w_in, 0, 8, 0 + qd * SLOTW, SLOTW))
            for half in range(2):
                blk("wa", w_a, half * 512)
                blk("ga", w_in, 6160 + half * 512)
            for half in range(2):
                blk("wb", w_b, half * 512)
                blk("gb", w_in, 7184 + half * 512)
            for half in range(2):
                blk("wo", w_o, half * 512)
            wd = 0
            for gi in range(6):
                if gi < 5:
                    blk("fg", w_fg, gi * 512)
                    blk("fu", w_fu, gi * 512)
                else:
                    g.append(("fg", w_fg, 0, 8, 2560, 256))
                    g.append(("fu", w_fu, 0, 8, 2560, 256))
                k0 = (wd // 2) * 8
                nk = min(8, NFC - k0)
                g.append(("WD", w_fd, k0, nk, (wd % 2) * 512, 512))
                wd += 1
            g.append(("pp", w_pp, 0, 2, 0, 1024))
            for half in range(2):
                blk("pg", w_pg, half * 512)
        return g

    groups = build_groups()
    GPS = len(groups) // NST
    ring_idx = {}
    wd_idx = {}
    for li in range(GPS):
        if groups[li][0] == "WD":
            wd_idx[li] = len(wd_idx)
        else:
            ring_idx[li] = len(ring_idx)
    wscr = nc.dram_tensor("wscr", [len(ring_idx), 128, 8 * SLOTW], BF16).ap()
    wdscr = nc.dram_tensor("wdscr", [6, 128, 8 * 512], BF16).ap()
    Bscr = [Buf(f"scr{i}") for i in range(len(ring_idx))]
    Bwdscr = [Buf(f"wdscr{i}") for i in range(6)]
    WS = {"next_load": 0, "next_use": 0, "free": list(range(NSLOT)), "slot_of": {}, "wd_ok": False, "wd_idx": 0}

    def w_prefetch():
        while WS["next_load"] < len(groups):
            gi = WS["next_load"]
            name, mat, k0, nk, c0, nco = groups[gi]
            st_, li = gi // GPS, gi % GPS
            src = mat[k0 * 128:(k0 + nk) * 128, c0:c0 + nco].rearrange("(kc p) n -> p kc n", p=128)
            if name == "WD":
                if not WS["wd_ok"]:
                    return
                piece = wd_idx[li]
                dst = wdn[:, k0:k0 + nk, c0:c0 + nco]
                scr = wdscr[piece][:, 0:nk * 512].rearrange("p (k c) -> p k c", k=nk)
                conv_st = piece % 2
                if st_ <= conv_st:
                    DMA("pool", dst, src, [], [Bwdn[piece]])
                    if st_ == conv_st:
                        DMA("sp", scr, dst, [Bwdn[piece]], [Bwdscr[piece]])
                else:
                    DMA("pool", dst, scr, [Bwdscr[piece]], [Bwdn[piece]])
                WS["next_load"] += 1
                continue
            if not WS["free"]:
                return
            s = WS["free"].pop(0)
            ri = ring_idx[li]
            img = wslot[s][:, :, :].rearrange("p k c -> p (k c)")
            conv_st = ri % 2
            if st_ <= conv_st:
                dst = img.rearrange("p (k c) -> p k c", k=nk)[:, :, 0:nco]
                if name == "gklr":
                    MS("pool", wslot[s][:, :, :], 0.0, [Bws[s]])
                DMA("pool", dst, src, [], [Bws[s]])
                if st_ == conv_st:
                    DMA("sp", wscr[ri], img, [Bws[s]], [Bscr[ri]])
            else:
                DMA("pool", img, wscr[ri], [Bscr[ri]], [Bws[s]])
            WS["slot_of"][gi] = s
            WS["next_load"] += 1

    def w_acquire(name):
        while groups[WS["next_use"]][0] == "WD":
            WS["next_use"] += 1
        gi = WS["next_use"]
        assert groups[gi][0] == name, (groups[gi][0], name)
        if gi not in WS["slot_of"]:
            w_prefetch()
        assert gi in WS["slot_of"], ("weight ring deadlock", name, gi)
        WS["next_use"] += 1
        return WS["slot_of"][gi]

    class Grp:
        def __init__(self, s_):
            self.s = (s_,)
            self.B = [Bws[s_]]

        def lhsT(self, kc, j, M=128):
            return wslot[self.s[0]][:, kc, j * 128:j * 128 + M]

    class Blk:
        def __init__(self, sA, sB):
            self.s = (sA, sB)
            self.B = [Bws[sA], Bws[sB]]
            self.v = [wslot[x][:, :, :].rearrange("p k c -> p (k c)").rearrange("p (k c) -> p k c", k=4) for x in (sA, sB)]

        def lhsT(self, kc, j, M=128):
            return self.v[kc // 4][:, kc % 4, j * 128:j * 128 + M]

        def rhs(self, kc):
            return self.v[kc // 4][:, kc % 4, :]

    def acquire_blk(name):
        sA = w_acquire(name)
        sB = w_acquire(name)
        return Blk(sA, sB)

    def w_release(*slots):
        for s in slots:
            WS["free"].append(s)
        w_prefetch()

    def mm_fm(base, pbs, wobj, j, src, srcB, W, M=128):
        lst = []
        blocks = [(0, TP)] + ([(TP, W - TP)] if W > TP else [])
        for (c0, cn) in blocks:
            for kc in range(8):
                lst.append((psum[0:M, base + c0:base + c0 + cn], wobj.lhsT(kc, j, M),
                            src[:, kc, c0:c0 + cn], kc == 0, kc == 7))
        MM(lst, wobj.B + srcB, pbs)

    def rstd_from(ssap, sB, R, n):
        ACT(ssap[0:R, 1:2], ssap[0:R, 0:1], AF.Ln, [sB], [sB], scale=1.0 / n, bias=EPS)
        ACT(ssap[0:R, 2:3], ssap[0:R, 1:2], AF.Exp, [sB], [sB], scale=-0.5)

    def fm_tile(ti, tt, col0, R, wn, wnB, norm=True):
        par = ti % 2
        if norm:
            sa, sB = stat_alloc()
            ACT(junk[0:R, :], ht[0:R, tt, :], AF.Square, [Bht[tt]], [Bjunk, sB], accum=sa[0:R, 0:1])
            rstd_from(sa, sB, R, 1024)
            TS("dve", hs[par][0:R, :], ht[0:R, tt, :], sa[0:R, 2:3], ALU.mult, [Bht[tt], sB], [Bhs[par]])
        else:
            CP("dve", hs[par][0:R, :], ht[0:R, tt, :], [Bht[tt]], [Bhs[par]])
        base, pbs = ps_alloc(1)
        pv = psum[:, base:base + 512].bitcast(BF16).rearrange("p (k c) -> p k c", k=8)
        TR([(pv[:, kc, 0:R], hs[par][0:R, kc * 128:(kc + 1) * 128], ident[0:R, 0:R]) for kc in range(8)],
           [Bhs[par], B["ident"]], pbs)
        if wn is not None:
            TT("dve", hT[:, :, col0:col0 + R], pv[:, :, 0:R], wn[:, :].unsqueeze(2).to_broadcast([128, 8, R]),
               ALU.mult, pbs + [wnB], [BhT[tt]])
        else:
            ACT(hT[:, :, col0:col0 + R], pv[:, :, 0:R], AF.Copy, pbs, [BhT[tt]])

    def norm_resid(base, pbs, R, tt, wb_, wbB):
        src = psum[0:R, base:base + 1024]
        sa, sB = stat_alloc()
        ACT(junk[0:R, :], src, AF.Square, pbs, [Bjunk, sB], accum=sa[0:R, 0:1])
        rstd_from(sa, sB, R, 1024)
        bt, bb = big_alloc()
        TT("dve", bt[0:R, :], src, wb_[0:R, :], ALU.mult, pbs + [wbB], [bb])
        STT("dve", ht[0:R, tt, :], bt[0:R, :], sa[0:R, 2:3], ht[0:R, tt, :], ALU.mult, ALU.add, [bb, sB, Bht[tt]], [Bht[tt]])

    preA = set()
    for st in range(NST):
        last = st == NST - 1
        W = WM if last else TP
        nbW = 2 if last else 1
        tiles = [(tt, tt * 128, 128) for tt in range(4)] + ([(4, TP, NS)] if last else [])
        allhT = [BhT[t[0]] for t in tiles]

        if st > 0:
            handoff(Bwdn, mixB_bufs)
            handoff(BactT, Bma + BmT)
            WS["wd_ok"] = False

        if st == 0:
            for tt in range(4):
                DMA("sp", ht[:, tt, :], xp[tt * 128:(tt + 1) * 128, :], [], [Bht[tt]])
            DMA("sp", ht[0:NS, 4, :], xs, [], [Bht[4]])
        w_prefetch()
        for ti, (tt, col0, R) in enumerate(tiles):
            if (st, tt) not in preA:
                fm_tile(ti, tt, col0, R, wn_pre, B["wn_pre"])
        wb_mix, Bwb_mix = wbc_load(w_mixpost)

        for (nm, dstT, dB, scl) in (("q", qT, Bq, float(128 ** -0.5)), ("k", kT, Bk, None)):
            blk_ = acquire_blk(nm)
            for h_ in range(4):
                b_, p_ = ps_alloc(nbW)
                mm_fm(b_, p_, blk_, h_, hT, allhT, W)
                ACT(dstT[:, h_, 0:W], psum[:, b_:b_ + W], AF.Copy, p_, [dB], scale=scl)
            w_release(*blk_.s)
        sg_ = w_acquire("gklr")
        b_, p_ = ps_alloc(nbW)
        mm_fm(b_, p_, Grp(sg_), 0, hT, allhT, W, M=16)
        ACT(gkl[0:16, 0:W], psum[0:16, b_:b_ + W], AF.Copy, p_, [B["gkl"]])
        w_release(sg_)
        for (nm, dstt, dB, fn) in (("v", v_tok, Bv, AF.Copy), ("g", sg_tok, Bsg, AF.Silu)):
            for half in range(2):
                blk_ = acquire_blk(nm)
                for (tt, col0, R) in tiles:
                    b_, p_ = ps_alloc(1)
                    MM([(psum[0:R, b_:b_ + 512], hT[:, kc, col0:col0 + R], blk_.rhs(kc), kc == 0, kc == 7)
                        for kc in range(8)], [BhT[tt]] + blk_.B, p_)
                    ACT(dstt[0:R, tt, half * 512:(half + 1) * 512], psum[0:R, b_:b_ + 512], fn, p_, [dB[tt]])
                w_release(*blk_.s)


        def gla_post(src_h, srcB, R, tt, col0, par):
            sg_ap = ssg[:, par, :]
            sgB = B[f"ssg{par}"]
            for h_ in range(4):
                ACT(junk[0:R, 0:256], src_h(h_), AF.Square, srcB, [Bjunk, sgB], accum=sg_ap[0:R, h_:h_ + 1])
            ACT(sg_ap[0:R, 4:8], sg_ap[0:R, 0:4], AF.Ln, [sgB], [sgB], scale=1.0 / 256, bias=EPS)
            ACT(sg_ap[0:R, 4:8], sg_ap[0:R, 4:8], AF.Exp, [sgB], [sgB], scale=-0.5)
            for h_ in range(4):
                hp = h_ % 2
                STT("dve", onb[hp][0:R, :], src_h(h_), sg_ap[0:R, 4 + h_:5 + h_],
                    wgn_bc[0:R, :], ALU.mult, ALU.mult, srcB + [sgB, B["wgn_bc"]], [Bon[hp]])
                TT("dve", ogt[0:R, h_ * 256:(h_ + 1) * 256], onb[hp][0:R, :], sg_tok[0:R, tt, h_ * 256:(h_ + 1) * 256],
                   ALU.mult, [Bon[hp], Bsg[tt]], [Bogt])
            base, pbs = G(0, 1)
            pv = psum[:, base:base + 512].bitcast(BF16).rearrange("p (k c) -> p k c", k=8)
            TR([(pv[:, kc, 0:R], ogt[0:R, kc * 128:(kc + 1) * 128], ident[0:R, 0:R]) for kc in range(8)],
               [Bogt, B["ident"]], pbs)
            ACT(goT[:, :, col0:col0 + R], pv[:, :, 0:R], AF.Copy, pbs, [BgoTc[tt]] + BgoT)

        def sample_prep():
            bgk, pgk = G(0, 1)
            MM([(psum[:, bgk + h_ * NS:bgk + (h_ + 1) * NS], wgk[0:17, h_ * 128:(h_ + 1) * 128], gkl[0:17, TP:TP + NS], True, True)
                for h_ in range(4)], [B["gkl"], B["wgk"]], pgk)
            pgv = psum[:, bgk:bgk + 4 * NS].rearrange("p (k c) -> p k c", k=4)
            ACT(aStmp[:, :, :], pgv, AF.Exp, pgk, [B["aStmp"]], scale=-1.0)
            ACT(aStmp[:, :, :], aStmp[:, :, :], AF.Ln, [B["aStmp"]], [B["aStmp"]], bias=1.0)
            ACT(aS[:, :, :], aStmp[:, :, :], AF.Exp, [B["aStmp"]], [B["aS"]], scale=-1.0 / 16)
            TT("dve", QA[:, :, :], qT[:, :, TP:TP + NS], aS[:, :, :], ALU.mult, [Bq, B["aS"]], [BQA])
            bkt, pkt = G(1, 1)
            pktv = psum[:, bkt:bkt + 512].bitcast(BF16)
            TR([(pktv[0:NS, h_ * 128:(h_ + 1) * 128], kT[:, h_, TP:TP + NS], ident[:, :]) for h_ in range(4)] +
               [(pktv[0:NS, 512 + h_ * 128:512 + (h_ + 1) * 128], qT[:, h_, TP:TP + NS], ident[:, :]) for h_ in range(4)],
               [Bk, Bq, B["ident"]], pkt)
            ACT(kS_tok, pktv[0:NS, 0:512], AF.Copy, pkt, [BkS])
            bt, bb = big_alloc()
            TT("dve", bt[0:NS, 0:512], pktv[0:NS, 512:1024], kS_tok, ALU.mult, pkt + [BkS], [bb])
            for h_ in range(4):
                ACT(junk[0:NS, 0:128], bt[0:NS, h_ * 128:(h_ + 1) * 128], AF.Copy, [bb], [Bjunk, B["aStmp"]],
                    accum=sqk[0:NS, h_:h_ + 1])
            for h_ in range(4):
                TS("dve", oS[:, h_ * 256:(h_ + 1) * 256], v_tok[0:NS, 4, h_ * 256:(h_ + 1) * 256], sqk[0:NS, h_:h_ + 1],
                   ALU.mult, [Bv[4], B["aStmp"]], [BoS])

        def sample_load(j):
            DMA("sp", SS[j % 2][:, :, :], sgla[j].rearrange("h k v -> k h v"), [], [BSS[j % 2]])

        def sample_iter(j):
            jp = j % 2
            if j == 0:
                sample_load(0)
            if j + 1 < NS:
                sample_load(j + 1)
            ACT(SSbf, SS[jp][:, :, :], AF.Copy, [BSS[jp]], [Bjunk])
            TS("dve", KSj, kS_tok, identf[0:NS, j:j + 1], ALU.mult, [BkS, B["identf"]], [BKSj])
            TT("dve", QmJ[jp][:, :, :], QA[:, :, :], maskrow[:, j, :, :], ALU.mult, [BQA, B["maskrow"]], [BQmJ[jp]])
            bkv, pkv = G(0, 2)
            MM([(psum[:, bkv + h_ * 256:bkv + (h_ + 1) * 256], KSj[:, h_ * 128:(h_ + 1) * 128],
                 v_tok[0:NS, 4, h_ * 256:(h_ + 1) * 256], True, True) for h_ in range(4)], [BKSj, Bv[4]], pkv)
            bo, pbo = G(2, 2)
            MM([(psum[0:NS, bo + h_ * 256:bo + (h_ + 1) * 256], QmJ[jp][:, h_, :], SSbf[:, h_, :], True, True)
                for h_ in range(4)], [BQmJ[jp], Bjunk], pbo)
            for h_ in range(4):
                STT("dve", SS[jp][:, h_, :], SS[jp][:, h_, :], aS[:, h_, j:j + 1], psum[:, bkv + h_ * 256:bkv + (h_ + 1) * 256],
                    ALU.mult, ALU.add, [BSS[jp], B["aS"]] + pkv, [BSS[jp]])
            DMA("sp", ngs[j].rearrange("h k v -> k h v"), SS[jp][:, :, :], [BSS[jp]], [])
            TT("dve", oS, oS, psum[0:NS, bo:bo + 1024], ALU.add, pbo + [BoS], [BoS])

        def conv_gen():
            for qd in range(4):
                sc = Grp(w_acquire("c"))
                sx = Grp(w_acquire("x"))
                sbk = Grp(w_acquire("b"))
                for j in range(2):
                    kc = qd * 2 + j
                    par = kc % 2
                    u = ubuf[par]
                    bc, pc = ps_alloc(nbW)
                    mm_fm(bc, pc, sc, j, hT, allhT, W)
                    ACT(cS[:, 0:W], psum[:, bc:bc + W], AF.Copy, pc, [BcS])
                    bx, px = ps_alloc(nbW)
                    mm_fm(bx, px, sx, j, hT, allhT, W)
                    TT("dve", u[:, 2:2 + W], psum[:, bx:bx + W], cS[:, 0:W], ALU.mult, px + [BcS], [Bu[par]])
                    CP("dve", u[:, 0:2], uhist[:, kc, :], [B["uhist"]], [Bu[par]])
                    CP("dve", uhist[:, kc, :], u[:, TP:TP + 2], [Bu[par]], [B["uhist"]])
                    TS("dve", ycv[:, 0:TP], u[:, 0:TP], wcv[:, 0, kc:kc + 1], ALU.mult, [Bu[par], B["wcv"]], [Byc])
                    STT("dve", ycv[:, 0:TP], u[:, 1:TP + 1], wcv[:, 1, kc:kc + 1], ycv[:, 0:TP], ALU.mult, ALU.add,
                        [Bu[par], B["wcv"], Byc], [Byc])
                    STT("dve", ycv[:, 0:TP], u[:, 2:TP + 2], wcv[:, 2, kc:kc + 1], ycv[:, 0:TP], ALU.mult, ALU.add,
                        [Bu[par], B["wcv"], Byc], [Byc])
                    if last:
                        TS("dve", ycv[:, TP:W], scT[:, 0, kc, :], wcv[:, 0, kc:kc + 1], ALU.mult, [B["scT"], B["wcv"]], [Byc])
                        STT("dve", ycv[:, TP:W], scT[:, 1, kc, :], wcv[:, 1, kc:kc + 1], ycv[:, TP:W], ALU.mult, ALU.add,
                            [B["scT"], B["wcv"], Byc], [Byc])
                        STT("dve", ycv[:, TP:W], u[:, TP + 2:W + 2], wcv[:, 2, kc:kc + 1], ycv[:, TP:W], ALU.mult, ALU.add,
                            [Bu[par], B["wcv"], Byc], [Byc])
                        CP("dve", utail[:, kc, :], u[:, TP:TP + 18], [Bu[par]], [B["utail"]])
                    bb_, pb_ = ps_alloc(nbW)
                    mm_fm(bb_, pb_, sbk, j, hT, allhT, W)
                    TT("dve", mergedT[:, kc, 0:W], psum[:, bb_:bb_ + W], ycv[:, 0:W], ALU.mult, pb_ + [Byc], [BmT[kc]])
                    yield
                w_release(sc.s[0], sx.s[0], sbk.s[0])
            if last:
                base, pbs = ps_alloc(2)
                TR([(psum[0:18, base + kc * 128:base + (kc + 1) * 128], utail[:, kc, :], identf[:, :]) for kc in range(8)],
                   [B["utail"], B["identf"]], pbs)
                bt, bb = big_alloc()
                ACT(bt[0:18, :], psum[0:18, base:base + 1024], AF.Copy, pbs, [bb])
                DMA("sp", ncp[:, :], bt[0:2, :], [bb], [])
                DMA("sp", ncs[:, 1, :], bt[2:18, :], [bb], [])

            for half in range(2):
                swa = acquire_blk("wa")
                sga = acquire_blk("ga")
                for j in range(4):
                    kc = half * 4 + j
                    by, py = ps_alloc(nbW)
                    mm_fm(by, py, swa, j, mergedT, BmT, W)
                    bg, pg = ps_alloc(nbW)
                    mm_fm(bg, pg, sga, j, hT, allhT, W)
                    bt, bb = big_alloc()
                    ACT(bt[:, 0:W], psum[:, bg:bg + W], AF.Tanh, pg, [bb], scale=0.5)
                    ACT(bt[:, 0:W], bt[:, 0:W], AF.Identity, [bb], [bb], scale=0.5, bias=0.5)
                    TT("dve", merged_a[:, kc, 0:W], psum[:, by:by + W], bt[:, 0:W], ALU.mult, py + [bb], [Bma[kc]])
                    yield
                w_release(*swa.s, *sga.s)


        def gla_gen():
            if last:
                sample_prep()
                yield
            def stage1(c):
                par_ = (st * 4 + c) % 2
                cols_ = slice(c * 128, (c + 1) * 128)
                bgk, pgk = G(0, 1)
                MM([(psum[:, bgk + q_ * 128:bgk + (q_ + 1) * 128], gkl[0:17, cols_], wgk[0:17, q_ * 128:(q_ + 1) * 128], True, True)
                    for q_ in range(4)], [B["gkl"], B["wgk"]], pgk)
                ACT(e1, psum[:, bgk:bgk + 512], AF.Exp, pgk, [Be1], scale=-1.0)
                ACT(lt[par_], e1, AF.Ln, [Be1], [Blt[par_]], bias=1.0)

            stage1(0)
            yield
            for c in range(4):
                cidx = st * 4 + c
                par = cidx % 2
                cols = slice(c * 128, (c + 1) * 128)
                bpb, ppb = G(1, 1)
                MM([(psum[:, bpb + h_ * 128:bpb + (h_ + 1) * 128], lt[par][:, h_ * 128:(h_ + 1) * 128], tri[:, :], True, True)
                    for h_ in range(4)], [Blt[par], B["tri"]], ppb)
                pbv = psum[:, bpb:bpb + 512].rearrange("p (h c) -> p h c", h=4)
                ACT(eb_a, pbv, AF.Exp, ppb, [Beb], scale=-1.0 / 16)
                ACT(einv_a, pbv, AF.Exp, ppb, [Beinv], scale=1.0 / 16)
                TT("dve", ed_a, einv_a, eb_a[:, :, 127:128].to_broadcast([128, 4, 128]), ALU.mult, [Beinv, Beb], [Bed])
                TT("dve", qe_a, qT[:, :, cols], eb_a, ALU.mult, [Bq, Beb], [Bqe])
                TT("dve", ke_a, kT[:, :, cols], einv_a, ALU.mult, [Bk, Beinv], [Bke])
                TT("dve", kd_a, kT[:, :, cols], ed_a, ALU.mult, [Bk, Bed], [Bkd])
                if c + 1 < 4:
                    stage1(c + 1)
                yield
                bkd, pkd = G(0, 1)
                pkdv = psum[:, bkd:bkd + 512].bitcast(BF16)
                TR([(pkdv[:, h_ * 128:(h_ + 1) * 128], kd_a[:, h_, :], ident[:, :]) for h_ in range(4)], [Bkd, B["ident"]], pkd)
                ACT(kdt_a, pkdv[:, 0:512], AF.Copy, pkd, [Bkdt])
                bsc, psc_ = G(1, 1)
                MM([(psum[:, bsc + h_ * 128:bsc + (h_ + 1) * 128], ke_a[:, h_, :], qe_a[:, h_, :], True, True) for h_ in range(4)],
                   [Bke, Bqe], psc_)
                TT("dve", scm_a, psum[:, bsc:bsc + 512].rearrange("p (h c) -> p h c", h=4),
                   tri[:, :].unsqueeze(1).to_broadcast([128, 4, 128]), ALU.mult, psc_ + [B["tri"]], [Bscm])
                yield
                po, ppo = G(2, 2)
                lst = []
                for h_ in range(4):
                    vh = v_tok[:, c, h_ * 256:(h_ + 1) * 256]
                    lst.append((psum[:, po + h_ * 256:po + (h_ + 1) * 256], scm_a[:, h_, :], vh, True, False))
                    lst.append((psum[:, po + h_ * 256:po + (h_ + 1) * 256], qe_a[:, h_, :], Sbf[par][:, h_, :], False, True))
                MM(lst, [Bscm, Bv[c], Bqe, BSbf[par]], ppo)
                bD, pD_ = G(0, 2)
                MM([(psum[:, bD + h_ * 256:bD + (h_ + 1) * 256], kdt_a[:, h_ * 128:(h_ + 1) * 128],
                     v_tok[:, c, h_ * 256:(h_ + 1) * 256], True, True) for h_ in range(4)], [Bkdt, Bv[c]], pD_)
                for h_ in range(4):
                    STT("dve", S[:, h_, :], S[:, h_, :], eb_a[:, h_, 127:128], psum[:, bD + h_ * 256:bD + (h_ + 1) * 256],
                        ALU.mult, ALU.add, [B["S"], Beb] + pD_, [B["S"]])
                ACT(Sbf[1 - par], S[:, :, :], AF.Copy, [B["S"]], [BSbf[1 - par]])
                yield
                gla_post(lambda h_, po=po: psum[:, po + h_ * 256:po + (h_ + 1) * 256], ppo, 128, c, c * 128, par)
                yield
                if last:
                    for j in range(c * 4, c * 4 + 4):
                        sample_iter(j)
                        yield
            if last:
                DMA("sp", ngp.rearrange("h k v -> k h v"), S[:, :, :], [B["S"]], [])
                gla_post(lambda h_: oS[:, h_ * 256:(h_ + 1) * 256], [BoS], NS, 4, TP, 0)


        ps_state["lo"], ps_state["hi"] = 4, 8
        g_gla, g_conv = gla_gen(), conv_gen()
        n_gla = 17 + (17 if last else 0)
        n_conv = 16
        done_g = 0
        for i_ in range(n_conv):
            while done_g * n_conv < (i_ + 1) * n_gla:
                if next(g_gla, "end") == "end":
                    break
                done_g += 1
            next(g_conv, None)
        for _ in g_gla:
            pass
        for _ in g_conv:
            pass
        ps_state["lo"], ps_state["hi"] = 0, 8

        for half in range(2):
            swb = acquire_blk("wb")
            sgb = acquire_blk("gb")
            for j in range(4):
                kc = half * 4 + j
                by, py = ps_alloc(nbW)
                mm_fm(by, py, swb, j, goT, BgoT + BgoTc[0:len(tiles)], W)
                bg, pg = ps_alloc(nbW)
                mm_fm(bg, pg, sgb, j, hT, allhT, W)
                bt, bb = big_alloc()
                ACT(bt[:, 0:W], psum[:, bg:bg + W], AF.Sigmoid, pg, [bb])
                TT("dve", bt[:, 0:W], psum[:, by:by + W], bt[:, 0:W], ALU.mult, py + [bb], [bb])
                TT("dve", mergedT[:, kc, 0:W], bt[:, 0:W], merged_a[:, kc, 0:W], ALU.add, [bb, Bma[kc]], [BmT[kc]])
            w_release(*swb.s, *sgb.s)
        handoff(mixB_bufs, Bwdn)
        WS["wd_ok"] = True
        wb_ffn, Bwb_ffn = wbc_load(w_ffnpost)

        so = [acquire_blk("wo") for _ in range(2)]
        for idx, (tt, col0, R) in enumerate(tiles):
            b_, p_ = ps_alloc(2)
            lst = []
            for i, blk_ in enumerate(so):
                for kc in range(8):
                    lst.append((psum[0:R, b_ + i * 512:b_ + (i + 1) * 512], mergedT[:, kc, col0:col0 + R],
                                blk_.rhs(kc), kc == 0, kc == 7))
            MM(lst, BmT + so[0].B + so[1].B, p_)
            norm_resid(b_, p_, R, tt, wb_mix, Bwb_mix)
            if idx >= LAG:
                fm_tile(idx - LAG, *tiles[idx - LAG], wn_ffn, B["wn_ffn"])
        for i_ in range(max(0, len(tiles) - LAG), len(tiles)):
            fm_tile(i_, *tiles[i_], wn_ffn, B["wn_ffn"])
        w_release(*so[0].s, *so[1].s)
        handoff(Bma + BmT, BactT)
        wb_ple, Bwb_ple = wbc_load(w_plepost)

        for gi in range(6):
            if gi < 5:
                sfg, sfu, nch = acquire_blk("fg"), acquire_blk("fu"), 4
            else:
                sfg, sfu, nch = Grp(w_acquire("fg")), Grp(w_acquire("fu")), 2
            for j in range(nch):
                ci = gi * 4 + j
                bg, pg = ps_alloc(nbW)
                mm_fm(bg, pg, sfg, j, hT, allhT, W)
                bu, pu = ps_alloc(nbW)
                mm_fm(bu, pu, sfu, j, hT, allhT, W)
                bt, bb = big_alloc()
                ACT(bt[:, 0:W], psum[:, bg:bg + W], AF.Silu, pg, [bb])
                TT("dve", actT[:, ci, 0:W], psum[:, bu:bu + W], bt[:, 0:W], ALU.mult, pu + [bb], [BactT[ci]])
            w_release(*sfg.s, *sfu.s)
        def ple_prep(ti, tt, col0, R):
            par = ti % 2
            src = pp_d[st * TP + tt * 128: st * TP + tt * 128 + 128, :] if tt < 4 else ps_d
            bt, bb = big_alloc()
            DMA("sp", bt[0:R, 0:256], src, [], [bb])
            CP("dve", pb16[par][0:R, :], bt[0:R, 0:256], [bb], [Bpb16[par]])
            base, pbs = ps_alloc(1)
            pv = psum[:, base:base + 512].bitcast(BF16).rearrange("p (k c) -> p k c", k=8)
            TR([(pv[:, kc, 0:R], pb16[par][0:R, kc * 128:(kc + 1) * 128], ident[0:R, 0:R]) for kc in range(2)],
               [Bpb16[par], B["ident"]], pbs)
            ACT(pT[:, :, col0:col0 + R], pv[:, 0:2, 0:R], AF.Copy, pbs, [BpT[tt]])
            fm_tile(ti, tt, col0, R, None, None, norm=False)

        for idx, (tt, col0, R) in enumerate(tiles):
            b_, p_ = ps_alloc(2)
            lst = []
            for half in range(2):
                for kc in range(NFC):
                    lst.append((psum[0:R, b_ + half * 512:b_ + (half + 1) * 512], actT[:, kc, col0:col0 + R],
                                wdn[:, kc, half * 512:(half + 1) * 512], kc == 0, kc == NFC - 1))
            MM(lst, BactT + Bwdn, p_)
            norm_resid(b_, p_, R, tt, wb_ffn, Bwb_ffn)
            if idx >= LAG:
                ple_prep(idx - LAG, *tiles[idx - LAG])
        for i_ in range(max(0, len(tiles) - LAG), len(tiles)):
            ple_prep(i_, *tiles[i_])

        spp = w_acquire("pp")
        spg = [acquire_blk("pg") for _ in range(2)]
        wppv = wslot[spp][:, :, :].rearrange("p (a b) n -> p a (b n)", a=2)
        for idx, (tt, col0, R) in enumerate(tiles):
            bp, ppp = ps_alloc(2)
            lst = []
            for half in range(2):
                for kc in range(2):
                    lst.append((psum[0:R, bp + half * 512:bp + (half + 1) * 512], pT[:, kc, col0:col0 + R],
                                wppv[:, kc, half * 512:(half + 1) * 512], kc == 0, kc == 1))
            MM(lst, [BpT[tt], Bws[spp]], ppp)
            bg, pg = ps_alloc(2)
            lst = []
            for i, blk_ in enumerate(spg):
                for kc in range(8):
                    lst.append((psum[0:R, bg + i * 512:bg + (i + 1) * 512], hT[:, kc, col0:col0 + R],
                                blk_.rhs(kc), kc == 0, kc == 7))
            MM(lst, [BhT[tt]] + spg[0].B + spg[1].B, pg)
            bt, bb = big_alloc()
            ACT(bt[0:R, :], psum[0:R, bg:bg + 1024], AF.Sigmoid, pg, [bb])
            TT("dve", bt[0:R, :], psum[0:R, bp:bp + 1024], bt[0:R, :], ALU.mult, ppp + [bb], [bb])
            sa, sB = stat_alloc()
            ACT(junk[0:R, :], bt[0:R, :], AF.Square, [bb], [Bjunk, sB], accum=sa[0:R, 0:1])
            rstd_from(sa, sB, R, 1024)
            TT("dve", bt[0:R, :], bt[0:R, :], wb_ple[0:R, :], ALU.mult, [bb, Bwb_ple], [bb])
            STT("dve", ht[0:R, tt, :], bt[0:R, :], sa[0:R, 2:3], ht[0:R, tt, :], ALU.mult, ALU.add, [bb, sB, Bht[tt]], [Bht[tt]])
            dst = yp[st * TP + tt * 128: st * TP + tt * 128 + 128, :] if tt < 4 else ys
            DMA("sp", dst, ht[0:R, tt, :], [Bht[tt]], [])
            if not last:
                r0 = (st + 1) * TP + tt * 128
                DMA("sp", ht[:, tt, :], xp[r0:r0 + 128, :], [], [Bht[tt]])
                if idx >= LAG:
                    fm_tile(idx - LAG, *tiles[idx - LAG], wn_pre, B["wn_pre"])
                    preA.add((st + 1, tiles[idx - LAG][0]))
        w_release(spp, *spg[0].s, *spg[1].s)

    P.emit(nc, es)
    es.close()
    return nc


_NC_CACHE = {}


def kernel(x_prompt, x_sample, state_conv, state_gla, p_prompt, p_sample,
           w_norm_mix_pre, w_in, w_conv, w_a_out, w_gk, b_gk, w_gla_norm, w_b_out, w_o,
           w_norm_mix_post, w_norm_ffn_pre, w_ffn_gate, w_ffn_up, w_ffn_down, w_norm_ffn_post,
           w_ple_proj, w_ple_gate, w_norm_ple_post):
    f = lambda a: np.ascontiguousarray(np.asarray(a, dtype=np.float32))
    if "nc" not in _NC_CACHE:
        _NC_CACHE["nc"] = build_program()
    nc = _NC_CACHE["nc"]
    shared = {
        "w_pre": f(w_norm_mix_pre[0]), "w_in": f(w_in[0]), "w_conv": f(w_conv[0]), "w_a": f(w_a_out[0]),
        "w_gk": f(w_gk[0]), "b_gk": f(b_gk[0]), "w_gn": f(w_gla_norm[0]), "w_b": f(w_b_out[0]), "w_o": f(w_o[0]),
        "w_mixpost": f(w_norm_mix_post[0]), "w_ffnpre": f(w_norm_ffn_pre[0]), "w_fg": f(w_ffn_gate[0]),
        "w_fu": f(w_ffn_up[0]), "w_fd": f(w_ffn_down[0]), "w_ffnpost": f(w_norm_ffn_post[0]),
        "w_pp": f(w_ple_proj[0]), "w_pg": f(w_ple_gate[0]), "w_plepost": f(w_norm_ple_post[0]),
    }
    x_prompt = np.asarray(x_prompt); x_sample = np.asarray(x_sample)
    state_conv = np.asarray(state_conv); state_gla = np.asarray(state_gla)
    p_prompt = np.asarray(p_prompt); p_sample = np.asarray(p_sample)
    in_maps = []
    for c in range(8):
        m = dict(shared)
        m["xp"] = f(x_prompt[c])
        m["xs"] = f(x_sample[c * NS:(c + 1) * NS, 0, :])
        m["sconv"] = f(state_conv[0, c * NS:(c + 1) * NS])
        m["sgla"] = f(state_gla[0, c * NS:(c + 1) * NS])
        m["pp"] = f(p_prompt[0, c])
        m["psm"] = f(p_sample[0, c * NS:(c + 1) * NS, 0, :])
        in_maps.append(m)
    res = run_bass_kernel_spmd(nc, in_maps, core_ids=list(range(8)))
    r = res.results
    y_prompt = np.stack([r[c]["yp"] for c in range(8)], axis=0).astype(np.float32)
    y_sample = np.concatenate([r[c]["ys"] for c in range(8)], axis=0)[:, None, :].astype(np.float32)
    new_conv_prompt = np.stack([r[c]["ncp"] for c in range(8)], axis=0)[None].astype(np.float32)
    new_gla_prompt = np.stack([r[c]["ngp"] for c in range(8)], axis=0)[None].astype(np.float32)
    new_conv_sample = np.concatenate([r[c]["ncs"] for c in range(8)], axis=0)[None].astype(np.float32)
    new_gla_sample = np.concatenate([r[c]["ngs"] for c in range(8)], axis=0)[None].astype(np.float32)
    return (y_prompt, y_sample, new_conv_prompt, new_gla_prompt, new_conv_sample, new_gla_sample)
```

```python
import contextlib
import numpy as np
import concourse.bass as bass
import concourse.mybir as mybir
from concourse.bass_utils import run_bass_kernel_spmd

F32 = mybir.dt.float32
BF16 = mybir.dt.bfloat16
AF = mybir.ActivationFunctionType
ALU = mybir.AluOpType

ENGS = ("sp", "pe", "act", "dve", "pool")
EPS = 1e-6
NST = 4
TP = 512
NS = 16
DFF = 2816
NFC = DFF // 128
STRICT = True


class Buf:
    __slots__ = ("name", "last_w", "readers", "excl")

    def __init__(self, name, excl=False):
        self.name = name
        self.last_w = None
        self.readers = []
        self.excl = excl


class Op:
    __slots__ = ("eng", "fn", "deps", "is_dma", "signal", "sig_val", "sem", "prev_same_sem")

    def __init__(self, eng, fn, is_dma):
        self.eng = eng
        self.fn = fn
        self.is_dma = is_dma
        self.deps = []
        self.signal = is_dma
        self.sig_val = None
        self.sem = None
        self.prev_same_sem = None


class Prog:
    def __init__(self):
        self.ops = {e: [] for e in ENGS}
        self.n_dma_sems = {"sp": 12, "pool": 8}

    def op(self, eng, fn, reads=(), writes=(), dma=False):
        o = Op(eng, fn, dma)
        deps = {}

        def add(w, kind):
            cur = deps.get(id(w))
            if cur is None or kind < cur[1]:
                deps[id(w)] = (w, kind)

        for b in reads:
            if b.last_w is not None:
                add(b.last_w, 0)
            if b.excl:
                for r in b.readers:
                    add(r, 2)
        for b in writes:
            if b.last_w is not None:
                add(b.last_w, 1)
            for r in b.readers:
                add(r, 1)
        for w, kind in deps.values():
            if w is o:
                continue
            if (not w.is_dma) and (not dma) and w.eng == eng:
                if eng == "pe" or kind == 2 or (kind == 1 and not STRICT):
                    continue
            o.deps.append(w)
            w.signal = True
        for b in reads:
            b.readers.append(o)
        for b in writes:
            b.last_w = o
            b.readers = []
        self.ops[eng].append(o)
        return o

    def emit(self, nc, es):
        sems = {e: es.enter_context(nc.semaphore("s_" + e)) for e in ("pe", "act", "dve", "pool")}
        dsems = {q: [es.enter_context(nc.semaphore(f"d_{q}{i}")) for i in range(n)]
                 for q, n in self.n_dma_sems.items()}
        for e in ENGS:
            cnt = 0
            dcnt = 0
            last_on_sem = {}
            for o in self.ops[e]:
                if o.is_dma:
                    pool = dsems[e]
                    k = dcnt % len(pool)
                    o.sem = pool[k]
                    o.sig_val = 16 * (dcnt // len(pool) + 1)
                    o.prev_same_sem = last_on_sem.get(k)
                    last_on_sem[k] = o
                    dcnt += 1
                elif o.signal:
                    cnt += 1
                    o.sem = sems[e]
                    o.sig_val = cnt
        finals = []
        for q in dsems:
            last = {}
            for o in self.ops[q]:
                if o.is_dma:
                    last[id(o.sem)] = o
            finals.extend(last.values())
        block = es.enter_context(nc.Block())

        def run(e, h):
            waited = {}

            def wait(sem, val):
                if waited.get(id(sem), 0) >= val:
                    return
                h.wait_ge(sem, val)
                waited[id(sem)] = val

            for o in self.ops[e]:
                for d in o.deps:
                    wait(d.sem, d.sig_val)
                if o.is_dma and o.prev_same_sem is not None:
                    wait(o.prev_same_sem.sem, o.prev_same_sem.sig_val)
                ins = o.fn(h)
                if o.signal:
                    ins.then_inc(o.sem, 16 if o.is_dma else 1)
            if e == "sp":
                for o in finals:
                    wait(o.sem, o.sig_val)

        @block.sync
        def _(h):
            run("sp", h)

        @block.tensor
        def _(h):
            run("pe", h)

        @block.scalar
        def _(h):
            run("act", h)

        @block.vector
        def _(h):
            run("dve", h)

        @block.gpsimd
        def _(h):
            run("pool", h)


def handoff(src, dst):
    users = []
    for s in src:
        users.extend(s.readers)
        if s.last_w is not None:
            users.append(s.last_w)
    for d in dst:
        d.readers = list(d.readers) + users


SLOTW = 256
NSLOT = 8
LAG = 9


def build_program():
    nc = bass.Bass("TRN2", target_bir_lowering=False)
    P = Prog()
    es = contextlib.ExitStack()

    def din(name, shape):
        return nc.dram_tensor(name, shape, F32, kind="ExternalInput").ap()

    def dout(name, shape):
        return nc.dram_tensor(name, shape, F32, kind="ExternalOutput").ap()

    xp = din("xp", [NST * TP, 1024])
    xs = din("xs", [NS, 1024])
    sconv = din("sconv", [NS, 2, 1024])
    sgla = din("sgla", [NS, 4, 128, 256])
    pp_d = din("pp", [NST * TP, 256])
    ps_d = din("psm", [NS, 256])
    w_pre = din("w_pre", [1024])
    w_in = din("w_in", [1024, 8208])
    w_conv = din("w_conv", [3, 1024])
    w_a = din("w_a", [1024, 1024])
    w_gk = din("w_gk", [16, 512])
    b_gk = din("b_gk", [512])
    w_gn = din("w_gn", [256])
    w_b = din("w_b", [1024, 1024])
    w_o = din("w_o", [1024, 1024])
    w_mixpost = din("w_mixpost", [1024])
    w_ffnpre = din("w_ffnpre", [1024])
    w_fg = din("w_fg", [1024, DFF])
    w_fu = din("w_fu", [1024, DFF])
    w_fd = din("w_fd", [DFF, 1024])
    w_ffnpost = din("w_ffnpost", [1024])
    w_pp = din("w_pp", [256, 1024])
    w_pg = din("w_pg", [1024, 1024])
    w_plepost = din("w_plepost", [1024])
    yp = dout("yp", [NST * TP, 1024])
    ys = dout("ys", [NS, 1024])
    ncp = dout("ncp", [2, 1024])
    ngp = dout("ngp", [4, 128, 256])
    ncs = dout("ncs", [NS, 2, 1024])
    ngs = dout("ngs", [NS, 4, 128, 256])

    def sb(name, shape, dt):
        return es.enter_context(nc.sbuf_tensor(name, shape, dt))

    WM = TP + NS

    ident = sb("ident", [128, 128], BF16)
    identf = sb("identf", [128, 128], F32)
    tri = sb("tri", [128, 128], F32)
    wn_pre = sb("wn_pre", [128, 8], F32)
    wn_ffn = sb("wn_ffn", [128, 8], F32)
    wcv = sb("wcv", [128, 3, 8], F32)
    wgn_bc = sb("wgn_bc", [128, 256], F32)
    wbc = [sb(f"wbc{i}", [128, 1024], F32) for i in range(2)]
    Bwbc = [Buf("wbc0"), Buf("wbc1")]
    wgk = sb("wgk", [32, 512], F32)
    gkl = sb("gkl", [32, WM], F32)
    maskrow = sb("maskrow", [128, NS, 4, NS], BF16)
    uhist = sb("uhist", [128, 8, 2], F32)
    scT = sb("scT", [128, 2, 8, NS], F32)
    utail = sb("utail", [128, 8, 18], F32)
    S = sb("S", [128, 4, 256], F32)
    stat = sb("stat", [128, 16, 4], F32)
    ssg = sb("ssg", [128, 2, 8], F32)
    aS = sb("aS", [128, 4, NS], F32)
    aStmp = sb("aStmp", [128, 4, NS], F32)
    sqk = sb("sqk", [NS, 4], F32)
    QA = sb("QA", [128, 4, NS], BF16)
    BQA = Buf("QA")
    B = {}

    def mk(*names):
        for n in names:
            B[n] = Buf(n)

    mk("ident", "identf", "tri", "wn_pre", "wn_ffn", "wcv", "wgn_bc",
       "wgk", "gkl", "maskrow", "uhist", "scT", "utail", "S", "aS", "aStmp", "ssg0", "ssg1")
    statB = [Buf(f"stat{i}") for i in range(16)]

    ht = sb("ht", [128, 5, 1024], F32)
    hT = sb("hT", [128, 8, WM], BF16)
    Bht = [Buf(f"ht{i}") for i in range(5)]
    BhT = [Buf(f"hT{i}") for i in range(5)]

    regA = sb("regA", [128, 12672], BF16)
    merged_a = regA[:, 0:8448].bitcast(F32).rearrange("p (k c) -> p k c", k=8)
    mergedT = regA[:, 8448:12672].rearrange("p (k c) -> p k c", k=8)
    actT = regA[:, 0:NFC * WM].rearrange("p (k c) -> p k c", k=NFC)
    Bma = [Buf(f"ma{i}") for i in range(8)]
    BmT = [Buf(f"mT{i}") for i in range(8)]
    BactT = [Buf(f"actT{i}") for i in range(NFC)]

    regB = sb("regB", [128, NFC * 1024], BF16)
    o = 0
    goT = regB[:, o:o + 8 * WM].rearrange("p (k c) -> p k c", k=8); o += 8 * WM
    qT = regB[:, o:o + 4 * WM].rearrange("p (k c) -> p k c", k=4); o += 4 * WM
    kT = regB[:, o:o + 4 * WM].rearrange("p (k c) -> p k c", k=4); o += 4 * WM
    v_tok = regB[:, o:o + 5 * 1024].rearrange("p (k c) -> p k c", k=5); o += 5 * 1024
    sg_tok = regB[:, o:o + 5 * 1024].rearrange("p (k c) -> p k c", k=5); o += 5 * 1024
    gtmp = []
    for i in range(5):
        gtmp.append(regB[:, o:o + 512]); o += 512
    qe_a, ke_a, kd_a, scm_a = [t.rearrange("p (h c) -> p h c", h=4) for t in gtmp[0:4]]
    kdt_a = gtmp[4]
    assert o <= NFC * 1024, o
    wdn = regB[:, 0:NFC * 1024].rearrange("p (k c) -> p k c", k=NFC)
    Sbf_t = [sb(f"Sbf{i}", [128, 4, 256], BF16) for i in range(2)]
    Sbf = [t[:, :, :] for t in Sbf_t]
    BgoT = [Buf(f"goT{i}") for i in range(8)]
    BgoTc = [Buf(f"goTc{i}") for i in range(5)]
    Bq, Bk = Buf("qT"), Buf("kT")
    Bv = [Buf(f"v{i}") for i in range(5)]
    Bsg = [Buf(f"sg{i}") for i in range(5)]
    BSbf = [Buf("Sbf0"), Buf("Sbf1")]
    Bqe, Bke, Bkd, Bscm, Bkdt = [Buf(n) for n in ("qe", "ke", "kd", "scm", "kdt")]
    Bwdn = [Buf(f"wdn{i}") for i in range(6)]
    mixB_bufs = BgoT + BgoTc + [Bq, Bk] + Bv + Bsg + [Bqe, Bke, Bkd, Bscm, Bkdt]

    regC = sb("regC", [128, 4608], F32)
    regCc = sb("regCc", [128, 2116], F32)
    cS = regCc[:, 0:528]
    ubuf = [regCc[:, 528:1058], regCc[:, 1058:1588]]
    ycv = regCc[:, 1588:2116]
    e1 = regC[:, 0:512]
    lt = [regC[:, 512:1024], regC[:, 1024:1536]]
    eb_a = regC[:, 1536:2048].rearrange("p (h c) -> p h c", h=4)
    einv_a = regC[:, 2048:2560].rearrange("p (h c) -> p h c", h=4)
    ed_a = regC[:, 2560:3072].rearrange("p (h c) -> p h c", h=4)
    onb = [regC[:, 3072:3328], regC[:, 3328:3584]]
    oS = regC[0:NS, 3584:4608]
    BcS, Bu, Byc = Buf("cS"), [Buf("u0"), Buf("u1")], Buf("yc")
    Be1, Blt = Buf("e1"), [Buf("lt0"), Buf("lt1")]
    Beb, Beinv, Bed = Buf("eb"), Buf("einv"), Buf("ed")
    Bon = [Buf("on0"), Buf("on1")]
    BoS = Buf("oS")
    convC = [BcS, Byc] + Bu
    glaC = [Be1] + Blt + [Beb, Beinv, Bed] + Bon

    NBT = 2
    bigt = [sb(f"bigt{i}", [128, 1024], F32) for i in range(NBT)]
    Bbig = [Buf(f"bigt{i}") for i in range(NBT)]
    junk = sb("junk", [128, 1024], BF16)
    Bjunk = Buf("junk")
    SSbf = junk[:, :].rearrange("p (h c) -> p h c", h=4)
    hs = [sb(f"hs{i}", [128, 1024], BF16) for i in range(2)]
    Bhs = [Buf("hs0"), Buf("hs1")]
    ogt, Bogt = hs[0], Bhs[0]
    kS_tok = hs[1][0:NS, 0:512]
    KSj = hs[1][0:NS, 512:1024]
    BkS, BKSj = Buf("kS"), Buf("KSj")
    SS = [sb(f"SS{i}", [128, 4, 256], F32) for i in range(2)]
    BSS = [Buf("SS0"), Buf("SS1")]
    QmJ = [sb(f"QmJ{i}", [128, 4, NS], BF16) for i in range(2)]
    BQmJ = [Buf("QmJ0"), Buf("QmJ1")]
    pb16 = [sb(f"pb16{i}", [128, 256], BF16) for i in range(2)]
    Bpb16 = [Buf("pb160"), Buf("pb161")]
    pT = sb("pT", [128, 2, WM], BF16)
    BpT = [Buf(f"pT{i}") for i in range(5)]

    wslot = [sb(f"wslot{i}", [128, 8, SLOTW], BF16) for i in range(NSLOT)]
    Bws = [Buf(f"wslot{i}") for i in range(NSLOT)]

    psum = es.enter_context(nc.psum_tensor("psum", [128, 4096], F32))
    Bps = [Buf(f"psb{i}", excl=True) for i in range(8)]
    ps_state = {"next": 0, "lo": 0, "hi": 8}

    def ps_alloc(n):
        p = ps_state["next"]
        lo, hi = ps_state["lo"], ps_state["hi"]
        if p < lo or p >= hi:
            p = lo
        if n == 2 and (p % 2):
            p += 1
        if p + n > hi:
            p = lo
        ps_state["next"] = p + n
        return p * 512, Bps[p:p + n]

    def G(b, n):
        return b * 512, Bps[b:b + n]

    big_state = {"next": 0}

    def big_alloc():
        i = big_state["next"]
        big_state["next"] = (i + 1) % NBT
        return bigt[i], Bbig[i]

    stat_state = {"next": 0}

    def stat_alloc():
        i = stat_state["next"]
        stat_state["next"] = (i + 1) % 16
        return stat[:, i, :], statB[i]

    def ACT(out, in_, func, reads, writes, scale=None, bias=None, accum=None):
        kw = {}
        if scale is not None:
            kw["scale"] = scale
        if bias is not None:
            kw["bias"] = bias
        if accum is not None:
            kw["accum_out"] = accum
        P.op("act", lambda h: h.activation(out=out, in_=in_, func=func, **kw), reads, writes)

    def TT(eng, out, in0, in1, op, reads, writes):
        P.op(eng, lambda h: h.tensor_tensor(out=out, in0=in0, in1=in1, op=op), reads, writes)

    def TS(eng, out, in0, s1, op0, reads, writes):
        P.op(eng, lambda h: h.tensor_scalar(out=out, in0=in0, scalar1=s1, scalar2=None, op0=op0), reads, writes)

    def STT(eng, out, in0, scalar, in1, op0, op1, reads, writes):
        P.op(eng, lambda h: h.scalar_tensor_tensor(out=out, in0=in0, scalar=scalar, in1=in1, op0=op0, op1=op1), reads, writes)

    def CP(eng, out, in_, reads, writes):
        P.op(eng, lambda h: h.tensor_copy(out=out, in_=in_), reads, writes)

    def MS(eng, ap, val, writes):
        P.op(eng, lambda h: h.memset(ap, val), (), writes)

    def MM(lst, reads, writes):
        def fn(h):
            ins = None
            for (out, lhsT, rhs, start, stop) in lst:
                ins = h.matmul(out, lhsT=lhsT, rhs=rhs, start=start, stop=stop)
            return ins
        P.op("pe", fn, reads, writes)

    def TR(lst, reads, writes):
        def fn(h):
            ins = None
            for (out, in_, idn) in lst:
                ins = h.transpose(out=out, in_=in_, identity=idn)
            return ins
        P.op("pe", fn, reads, writes)

    def DMA(q, out, in_, reads, writes, noncontig=False):
        def fn(h):
            if noncontig:
                with nc.allow_non_contiguous_dma(reason="small strided constant load"):
                    return h.dma_start(out=out, in_=in_)
            return h.dma_start(out=out, in_=in_)
        P.op(q, fn, reads, writes, dma=True)

    MS("pool", identf[:], 1.0, [B["identf"]])
    P.op("pool", lambda h: h.affine_select(out=identf[:], in_=identf[:], pattern=[[-1, 128]], compare_op=ALU.is_equal,
                                           fill=0.0, base=0, channel_multiplier=1), [B["identf"]], [B["identf"]])
    CP("dve", ident[:], identf[:], [B["identf"]], [B["ident"]])
    MS("pool", tri[:], 1.0, [B["tri"]])
    P.op("pool", lambda h: h.affine_select(out=tri[:], in_=tri[:], pattern=[[1, 128]], compare_op=ALU.is_ge,
                                           fill=0.0, base=0, channel_multiplier=-1), [B["tri"]], [B["tri"]])
    DMA("sp", wn_pre[:], w_pre.rearrange("(kc p) -> p kc", p=128), [], [B["wn_pre"]], noncontig=True)
    DMA("sp", wn_ffn[:], w_ffnpre.rearrange("(kc p) -> p kc", p=128), [], [B["wn_ffn"]], noncontig=True)
    DMA("sp", wcv[:], w_conv.rearrange("j (kc p) -> p j kc", p=128), [], [B["wcv"]], noncontig=True)
    DMA("sp", wgn_bc[:], w_gn.partition_broadcast(128), [], [B["wgn_bc"]])
    DMA("sp", wgk[0:16, :], w_gk, [], [B["wgk"]])
    DMA("sp", wgk[16:17, :], b_gk.rearrange("(o n) -> o n", o=1), [], [B["wgk"]])
    MS("dve", gkl[:], 1.0, [B["gkl"]])
    MS("dve", uhist[:], 0.0, [B["uhist"]])
    MS("dve", S[:], 0.0, [B["S"]])
    MS("dve", Sbf[0], 0.0, [BSbf[0]])
    MS("dve", stat[:], 1.0, statB)
    MS("dve", maskrow[:], 0.0, [B["maskrow"]])
    for j in range(NS):
        MS("dve", maskrow[:, j, :, j:j + 1], 1.0, [B["maskrow"]])
    for t in range(2):
        bt, bb = big_alloc()
        DMA("sp", bt[0:NS, :], sconv[:, t, :], [], [bb])
        if t == 1:
            DMA("sp", ncs[:, 0, :], bt[0:NS, :], [bb], [])
        base, pbs = ps_alloc(1)
        TR([(psum[:, base + kc * NS:base + (kc + 1) * NS], bt[0:NS, kc * 128:(kc + 1) * 128], identf[0:NS, 0:NS])
            for kc in range(8)], [bb, B["identf"]], pbs)
        ACT(scT[:, t, :, :], psum[:, base:base + 8 * NS].rearrange("p (k c) -> p k c", k=8), AF.Copy, pbs, [B["scT"]])

    wbc_state = {"i": 0}

    def wbc_load(src):
        i = wbc_state["i"]
        wbc_state["i"] = 1 - i
        DMA("sp", wbc[i][:], src.partition_broadcast(128), [], [Bwbc[i]])
        return wbc[i], Bwbc[i]

    def build_groups():
        g = []

        def blk(name, mat, c0):
            g.append((name, mat, 0, 4, c0, 512))
            g.append((name, mat, 4, 4, c0, 512))

        for st in range(NST):
            blk("q", w_in, 3072)
            blk("k", w_in, 3584)
            g.append(("gklr", w_in, 0, 8, 6144, 16))
            for i in range(2):
                blk("v", w_in, 4096 + i * 512)
            for i in range(2):
                blk("g", w_in, 5120 + i * 512)
            for qd in range(4):
                g.append(("c", w_in, 0, 8, 1024 + qd * SLOTW, SLOTW))
                g.append(("x", w_in, 0, 8, 2048 + qd * SLOTW, SLOTW))
                g.append(("b", w_in, 0, 8, 0 + qd * SLOTW, SLOTW))
            for half in range(2):
                blk("wa", w_a, half * 512)
                blk("ga", w_in, 6160 + half * 512)
            for half in range(2):
                blk("wb", w_b, half * 512)
                blk("gb", w_in, 7184 + half * 512)
            for half in range(2):
                blk("wo", w_o, half * 512)
            wd = 0
            for gi in range(6):
                if gi < 5:
                    blk("fg", w_fg, gi * 512)
                    blk("fu", w_fu, gi * 512)
                else:
                    g.append(("fg", w_fg, 0, 8, 2560, 256))
                    g.append(("fu", w_fu, 0, 8, 2560, 256))
                k0 = (wd // 2) * 8
                nk = min(8, NFC - k0)
                g.append(("WD", w_fd, k0, nk, (wd % 2) * 512, 512))
                wd += 1
            g.append(("pp", w_pp, 0, 2, 0, 1024))
            for half in range(2):
                blk("pg", w_pg, half * 512)
        return g

    groups = build_groups()
    GPS = len(groups) // NST
    ring_idx = {}
    wd_idx = {}
    for li in range(GPS):
        if groups[li][0] == "WD":
            wd_idx[li] = len(wd_idx)
        else:
            ring_idx[li] = len(ring_idx)
    wscr = nc.dram_tensor("wscr", [len(ring_idx), 128, 8 * SLOTW], BF16).ap()
    wdscr = nc.dram_tensor("wdscr", [6, 128, 8 * 512], BF16).ap()
    Bscr = [Buf(f"scr{i}") for i in range(len(ring_idx))]
    Bwdscr = [Buf(f"wdscr{i}") for i in range(6)]
    WS = {"next_load": 0, "next_use": 0, "free": list(range(NSLOT)), "slot_of": {}, "wd_ok": False, "wd_idx": 0}

    def w_prefetch():
        while WS["next_load"] < len(groups):
            gi = WS["next_load"]
            name, mat, k0, nk, c0, nco = groups[gi]
            st_, li = gi // GPS, gi % GPS
            src = mat[k0 * 128:(k0 + nk) * 128, c0:c0 + nco].rearrange("(kc p) n -> p kc n", p=128)
            if name == "WD":
                if not WS["wd_ok"]:
                    return
                piece = wd_idx[li]
                dst = wdn[:, k0:k0 + nk, c0:c0 + nco]
                scr = wdscr[piece][:, 0:nk * 512].rearrange("p (k c) -> p k c", k=nk)
                conv_st = piece % 2
                if st_ <= conv_st:
                    DMA("pool", dst, src, [], [Bwdn[piece]])
                    if st_ == conv_st:
                        DMA("sp", scr, dst, [Bwdn[piece]], [Bwdscr[piece]])
                else:
                    DMA("pool", dst, scr, [Bwdscr[piece]], [Bwdn[piece]])
                WS["next_load"] += 1
                continue
            if not WS["free"]:
                return
            s = WS["free"].pop(0)
            ri = ring_idx[li]
            img = wslot[s][:, :, :].rearrange("p k c -> p (k c)")
            conv_st = ri % 2
            if st_ <= conv_st:
                dst = img.rearrange("p (k c) -> p k c", k=nk)[:, :, 0:nco]
                if name == "gklr":
                    MS("pool", wslot[s][:, :, :], 0.0, [Bws[s]])
                DMA("pool", dst, src, [], [Bws[s]])
                if st_ == conv_st:
                    DMA("sp", wscr[ri], img, [Bws[s]], [Bscr[ri]])
            else:
                DMA("pool", img, wscr[ri], [Bscr[ri]], [Bws[s]])
            WS["slot_of"][gi] = s
            WS["next_load"] += 1

    def w_acquire(name):
        while groups[WS["next_use"]][0] == "WD":
            WS["next_use"] += 1
        gi = WS["next_use"]
        assert groups[gi][0] == name, (groups[gi][0], name)
        if gi not in WS["slot_of"]:
            w_prefetch()
        assert gi in WS["slot_of"], ("weight ring deadlock", name, gi)
        WS["next_use"] += 1
        return WS["slot_of"][gi]

    class Grp:
        def __init__(self, s_):
            self.s = (s_,)
            self.B = [Bws[s_]]

        def lhsT(self, kc, j, M=128):
            return wslot[self.s[0]][:, kc, j * 128:j * 128 + M]

    class Blk:
        def __init__(self, sA, sB):
            self.s = (sA, sB)
            self.B = [Bws[sA], Bws[sB]]
            self.v = [wslot[x][:, :, :].rearrange("p k c -> p (k c)").rearrange("p (k c) -> p k c", k=4) for x in (sA, sB)]

        def lhsT(self, kc, j, M=128):
            return self.v[kc // 4][:, kc % 4, j * 128:j * 128 + M]

        def rhs(self, kc):
            return self.v[kc // 4][:, kc % 4, :]

    def acquire_blk(name):
        sA = w_acquire(name)
        sB = w_acquire(name)
        return Blk(sA, sB)

    def w_release(*slots):
        for s in slots:
            WS["free"].append(s)
        w_prefetch()

    def mm_fm(base, pbs, wobj, j, src, srcB, W, M=128):
        lst = []
        blocks = [(0, TP)] + ([(TP, W - TP)] if W > TP else [])
        for (c0, cn) in blocks:
            for kc in range(8):
                lst.append((psum[0:M, base + c0:base + c0 + cn], wobj.lhsT(kc, j, M),
                            src[:, kc, c0:c0 + cn], kc == 0, kc == 7))
        MM(lst, wobj.B + srcB, pbs)

    def rstd_from(ssap, sB, R, n):
        ACT(ssap[0:R, 1:2], ssap[0:R, 0:1], AF.Ln, [sB], [sB], scale=1.0 / n, bias=EPS)
        ACT(ssap[0:R, 2:3], ssap[0:R, 1:2], AF.Exp, [sB], [sB], scale=-0.5)

    def fm_tile(ti, tt, col0, R, wn, wnB, norm=True):
        par = ti % 2
        if norm:
            sa, sB = stat_alloc()
            ACT(junk[0:R, :], ht[0:R, tt, :], AF.Square, [Bht[tt]], [Bjunk, sB], accum=sa[0:R, 0:1])
            rstd_from(sa, sB, R, 1024)
            TS("dve", hs[par][0:R, :], ht[0:R, tt, :], sa[0:R, 2:3], ALU.mult, [Bht[tt], sB], [Bhs[par]])
        else:
            CP("dve", hs[par][0:R, :], ht[0:R, tt, :], [Bht[tt]], [Bhs[par]])
        base, pbs = ps_alloc(1)
        pv = psum[:, base:base + 512].bitcast(BF16).rearrange("p (k c) -> p k c", k=8)
        TR([(pv[:, kc, 0:R], hs[par][0:R, kc * 128:(kc + 1) * 128], ident[0:R, 0:R]) for kc in range(8)],
           [Bhs[par], B["ident"]], pbs)
        if wn is not None:
            TT("dve", hT[:, :, col0:col0 + R], pv[:, :, 0:R], wn[:, :].unsqueeze(2).to_broadcast([128, 8, R]),
               ALU.mult, pbs + [wnB], [BhT[tt]])
        else:
            ACT(hT[:, :, col0:col0 + R], pv[:, :, 0:R], AF.Copy, pbs, [BhT[tt]])

    def norm_resid(base, pbs, R, tt, wb_, wbB):
        src = psum[0:R, base:base + 1024]
        sa, sB = stat_alloc()
        ACT(junk[0:R, :], src, AF.Square, pbs, [Bjunk, sB], accum=sa[0:R, 0:1])
        rstd_from(sa, sB, R, 1024)
        bt, bb = big_alloc()
        TT("dve", bt[0:R, :], src, wb_[0:R, :], ALU.mult, pbs + [wbB], [bb])
        STT("dve", ht[0:R, tt, :], bt[0:R, :], sa[0:R, 2:3], ht[0:R, tt, :], ALU.mult, ALU.add, [bb, sB, Bht[tt]], [Bht[tt]])

    preA = set()
    for st in range(NST):
        last = st == NST - 1
        W = WM if last else TP
        nbW = 2 if last else 1
        tiles = [(tt, tt * 128, 128) for tt in range(4)] + ([(4, TP, NS)] if last else [])
        allhT = [BhT[t[0]] for t in tiles]

        if st > 0:
            handoff(Bwdn, mixB_bufs)
            handoff(BactT, Bma + BmT)
            WS["wd_ok"] = False

        if st == 0:
            for tt in range(4):
                DMA("sp", ht[:, tt, :], xp[tt * 128:(tt + 1) * 128, :], [], [Bht[tt]])
            DMA("sp", ht[0:NS, 4, :], xs, [], [Bht[4]])
        w_prefetch()
        for ti, (tt, col0, R) in enumerate(tiles):
            if (st, tt) not in preA:
                fm_tile(ti, tt, col0, R, wn_pre, B["wn_pre"])
        wb_mix, Bwb_mix = wbc_load(w_mixpost)

        for (nm, dstT, dB, scl) in (("q", qT, Bq, float(128 ** -0.5)), ("k", kT, Bk, None)):
            blk_ = acquire_blk(nm)
            for h_ in range(4):
                b_, p_ = ps_alloc(nbW)
                mm_fm(b_, p_, blk_, h_, hT, allhT, W)
                ACT(dstT[:, h_, 0:W], psum[:, b_:b_ + W], AF.Copy, p_, [dB], scale=scl)
            w_release(*blk_.s)
        sg_ = w_acquire("gklr")
        b_, p_ = ps_alloc(nbW)
        mm_fm(b_, p_, Grp(sg_), 0, hT, allhT, W, M=16)
        ACT(gkl[0:16, 0:W], psum[0:16, b_:b_ + W], AF.Copy, p_, [B["gkl"]])
        w_release(sg_)
        for (nm, dstt, dB, fn) in (("v", v_tok, Bv, AF.Copy), ("g", sg_tok, Bsg, AF.Silu)):
            for half in range(2):
                blk_ = acquire_blk(nm)
                for (tt, col0, R) in tiles:
                    b_, p_ = ps_alloc(1)
                    MM([(psum[0:R, b_:b_ + 512], hT[:, kc, col0:col0 + R], blk_.rhs(kc), kc == 0, kc == 7)
                        for kc in range(8)], [BhT[tt]] + blk_.B, p_)
                    ACT(dstt[0:R, tt, half * 512:(half + 1) * 512], psum[0:R, b_:b_ + 512], fn, p_, [dB[tt]])
                w_release(*blk_.s)


        def gla_post(src_h, srcB, R, tt, col0, par):
            sg_ap = ssg[:, par, :]
            sgB = B[f"ssg{par}"]
            for h_ in range(4):
                ACT(junk[0:R, 0:256], src_h(h_), AF.Square, srcB, [Bjunk, sgB], accum=sg_ap[0:R, h_:h_ + 1])
            ACT(sg_ap[0:R, 4:8], sg_ap[0:R, 0:4], AF.Ln, [sgB], [sgB], scale=1.0 / 256, bias=EPS)
            ACT(sg_ap[0:R, 4:8], sg_ap[0:R, 4:8], AF.Exp, [sgB], [sgB], scale=-0.5)
            for h_ in range(4):
                hp = h_ % 2
                STT("dve", onb[hp][0:R, :], src_h(h_), sg_ap[0:R, 4 + h_:5 + h_],
                    wgn_bc[0:R, :], ALU.mult, ALU.mult, srcB + [sgB, B["wgn_bc"]], [Bon[hp]])
                TT("dve", ogt[0:R, h_ * 256:(h_ + 1) * 256], onb[hp][0:R, :], sg_tok[0:R, tt, h_ * 256:(h_ + 1) * 256],
                   ALU.mult, [Bon[hp], Bsg[tt]], [Bogt])
            base, pbs = G(0, 1)
            pv = psum[:, base:base + 512].bitcast(BF16).rearrange("p (k c) -> p k c", k=8)
            TR([(pv[:, kc, 0:R], ogt[0:R, kc * 128:(kc + 1) * 128], ident[0:R, 0:R]) for kc in range(8)],
               [Bogt, B["ident"]], pbs)
            ACT(goT[:, :, col0:col0 + R], pv[:, :, 0:R], AF.Copy, pbs, [BgoTc[tt]] + BgoT)

        def sample_prep():
            bgk, pgk = G(0, 1)
            MM([(psum[:, bgk + h_ * NS:bgk + (h_ + 1) * NS], wgk[0:17, h_ * 128:(h_ + 1) * 128], gkl[0:17, TP:TP + NS], True, True)
                for h_ in range(4)], [B["gkl"], B["wgk"]], pgk)
            pgv = psum[:, bgk:bgk + 4 * NS].rearrange("p (k c) -> p k c", k=4)
            ACT(aStmp[:, :, :], pgv, AF.Exp, pgk, [B["aStmp"]], scale=-1.0)
            ACT(aStmp[:, :, :], aStmp[:, :, :], AF.Ln, [B["aStmp"]], [B["aStmp"]], bias=1.0)
            ACT(aS[:, :, :], aStmp[:, :, :], AF.Exp, [B["aStmp"]], [B["aS"]], scale=-1.0 / 16)
            TT("dve", QA[:, :, :], qT[:, :, TP:TP + NS], aS[:, :, :], ALU.mult, [Bq, B["aS"]], [BQA])
            bkt, pkt = G(1, 1)
            pktv = psum[:, bkt:bkt + 512].bitcast(BF16)
            TR([(pktv[0:NS, h_ * 128:(h_ + 1) * 128], kT[:, h_, TP:TP + NS], ident[:, :]) for h_ in range(4)] +
               [(pktv[0:NS, 512 + h_ * 128:512 + (h_ + 1) * 128], qT[:, h_, TP:TP + NS], ident[:, :]) for h_ in range(4)],
               [Bk, Bq, B["ident"]], pkt)
            ACT(kS_tok, pktv[0:NS, 0:512], AF.Copy, pkt, [BkS])
            bt, bb = big_alloc()
            TT("dve", bt[0:NS, 0:512], pktv[0:NS, 512:1024], kS_tok, ALU.mult, pkt + [BkS], [bb])
            for h_ in range(4):
                ACT(junk[0:NS, 0:128], bt[0:NS, h_ * 128:(h_ + 1) * 128], AF.Copy, [bb], [Bjunk, B["aStmp"]],
                    accum=sqk[0:NS, h_:h_ + 1])
            for h_ in range(4):
                TS("dve", oS[:, h_ * 256:(h_ + 1) * 256], v_tok[0:NS, 4, h_ * 256:(h_ + 1) * 256], sqk[0:NS, h_:h_ + 1],
                   ALU.mult, [Bv[4], B["aStmp"]], [BoS])

        def sample_load(j):
            DMA("sp", SS[j % 2][:, :, :], sgla[j].rearrange("h k v -> k h v"), [], [BSS[j % 2]])

        def sample_iter(j):
            jp = j % 2
            if j == 0:
                sample_load(0)
            if j + 1 < NS:
                sample_load(j + 1)
            ACT(SSbf, SS[jp][:, :, :], AF.Copy, [BSS[jp]], [Bjunk])
            TS("dve", KSj, kS_tok, identf[0:NS, j:j + 1], ALU.mult, [BkS, B["identf"]], [BKSj])
            TT("dve", QmJ[jp][:, :, :], QA[:, :, :], maskrow[:, j, :, :], ALU.mult, [BQA, B["maskrow"]], [BQmJ[jp]])
            bkv, pkv = G(0, 2)
            MM([(psum[:, bkv + h_ * 256:bkv + (h_ + 1) * 256], KSj[:, h_ * 128:(h_ + 1) * 128],
                 v_tok[0:NS, 4, h_ * 256:(h_ + 1) * 256], True, True) for h_ in range(4)], [BKSj, Bv[4]], pkv)
            bo, pbo = G(2, 2)
            MM([(psum[0:NS, bo + h_ * 256:bo + (h_ + 1) * 256], QmJ[jp][:, h_, :], SSbf[:, h_, :], True, True)
                for h_ in range(4)], [BQmJ[jp], Bjunk], pbo)
            for h_ in range(4):
                STT("dve", SS[jp][:, h_, :], SS[jp][:, h_, :], aS[:, h_, j:j + 1], psum[:, bkv + h_ * 256:bkv + (h_ + 1) * 256],
                    ALU.mult, ALU.add, [BSS[jp], B["aS"]] + pkv, [BSS[jp]])
            DMA("sp", ngs[j].rearrange("h k v -> k h v"), SS[jp][:, :, :], [BSS[jp]], [])
            TT("dve", oS, oS, psum[0:NS, bo:bo + 1024], ALU.add, pbo + [BoS], [BoS])

        def conv_gen():
            for qd in range(4):
                sc = Grp(w_acquire("c"))
                sx = Grp(w_acquire("x"))
                sbk = Grp(w_acquire("b"))
                for j in range(2):
                    kc = qd * 2 + j
                    par = kc % 2
                    u = ubuf[par]
                    bc, pc = ps_alloc(nbW)
                    mm_fm(bc, pc, sc, j, hT, allhT, W)
                    ACT(cS[:, 0:W], psum[:, bc:bc + W], AF.Copy, pc, [BcS])
                    bx, px = ps_alloc(nbW)
                    mm_fm(bx, px, sx, j, hT, allhT, W)
                    TT("dve", u[:, 2:2 + W], psum[:, bx:bx + W], cS[:, 0:W], ALU.mult, px + [BcS], [Bu[par]])
                    CP("dve", u[:, 0:2], uhist[:, kc, :], [B["uhist"]], [Bu[par]])
                    CP("dve", uhist[:, kc, :], u[:, TP:TP + 2], [Bu[par]], [B["uhist"]])
                    TS("dve", ycv[:, 0:TP], u[:, 0:TP], wcv[:, 0, kc:kc + 1], ALU.mult, [Bu[par], B["wcv"]], [Byc])
                    STT("dve", ycv[:, 0:TP], u[:, 1:TP + 1], wcv[:, 1, kc:kc + 1], ycv[:, 0:TP], ALU.mult, ALU.add,
                        [Bu[par], B["wcv"], Byc], [Byc])
                    STT("dve", ycv[:, 0:TP], u[:, 2:TP + 2], wcv[:, 2, kc:kc + 1], ycv[:, 0:TP], ALU.mult, ALU.add,
                        [Bu[par], B["wcv"], Byc], [Byc])
                    if last:
                        TS("dve", ycv[:, TP:W], scT[:, 0, kc, :], wcv[:, 0, kc:kc + 1], ALU.mult, [B["scT"], B["wcv"]], [Byc])
                        STT("dve", ycv[:, TP:W], scT[:, 1, kc, :], wcv[:, 1, kc:kc + 1], ycv[:, TP:W], ALU.mult, ALU.add,
                            [B["scT"], B["wcv"], Byc], [Byc])
                        STT("dve", ycv[:, TP:W], u[:, TP + 2:W + 2], wcv[:, 2, kc:kc + 1], ycv[:, TP:W], ALU.mult, ALU.add,
                            [Bu[par], B["wcv"], Byc], [Byc])
                        CP("dve", utail[:, kc, :], u[:, TP:TP + 18], [Bu[par]], [B["utail"]])
                    bb_, pb_ = ps_alloc(nbW)
                    mm_fm(bb_, pb_, sbk, j, hT, allhT, W)
                    TT("dve", mergedT[:, kc, 0:W], psum[:, bb_:bb_ + W], ycv[:, 0:W], ALU.mult, pb_ + [Byc], [BmT[kc]])
                    yield
                w_release(sc.s[0], sx.s[0], sbk.s[0])
            if last:
                base, pbs = ps_alloc(2)
                TR([(psum[0:18, base + kc * 128:base + (kc + 1) * 128], utail[:, kc, :], identf[:, :]) for kc in range(8)],
                   [B["utail"], B["identf"]], pbs)
                bt, bb = big_alloc()
                ACT(bt[0:18, :], psum[0:18, base:base + 1024], AF.Copy, pbs, [bb])
                DMA("sp", ncp[:, :], bt[0:2, :], [bb], [])
                DMA("sp", ncs[:, 1, :], bt[2:18, :], [bb], [])

            for half in range(2):
                swa = acquire_blk("wa")
                sga = acquire_blk("ga")
                for j in range(4):
                    kc = half * 4 + j
                    by, py = ps_alloc(nbW)
                    mm_fm(by, py, swa, j, mergedT, BmT, W)
                    bg, pg = ps_alloc(nbW)
                    mm_fm(bg, pg, sga, j, hT, allhT, W)
                    bt, bb = big_alloc()
                    ACT(bt[:, 0:W], psum[:, bg:bg + W], AF.Tanh, pg, [bb], scale=0.5)
                    ACT(bt[:, 0:W], bt[:, 0:W], AF.Identity, [bb], [bb], scale=0.5, bias=0.5)
                    TT("dve", merged_a[:, kc, 0:W], psum[:, by:by + W], bt[:, 0:W], ALU.mult, py + [bb], [Bma[kc]])
                    yield
                w_release(*swa.s, *sga.s)


        def gla_gen():
            if last:
                sample_prep()
                yield
            def stage1(c):
                par_ = (st * 4 + c) % 2
                cols_ = slice(c * 128, (c + 1) * 128)
                bgk, pgk = G(0, 1)
                MM([(psum[:, bgk + q_ * 128:bgk + (q_ + 1) * 128], gkl[0:17, cols_], wgk[0:17, q_ * 128:(q_ + 1) * 128], True, True)
                    for q_ in range(4)], [B["gkl"], B["wgk"]], pgk)
                ACT(e1, psum[:, bgk:bgk + 512], AF.Exp, pgk, [Be1], scale=-1.0)
                ACT(lt[par_], e1, AF.Ln, [Be1], [Blt[par_]], bias=1.0)

            stage1(0)
            yield
            for c in range(4):
                cidx = st * 4 + c
                par = cidx % 2
                cols = slice(c * 128, (c + 1) * 128)
                bpb, ppb = G(1, 1)
                MM([(psum[:, bpb + h_ * 128:bpb + (h_ + 1) * 128], lt[par][:, h_ * 128:(h_ + 1) * 128], tri[:, :], True, True)
                    for h_ in range(4)], [Blt[par], B["tri"]], ppb)
                pbv = psum[:, bpb:bpb + 512].rearrange("p (h c) -> p h c", h=4)
                ACT(eb_a, pbv, AF.Exp, ppb, [Beb], scale=-1.0 / 16)
                ACT(einv_a, pbv, AF.Exp, ppb, [Beinv], scale=1.0 / 16)
                TT("dve", ed_a, einv_a, eb_a[:, :, 127:128].to_broadcast([128, 4, 128]), ALU.mult, [Beinv, Beb], [Bed])
                TT("dve", qe_a, qT[:, :, cols], eb_a, ALU.mult, [Bq, Beb], [Bqe])
                TT("dve", ke_a, kT[:, :, cols], einv_a, ALU.mult, [Bk, Beinv], [Bke])
                TT("dve", kd_a, kT[:, :, cols], ed_a, ALU.mult, [Bk, Bed], [Bkd])
                if c + 1 < 4:
                    stage1(c + 1)
                yield
                bkd, pkd = G(0, 1)
                pkdv = psum[:, bkd:bkd + 512].bitcast(BF16)
                TR([(pkdv[:, h_ * 128:(h_ + 1) * 128], kd_a[:, h_, :], ident[:, :]) for h_ in range(4)], [Bkd, B["ident"]], pkd)
                ACT(kdt_a, pkdv[:, 0:512], AF.Copy, pkd, [Bkdt])
                bsc, psc_ = G(1, 1)
                MM([(psum[:, bsc + h_ * 128:bsc + (h_ + 1) * 128], ke_a[:, h_, :], qe_a[:, h_, :], True, True) for h_ in range(4)],
                   [Bke, Bqe], psc_)
                TT("dve", scm_a, psum[:, bsc:bsc + 512].rearrange("p (h c) -> p h c", h=4),
                   tri[:, :].unsqueeze(1).to_broadcast([128, 4, 128]), ALU.mult, psc_ + [B["tri"]], [Bscm])
                yield
                po, ppo = G(2, 2)
                lst = []
                for h_ in range(4):
                    vh = v_tok[:, c, h_ * 256:(h_ + 1) * 256]
                    lst.append((psum[:, po + h_ * 256:po + (h_ + 1) * 256], scm_a[:, h_, :], vh, True, False))
                    lst.append((psum[:, po + h_ * 256:po + (h_ + 1) * 256], qe_a[:, h_, :], Sbf[par][:, h_, :], False, True))
                MM(lst, [Bscm, Bv[c], Bqe, BSbf[par]], ppo)
                bD, pD_ = G(0, 2)
                MM([(psum[:, bD + h_ * 256:bD + (h_ + 1) * 256], kdt_a[:, h_ * 128:(h_ + 1) * 128],
                     v_tok[:, c, h_ * 256:(h_ + 1) * 256], True, True) for h_ in range(4)], [Bkdt, Bv[c]], pD_)
                for h_ in range(4):
                    STT("dve", S[:, h_, :], S[:, h_, :], eb_a[:, h_, 127:128], psum[:, bD + h_ * 256:bD + (h_ + 1) * 256],
                        ALU.mult, ALU.add, [B["S"], Beb] + pD_, [B["S"]])
                ACT(Sbf[1 - par], S[:, :, :], AF.Copy, [B["S"]], [BSbf[1 - par]])
                yield
                gla_post(lambda h_, po=po: psum[:, po + h_ * 256:po + (h_ + 1) * 256], ppo, 128, c, c * 128, par)
                yield
                if last:
                    for j in range(c * 4, c * 4 + 4):
                        sample_iter(j)
                        yield
            if last:
                DMA("sp", ngp.rearrange("h k v -> k h v"), S[:, :, :], [B["S"]], [])
                gla_post(lambda h_: oS[:, h_ * 256:(h_ + 1) * 256], [BoS], NS, 4, TP, 0)


        ps_state["lo"], ps_state["hi"] = 4, 8
        g_gla, g_conv = gla_gen(), conv_gen()
        n_gla = 17 + (17 if last else 0)
        n_conv = 16
        done_g = 0
        for i_ in range(n_conv):
            while done_g * n_conv < (i_ + 1) * n_gla:
                if next(g_gla, "end") == "end":
                    break
                done_g += 1
            next(g_conv, None)
        for _ in g_gla:
            pass
        for _ in g_conv:
            pass
        ps_state["lo"], ps_state["hi"] = 0, 8

        for half in range(2):
            swb = acquire_blk("wb")
            sgb = acquire_blk("gb")
            for j in range(4):
                kc = half * 4 + j
                by, py = ps_alloc(nbW)
                mm_fm(by, py, swb, j, goT, BgoT + BgoTc[0:len(tiles)], W)
                bg, pg = ps_alloc(nbW)
                mm_fm(bg, pg, sgb, j, hT, allhT, W)
                bt, bb = big_alloc()
                ACT(bt[:, 0:W], psum[:, bg:bg + W], AF.Sigmoid, pg, [bb])
                TT("dve", bt[:, 0:W], psum[:, by:by + W], bt[:, 0:W], ALU.mult, py + [bb], [bb])
                TT("dve", mergedT[:, kc, 0:W], bt[:, 0:W], merged_a[:, kc, 0:W], ALU.add, [bb, Bma[kc]], [BmT[kc]])
            w_release(*swb.s, *sgb.s)
        handoff(mixB_bufs, Bwdn)
        WS["wd_ok"] = True
        wb_ffn, Bwb_ffn = wbc_load(w_ffnpost)

        so = [acquire_blk("wo") for _ in range(2)]
        for idx, (tt, col0, R) in enumerate(tiles):
            b_, p_ = ps_alloc(2)
            lst = []
            for i, blk_ in enumerate(so):
                for kc in range(8):
                    lst.append((psum[0:R, b_ + i * 512:b_ + (i + 1) * 512], mergedT[:, kc, col0:col0 + R],
                                blk_.rhs(kc), kc == 0, kc == 7))
            MM(lst, BmT + so[0].B + so[1].B, p_)
            norm_resid(b_, p_, R, tt, wb_mix, Bwb_mix)
            if idx >= LAG:
                fm_tile(idx - LAG, *tiles[idx - LAG], wn_ffn, B["wn_ffn"])
        for i_ in range(max(0, len(tiles) - LAG), len(tiles)):
            fm_tile(i_, *tiles[i_], wn_ffn, B["wn_ffn"])
        w_release(*so[0].s, *so[1].s)
        handoff(Bma + BmT, BactT)
        wb_ple, Bwb_ple = wbc_load(w_plepost)

        for gi in range(6):
            if gi < 5:
                sfg, sfu, nch = acquire_blk("fg"), acquire_blk("fu"), 4
            else:
                sfg, sfu, nch = Grp(w_acquire("fg")), Grp(w_acquire("fu")), 2
            for j in range(nch):
                ci = gi * 4 + j
                bg, pg = ps_alloc(nbW)
                mm_fm(bg, pg, sfg, j, hT, allhT, W)
                bu, pu = ps_alloc(nbW)
                mm_fm(bu, pu, sfu, j, hT, allhT, W)
                bt, bb = big_alloc()
                ACT(bt[:, 0:W], psum[:, bg:bg + W], AF.Silu, pg, [bb])
                TT("dve", actT[:, ci, 0:W], psum[:, bu:bu + W], bt[:, 0:W], ALU.mult, pu + [bb], [BactT[ci]])
            w_release(*sfg.s, *sfu.s)
        def ple_prep(ti, tt, col0, R):
            par = ti % 2
            src = pp_d[st * TP + tt * 128: st * TP + tt * 128 + 128, :] if tt < 4 else ps_d
            bt, bb = big_alloc()
            DMA("sp", bt[0:R, 0:256], src, [], [bb])
            CP("dve", pb16[par][0:R, :], bt[0:R, 0:256], [bb], [Bpb16[par]])
            base, pbs = ps_alloc(1)
            pv = psum[:, base:base + 512].bitcast(BF16).rearrange("p (k c) -> p k c", k=8)
            TR([(pv[:, kc, 0:R], pb16[par][0:R, kc * 128:(kc + 1) * 128], ident[0:R, 0:R]) for kc in range(2)],
               [Bpb16[par], B["ident"]], pbs)
            ACT(pT[:, :, col0:col0 + R], pv[:, 0:2, 0:R], AF.Copy, pbs, [BpT[tt]])
            fm_tile(ti, tt, col0, R, None, None, norm=False)

        for idx, (tt, col0, R) in enumerate(tiles):
            b_, p_ = ps_alloc(2)
            lst = []
            for half in range(2):
                for kc in range(NFC):
                    lst.append((psum[0:R, b_ + half * 512:b_ + (half + 1) * 512], actT[:, kc, col0:col0 + R],
                                wdn[:, kc, half * 512:(half + 1) * 512], kc == 0, kc == NFC - 1))
            MM(lst, BactT + Bwdn, p_)
            norm_resid(b_, p_, R, tt, wb_ffn, Bwb_ffn)
            if idx >= LAG:
                ple_prep(idx - LAG, *tiles[idx - LAG])
        for i_ in range(max(0, len(tiles) - LAG), len(tiles)):
            ple_prep(i_, *tiles[i_])

        spp = w_acquire("pp")
        spg = [acquire_blk("pg") for _ in range(2)]
        wppv = wslot[spp][:, :, :].rearrange("p (a b) n -> p a (b n)", a=2)
        for idx, (tt, col0, R) in enumerate(tiles):
            bp, ppp = ps_alloc(2)
            lst = []
            for half in range(2):
                for kc in range(2):
                    lst.append((psum[0:R, bp + half * 512:bp + (half + 1) * 512], pT[:, kc, col0:col0 + R],
                                wppv[:, kc, half * 512:(half + 1) * 512], kc == 0, kc == 1))
            MM(lst, [BpT[tt], Bws[spp]], ppp)
            bg, pg = ps_alloc(2)
            lst = []
            for i, blk_ in enumerate(spg):
                for kc in range(8):
                    lst.append((psum[0:R, bg + i * 512:bg + (i + 1) * 512], hT[:, kc, col0:col0 + R],
                                blk_.rhs(kc), kc == 0, kc == 7))
            MM(lst, [BhT[tt]] + spg[0].B + spg[1].B, pg)
            bt, bb = big_alloc()
            ACT(bt[0:R, :], psum[0:R, bg:bg + 1024], AF.Sigmoid, pg, [bb])
            TT("dve", bt[0:R, :], psum[0:R, bp:bp + 1024], bt[0:R, :], ALU.mult, ppp + [bb], [bb])
            sa, sB = stat_alloc()
            ACT(junk[0:R, :], bt[0:R, :], AF.Square, [bb], [Bjunk, sB], accum=sa[0:R, 0:1])
            rstd_from(sa, sB, R, 1024)
            TT("dve", bt[0:R, :], bt[0:R, :], wb_ple[0:R, :], ALU.mult, [bb, Bwb_ple], [bb])
            STT("dve", ht[0:R, tt, :], bt[0:R, :], sa[0:R, 2:3], ht[0:R, tt, :], ALU.mult, ALU.add, [bb, sB, Bht[tt]], [Bht[tt]])
            dst = yp[st * TP + tt * 128: st * TP + tt * 128 + 128, :] if tt < 4 else ys
            DMA("sp", dst, ht[0:R, tt, :], [Bht[tt]], [])
            if not last:
                r0 = (st + 1) * TP + tt * 128
                DMA("sp", ht[:, tt, :], xp[r0:r0 + 128, :], [], [Bht[tt]])
                if idx >= LAG:
                    fm_tile(idx - LAG, *tiles[idx - LAG], wn_pre, B["wn_pre"])
                    preA.add((st + 1, tiles[idx - LAG][0]))
        w_release(spp, *spg[0].s, *spg[1].s)

    P.emit(nc, es)
    es.close()
    return nc


_NC_CACHE = {}


def kernel(x_prompt, x_sample, state_conv, state_gla, p_prompt, p_sample,
           w_norm_mix_pre, w_in, w_conv, w_a_out, w_gk, b_gk, w_gla_norm, w_b_out, w_o,
           w_norm_mix_post, w_norm_ffn_pre, w_ffn_gate, w_ffn_up, w_ffn_down, w_norm_ffn_post,
           w_ple_proj, w_ple_gate, w_norm_ple_post):
    f = lambda a: np.ascontiguousarray(np.asarray(a, dtype=np.float32))
    if "nc" not in _NC_CACHE:
        _NC_CACHE["nc"] = build_program()
    nc = _NC_CACHE["nc"]
    shared = {
        "w_pre": f(w_norm_mix_pre[0]), "w_in": f(w_in[0]), "w_conv": f(w_conv[0]), "w_a": f(w_a_out[0]),
        "w_gk": f(w_gk[0]), "b_gk": f(b_gk[0]), "w_gn": f(w_gla_norm[0]), "w_b": f(w_b_out[0]), "w_o": f(w_o[0]),
        "w_mixpost": f(w_norm_mix_post[0]), "w_ffnpre": f(w_norm_ffn_pre[0]), "w_fg": f(w_ffn_gate[0]),
        "w_fu": f(w_ffn_up[0]), "w_fd": f(w_ffn_down[0]), "w_ffnpost": f(w_norm_ffn_post[0]),
        "w_pp": f(w_ple_proj[0]), "w_pg": f(w_ple_gate[0]), "w_plepost": f(w_norm_ple_post[0]),
    }
    x_prompt = np.asarray(x_prompt); x_sample = np.asarray(x_sample)
    state_conv = np.asarray(state_conv); state_gla = np.asarray(state_gla)
    p_prompt = np.asarray(p_prompt); p_sample = np.asarray(p_sample)
    in_maps = []
    for c in range(8):
        m = dict(shared)
        m["xp"] = f(x_prompt[c])
        m["xs"] = f(x_sample[c * NS:(c + 1) * NS, 0, :])
        m["sconv"] = f(state_conv[0, c * NS:(c + 1) * NS])
        m["sgla"] = f(state_gla[0, c * NS:(c + 1) * NS])
        m["pp"] = f(p_prompt[0, c])
        m["psm"] = f(p_sample[0, c * NS:(c + 1) * NS, 0, :])
        in_maps.append(m)
    res = run_bass_kernel_spmd(nc, in_maps, core_ids=list(range(8)))
    r = res.results
    y_prompt = np.stack([r[c]["yp"] for c in range(8)], axis=0).astype(np.float32)
    y_sample = np.concatenate([r[c]["ys"] for c in range(8)], axis=0)[:, None, :].astype(np.float32)
    new_conv_prompt = np.stack([r[c]["ncp"] for c in range(8)], axis=0)[None].astype(np.float32)
    new_gla_prompt = np.stack([r[c]["ngp"] for c in range(8)], axis=0)[None].astype(np.float32)
    new_conv_sample = np.concatenate([r[c]["ncs"] for c in range(8)], axis=0)[None].astype(np.float32)
    new_gla_sample = np.concatenate([r[c]["ngs"] for c in range(8)], axis=0)[None].astype(np.float32)
    return (y_prompt, y_sample, new_conv_prompt, new_gla_prompt, new_conv_sample, new_gla_sample)
```
